# Optimizing a Trainium2 kernel written in Bass

```python
import jax, jax.numpy as jnp
from jax import lax
import numpy as np

D_MODEL = 2048
BATCH = 8
SEQ = 2048
DEPTH = 1

GM_WIDTH = 2048
CHUNK = 128
GM_GROUPS = 16
GM_GROUP_DIM = GM_WIDTH // GM_GROUPS
MLA_HEADS = 16
Q_LORA = 512
KV_LORA = 256
QK_NOPE = 128
QK_ROPE = 64
V_HEAD = 128
ROPE_THETA = 10000.0
Q_BLOCK = 128
D_FF = 5632
CONV_W = 3
EPS = 1e-6
N_MOD = 6
IN_SIZES = (GM_WIDTH, GM_WIDTH, Q_LORA, KV_LORA, QK_ROPE, D_MODEL, D_MODEL)
IN_COLS = sum(IN_SIZES)
IN_SPLITS = tuple(int(s) for s in np.cumsum(IN_SIZES)[:-1])

kernel_name = "hybrid_gmlp_mla_convffn_block"


def rmsnorm(x, g):
    xf = x.astype(jnp.float32)
    y = xf * lax.rsqrt(jnp.mean(xf * xf, axis=-1, keepdims=True) + EPS)
    return (y * g.astype(jnp.float32)).astype(x.dtype)


def layernorm(x, g, b):
    xf = x.astype(jnp.float32)
    mu = jnp.mean(xf, axis=-1, keepdims=True)
    var = jnp.mean(jnp.square(xf - mu), axis=-1, keepdims=True)
    y = (xf - mu) * lax.rsqrt(var + EPS)
    return (y * g.astype(jnp.float32) + b.astype(jnp.float32)).astype(x.dtype)


def rope_tables(positions, dtype):
    inv = ROPE_THETA ** (-jnp.arange(0, QK_ROPE, 2, dtype=jnp.float32) / QK_ROPE)
    ang = positions.astype(jnp.float32)[..., None] * inv
    return jnp.cos(ang).astype(dtype), jnp.sin(ang).astype(dtype)


def apply_rope(x, cos, sin):
    x1, x2 = jnp.split(x, 2, axis=-1)
    return jnp.concatenate([x1 * cos - x2 * sin, x2 * cos + x1 * sin], axis=-1)


def gmlp_spatial_gating(u, v, ln_g, ln_b, w_s, b_s):
    B, S, _ = v.shape
    v = layernorm(v, ln_g, ln_b)
    v = v.reshape(B, S // CHUNK, CHUNK, GM_GROUPS, GM_GROUP_DIM)
    mask = jnp.tril(jnp.ones((CHUNK, CHUNK), dtype=w_s.dtype))
    mixed = jnp.einsum('bnpgd,gqp->bnqgd', v, w_s * mask) + b_s.T[None, None, :, :, None]
    return u * mixed.reshape(B, S, GM_WIDTH)


def mla_attention(q_lat, kv_lat, k_pe, positions, q_norm_g, w_uq, kv_norm_g, w_ukv):
    B, S, _ = q_lat.shape
    q = (rmsnorm(q_lat, q_norm_g) @ w_uq).reshape(B, S, MLA_HEADS, QK_NOPE + QK_ROPE)
    kv = (rmsnorm(kv_lat, kv_norm_g) @ w_ukv).reshape(B, S, MLA_HEADS, QK_NOPE + V_HEAD)
    q_nope, q_pe = q[..., :QK_NOPE], q[..., QK_NOPE:]
    k_nope, v = kv[..., :QK_NOPE], kv[..., QK_NOPE:]
    cos, sin = rope_tables(positions, q.dtype)
    q_pe = apply_rope(q_pe, cos[:, :, None], sin[:, :, None])
    k_pe = apply_rope(k_pe, cos, sin)
    q = jnp.concatenate([q_nope, q_pe], axis=-1)
    k = jnp.concatenate([k_nope, jnp.broadcast_to(k_pe[:, :, None], (B, S, MLA_HEADS, QK_ROPE))], axis=-1)
    scale = (QK_NOPE + QK_ROPE) ** -0.5
    n_blocks = S // Q_BLOCK
    q_blocks = q.reshape(B, n_blocks, Q_BLOCK, MLA_HEADS, QK_NOPE + QK_ROPE).transpose(1, 0, 2, 3, 4)
    key_pos = jnp.arange(S)

    def attend(args):
        qb, i = args
        s = jnp.einsum('bqhd,bkhd->bhqk', qb, k).astype(jnp.float32) * scale
        q_pos = i * Q_BLOCK + jnp.arange(Q_BLOCK)
        causal = key_pos[None, :] <= q_pos[:, None]
        s = jnp.where(causal[None, None], s, -1e30)
        p = jax.nn.softmax(s, axis=-1).astype(v.dtype)
        return jnp.einsum('bhqk,bkhd->bqhd', p, v)

    o = lax.map(attend, (q_blocks, jnp.arange(n_blocks)))
    return o.transpose(1, 0, 2, 3, 4).reshape(B, S, MLA_HEADS * V_HEAD)


def causal_dwconv(h, w, b):
    S = h.shape[1]
    hp = jnp.pad(h, ((0, 0), (CONV_W - 1, 0), (0, 0)))
    return sum(w[k] * hp[:, k:k + S] for k in range(CONV_W)) + b


def setup_inputs(seed: int = 0) -> dict:
    key = jax.random.key(seed)
    ks = jax.random.split(key, 32)
    f32 = jnp.float32
    nrm = lambda k, shape, s: jax.random.normal(k, shape, f32) * s
    gain = lambda k, n: 1.0 + 0.02 * jax.random.normal(k, (n,), f32)
    offset = jax.random.randint(ks[2], (BATCH, 1), 0, 4096, dtype=jnp.int32)
    positions = (jnp.arange(SEQ, dtype=jnp.int32)[None, :] + offset).astype(jnp.int32)
    return {
        "x": nrm(ks[0], (BATCH, SEQ, D_MODEL), 1.0),
        "c": nrm(ks[1], (BATCH, D_MODEL), 1.0),
        "positions": positions,
        "w_ada": nrm(ks[3], (D_MODEL, N_MOD * D_MODEL), 0.5 * D_MODEL ** -0.5),
        "b_ada": nrm(ks[4], (N_MOD * D_MODEL,), 0.01),
        "pre_norm1_g": gain(ks[5], D_MODEL),
        "w_in": nrm(ks[6], (D_MODEL, IN_COLS), D_MODEL ** -0.5),
        "gm_ln_g": gain(ks[7], GM_WIDTH),
        "gm_ln_b": nrm(ks[8], (GM_WIDTH,), 0.01),
        "gm_w_s": nrm(ks[9], (GM_GROUPS, CHUNK, CHUNK), CHUNK ** -0.5),
        "gm_b_s": 1.0 + 0.02 * jax.random.normal(ks[10], (GM_GROUPS, CHUNK), f32),
        "w_branch_a": nrm(ks[11], (GM_WIDTH, D_MODEL), GM_WIDTH ** -0.5),
        "q_norm_g": gain(ks[12], Q_LORA),
        "w_uq": nrm(ks[13], (Q_LORA, MLA_HEADS * (QK_NOPE + QK_ROPE)), Q_LORA ** -0.5),
        "kv_norm_g": gain(ks[14], KV_LORA),
        "w_ukv": nrm(ks[15], (KV_LORA, MLA_HEADS * (QK_NOPE + V_HEAD)), KV_LORA ** -0.5),
        "w_branch_b": nrm(ks[16], (MLA_HEADS * V_HEAD, D_MODEL), (MLA_HEADS * V_HEAD) ** -0.5),
        "w_out": nrm(ks[17], (D_MODEL, D_MODEL), D_MODEL ** -0.5),
        "post_norm1_g": gain(ks[18], D_MODEL),
        "pre_norm2_g": gain(ks[19], D_MODEL),
        "w_up": nrm(ks[20], (D_MODEL, 2 * D_FF), D_MODEL ** -0.5),
        "conv_w": nrm(ks[21], (CONV_W, 2 * D_FF), CONV_W ** -0.5),
        "conv_b": nrm(ks[22], (2 * D_FF,), 0.01),
        "w_down": nrm(ks[23], (D_FF, D_MODEL), D_FF ** -0.5),
        "post_norm2_g": gain(ks[24], D_MODEL),
    }


def reference(x, c, positions, w_ada, b_ada, pre_norm1_g, w_in, gm_ln_g, gm_ln_b, gm_w_s, gm_b_s,
              w_branch_a, q_norm_g, w_uq, kv_norm_g, w_ukv, w_branch_b, w_out, post_norm1_g,
              pre_norm2_g, w_up, conv_w, conv_b, w_down, post_norm2_g):
    B = x.shape[0]
    mod = (jax.nn.silu(c) @ w_ada + b_ada).reshape(B, N_MOD, D_MODEL)
    shift1, scale1, gate1 = mod[:, None, 0], mod[:, None, 1], mod[:, None, 2]
    shift2, scale2, gate2 = mod[:, None, 3], mod[:, None, 4], mod[:, None, 5]

    for _ in range(DEPTH):
        h = rmsnorm(x, pre_norm1_g) * (1.0 + scale1) + shift1
        z = h @ w_in
        u, v, q_lat, kv_lat, k_pe, g_a, g_b = jnp.split(z, IN_SPLITS, axis=-1)
        y_a = gmlp_spatial_gating(jax.nn.gelu(u), jax.nn.gelu(v), gm_ln_g, gm_ln_b, gm_w_s, gm_b_s) @ w_branch_a
        y_b = mla_attention(q_lat, kv_lat, k_pe, positions, q_norm_g, w_uq, kv_norm_g, w_ukv) @ w_branch_b
        merged = jax.nn.sigmoid(g_a) * y_a + jax.nn.sigmoid(g_b) * y_b
        x = x + gate1 * rmsnorm(merged @ w_out, post_norm1_g)

        h = rmsnorm(x, pre_norm2_g) * (1.0 + scale2) + shift2
        up = causal_dwconv(h @ w_up, conv_w, conv_b)
        gate_h, val_h = jnp.split(up, 2, axis=-1)
        ffn = (jax.nn.silu(gate_h) * val_h) @ w_down
        x = x + gate2 * rmsnorm(ffn, post_norm2_g)
    return x
```

```python
import numpy as np
import ml_dtypes
import concourse.bass as bass
import concourse.mybir as mybir
from concourse.bass_utils import run_bass_kernel_spmd

F32 = mybir.dt.float32
BF16 = mybir.dt.bfloat16
I32 = mybir.dt.int32
AF = mybir.ActivationFunctionType
ALU = mybir.AluOpType
AX = mybir.AxisListType

D = 2048
S = 2048
KC = D // 128
NB = 2
NT = S // NB
EPS = 1e-6
D_FF = 5632
FC = D_FF // 128
N_HEADS = 16

C_BADA, C_G1, C_G2, C_GP1, C_GP2, C_LNG, C_LNB, C_QG, C_KVG, C_C, C_CW, C_CB = (
    0, 96, 112, 128, 144, 160, 176, 192, 196, 198, 214, 478)
C_INV, C_SGN, C_NPI, C_NPS = 566, 567, 568, 569
NCOL = 570


class Ev:
    __slots__ = ("sem", "val")

    def __init__(self, sem, val):
        self.sem = sem
        self.val = val


class Tok:
    __slots__ = ("w", "r", "dsem", "dcnt", "name", "excl")

    def __init__(self, name=""):
        self.excl = False
        self.w = None
        self.r = {}
        self.dsem = None
        self.dcnt = 0
        self.name = name


class Prog:
    ENG = ("pe", "act", "dve", "pool", "sp")

    def __init__(self, nc):
        self.nc = nc
        self.ops = {e: [] for e in self.ENG}
        self.sems = {e: nc.alloc_semaphore("s_" + e) for e in self.ENG}
        self.cnt = {e: 0 for e in self.ENG}
        self.waited = {e: {} for e in self.ENG}
        self.semobj = {self.sems[e].num: self.sems[e] for e in self.ENG}
        self.ntok = 0

    def tok(self, name=""):
        return Tok(name)

    def toks(self, n):
        return [Tok() for _ in range(n)]

    def dtok(self, name=""):
        t = Tok(name)
        self.ntok += 1
        t.dsem = self.nc.alloc_semaphore("d%d_%s" % (self.ntok, name))
        self.semobj[t.dsem.num] = t.dsem
        return t

    def _deps(self, eng, reads, writes, excl_own=None):
        need = {}
        for t in reads:
            if t.w is not None:
                need[t.w.sem] = max(need.get(t.w.sem, 0), t.w.val)
            if t.excl:
                for s, v in t.r.items():
                    if s != excl_own:
                        need[s] = max(need.get(s, 0), v)
        for t in writes:
            if t.w is not None:
                need[t.w.sem] = max(need.get(t.w.sem, 0), t.w.val)
            for s, v in t.r.items():
                need[s] = max(need.get(s, 0), v)
        w = self.waited[eng]
        waits = []
        if eng == "pe":
            need.pop(self.sems["pe"].num, None)
        for s, v in need.items():
            if w.get(s, 0) < v:
                waits.append((s, v))
                w[s] = v
        return waits

    def _mark(self, ev, reads, writes):
        for t in reads:
            t.r[ev.sem] = max(t.r.get(ev.sem, 0), ev.val)
        for t in writes:
            t.w = ev
            t.r = {}

    def op(self, eng, fn, reads=(), writes=()):
        waits = self._deps(eng, reads, writes, excl_own=self.sems[eng].num)
        self.cnt[eng] += 1
        ev = Ev(self.sems[eng].num, self.cnt[eng])
        self.ops[eng].append((waits, fn, (self.sems[eng].num, 1)))
        self._mark(ev, reads, writes)
        return ev

    def dma(self, eng, fn, reads=(), writes=(), sem_tok=None):
        st = sem_tok if sem_tok is not None else writes[0]
        assert st.dsem is not None
        waits = self._deps(eng, reads, writes)
        st.dcnt += 16
        ev = Ev(st.dsem.num, st.dcnt)
        self.ops[eng].append((waits, fn, (st.dsem.num, 16)))
        self._mark(ev, reads, writes)
        return ev

    def wait_all(self, eng, toks):
        waits = self._deps(eng, [], toks)
        self.ops[eng].append((waits, None, None))

    def emit(self):
        nc, ops, semobj = self.nc, self.ops, self.semobj

        def run(e, lst):
            for waits, fn, inc in lst:
                for s, v in waits:
                    e.wait_ge(semobj[s], v)
                if fn is None:
                    continue
                ins = fn(e)
                if inc is not None:
                    ins.then_inc(semobj[inc[0]], inc[1])

        with nc.Block() as block:
            @block.tensor
            def _(e):
                run(e, ops["pe"])

            @block.scalar
            def _(e):
                run(e, ops["act"])

            @block.vector
            def _(e):
                run(e, ops["dve"])

            @block.gpsimd
            def _(e):
                run(e, ops["pool"])

            @block.sync
            def _(e):
                run(e, ops["sp"])


class WPool:
    def __init__(self, P, nc, nslots, elems):
        self.P = P
        self.slots = [nc.alloc_sbuf_tensor("wslot%d" % i, [128, elems], BF16) for i in range(nslots)]
        self.toks = [P.dtok("w%d" % i) for i in range(nslots)]
        self.plan = []
        self.issued = 0
        self.cur = 0
        self.live = 1

    def declare(self, tag, src, kc, ncols):
        self.plan.append((tag, src, kc, ncols))

    def _issue(self, i):
        tag, src, kc, ncols = self.plan[i]
        s = i % len(self.slots)
        dst = self.slots[s][:, 0:kc * ncols].rearrange("p (k o) -> p k o", k=kc)
        self.P.dma("pool", lambda e, dst=dst, src=src: e.dma_start(out=dst, in_=src), writes=[self.toks[s]])

    def next(self, tag):
        i = self.cur
        assert self.plan[i][0] == tag, (self.plan[i][0], tag)
        while self.issued < min(len(self.plan), i + 1 + len(self.slots) - self.live):
            self._issue(self.issued)
            self.issued += 1
        self.cur += 1
        _, _, kc, ncols = self.plan[i]
        s = i % len(self.slots)
        return self.slots[s][:, 0:kc * ncols].rearrange("p (k o) -> p k o", k=kc), self.toks[s]


O_U, O_V, O_QL, O_KVL, O_KPE, O_GA, O_GB = 0, 2048, 4096, 4608, 4864, 4928, 6976


def build(stop_after="all", dbg=()):
    nc = bass.Bass("TRN2", target_bir_lowering=False)
    P = Prog(nc)
    dt = lambda name, shape, ty, kind: nc.dram_tensor(name, shape, ty, kind=kind).ap()
    x_d = dt("x", [S, D], F32, "ExternalInput")
    pos_d = dt("pos", [1, S], I32, "ExternalInput")
    cols_d = dt("cols", [128, NCOL], F32, "ExternalInput")
    w_ada_d = dt("w_ada", [D, 6 * D], F32, "ExternalInput")
    w_in_d = dt("w_in", [D, 9024], F32, "ExternalInput")
    wsT_d = dt("wsT", [128, 16, 128], F32, "ExternalInput")
    bs_d = dt("bs", [1, 2048], F32, "ExternalInput")
    w_ag_d = dt("w_ag", [D, 2 * D], F32, "ExternalInput")
    w_bg_d = dt("w_bg", [D, 2 * D], F32, "ExternalInput")
    w_lat_d = dt("w_lat", [D, 1024], F32, "ExternalInput")
    w_pair_d = dt("w_pair", [8, 128, 3072], F32, "ExternalInput")
    w_out_d = dt("w_out", [D, D], F32, "ExternalInput")
    w_up2_d = dt("w_up2", [D, 2 * D_FF], F32, "ExternalInput")
    w_down_d = dt("w_down", [D_FF, D], F32, "ExternalInput")
    out_d = dt("out", [S, D], F32, "ExternalOutput")
    gv_d = nc.dram_tensor("gv_scr", [FC, 128, S], BF16).ap()
    dbg_d = {}
    for name, shape, ty in dbg:
        dbg_d[name] = dt("dbg_" + name, shape, ty, "ExternalOutput")

    sb = lambda name, shape, ty: nc.alloc_sbuf_tensor("sb_" + name, shape, ty)
    cols = sb("cols", [128, NCOL], F32)
    modc = sb("modc", [128, 96], F32)
    prm = sb("prm", [128, 6, KC], F32)
    scb = sb("scb", [128, KC], BF16)
    ident = sb("ident", [128, 128], BF16)
    identf = sb("identf", [128, 128], F32)
    ones = sb("ones", [128, 128], BF16)
    stat = sb("stat", [128, 64], F32)
    epsc = sb("epsc", [128, 1], F32)
    fdummy = sb("fdummy", [128, 2], F32)
    big0 = sb("big0", [128, 32768], BF16)
    hT = big0[:, 0:16384].rearrange("p (k t) -> p k t", k=KC)
    A3 = big0[:, 16384:32768].rearrange("p (n c) -> p n c", n=8)
    A4 = big0[:, 16384:32768].rearrange("p (n g q) -> p n g q", n=8, g=16)
    mergedT = sb("mergedT", [128, KC, NT], BF16)
    scr = sb("scr", [128, 12288], BF16)
    xs = [scr[:, 0:4096].bitcast(F32), scr[:, 4096:8192].bitcast(F32)]
    xn = [scr[:, 8192:10240], scr[:, 10240:12288]]
    WmT = scr[:, 0:2048].rearrange("p (g q) -> p g q", g=16)
    bsb = scr[:, 2048:6144].bitcast(F32).rearrange("p (g q) -> p g q", g=16)
    Cg = scr[:, 6144:10240].bitcast(F32).rearrange("p (g q) -> p g q", g=16)
    tmpb = [scr[:, 10240:10752], scr[:, 11264:11776]]
    tmpf = [scr[:, 10240:11264].bitcast(F32), scr[:, 11264:12288].bitcast(F32)]
    junkv = scr[:, 10240:12288]
    oT = big0[:, 16384:32768].rearrange("p (h t) -> p h t", h=16)
    qgT = sb("qgT", [128, 4, NT], BF16)
    kvgT = sb("kvgT", [128, 2, S], BF16)
    kpe_dup = sb("kpe_dup", [128, S], BF16)
    cos2 = sb("cos2", [128, NT], BF16)
    sin_s = sb("sin_s", [128, NT], BF16)
    maskneg = sb("maskneg", [128, 128], BF16)
    KnT_b = sb("KnT_b", [128, S], BF16)
    Vh_b = sb("Vh_b", [128, 16, 128], BF16)
    masktmp = sb("masktmp", [128, 128], F32)
    sqb = [scr[:, 0:512], scr[:, 512:1024]]
    posi = scr[:, 2048:4096].bitcast(I32)
    angf = scr[:, 4096:6144].bitcast(F32)
    targ = scr[:, 6144:8192].bitcast(F32)
    QnT = [scr[:, 0:1024], scr[:, 1024:2048]]
    Qr = [scr[:, 2048:3072], scr[:, 3072:4096]]
    KnT = scr[:, 4096:6144]
    Vh = scr[:, 6144:8192].rearrange("p (k d) -> p k d", k=16)
    ptile = [scr[:, 8192 + 512 * k:8192 + 512 * (k + 1)] for k in range(4)]
    rec = scr[:, 10240:11264].bitcast(F32)
    rt1 = scr[:, 11264:12288].bitcast(F32)
    yT = big0[:, :].bitcast(F32).rearrange("p (k t) -> p k t", k=KC)
    h2T = big0[:, :].rearrange("p (k t) -> p k t", k=KC)
    sqe = [scr[:, 8192:8704], scr[:, 8704:9216]]
    sqw = [scr[:, 8192:9216], scr[:, 9216:10240]]
    rstd_e = scr[:, 10240:12288].bitcast(F32)
    hbuf = [scr[:, 0:1028].bitcast(F32), scr[:, 1056:2084].bitcast(F32)]
    ybuf = [scr[:, 2112:3136].bitcast(F32), scr[:, 3136:4160].bitcast(F32)]
    sgb = scr[:, 4160:4672]
    gvs = [scr[:, 4672:5184], scr[:, 5184:5696]]
    gslot = [mergedT[:, :, :].rearrange("p k t -> p (k t)")[:, 4096 * i:4096 * (i + 1)].rearrange("p (f t) -> p f t", f=4)
             for i in range(3)]
    ps = [nc.alloc_psum_tensor("ps%d" % i, [128, 512], F32) for i in range(8)]
    t_ps = P.toks(8)
    for t in t_ps:
        t.excl = True
    t_pmod = t_ps[0]

    t_cols = P.dtok("cols")
    t_modc, t_prm, t_scb, t_ident, t_epsc, t_ones, t_fd = P.toks(7)
    t_stat = P.toks(64)
    t_xs = [P.dtok("xs0"), P.dtok("xs1")]
    t_xn = P.toks(2)
    t_hT = [[P.tok() for _ in range(NT // 128)] for _ in range(KC)]
    t_A = P.toks(8)
    t_mg = [[P.tok() for _ in range(2)] for _ in range(KC)]
    t_WmT = P.dtok("wmT")
    t_bsb = P.dtok("bsb")
    t_Cg, = P.toks(1)
    t_tmp = P.toks(2)
    t_dbg = P.dtok("dbg")
    d_out = []
    all_hT = [t for row in t_hT for t in row]
    t_qg = [[P.tok() for _ in range(2)] for _ in range(4)]
    t_kvg = [[P.tok() for _ in range(4)] for _ in range(2)]
    t_rq = P.toks(2)
    t_rkv = P.toks(4)
    t_rkvc, t_mask, t_cos, t_sin, t_posi, t_angf, t_targ = P.toks(7)
    t_posi = P.dtok("posi")
    t_kpe = P.toks(4)
    t_sqb = P.toks(2)
    t_QnT, t_Qr, t_pt = P.toks(2), P.toks(2), P.toks(4)
    t_KnT = P.toks(4)
    t_Vh, t_rec, t_rt1 = P.toks(3)
    t_oT = [[P.tok() for _ in range(2)] for _ in range(16)]
    t_KnT_b = P.toks(4)
    t_Vh_b = P.tok()
    KV = ((KnT, Vh, t_KnT, t_Vh), (KnT_b, Vh_b, t_KnT_b, t_Vh_b))
    t_yT = [[P.tok() for _ in range(2)] for _ in range(KC)]
    t_h2T = [[P.tok() for _ in range(S // 128)] for _ in range(KC)]
    t_sqe = P.toks(2)
    t_rse = P.toks(2)
    t_hb, t_yb = P.toks(2), P.toks(2)
    t_sg, = P.toks(1)
    t_gvs = P.toks(2)
    t_gslot = [P.dtok("gs%d" % i) for i in range(3)]
    t_gvst = [P.dtok("gvst%d" % i) for i in range(2)]
    d_x1 = P.toks(S // 128)
    d_gv = [[P.tok() for _ in range(4)] for _ in range(FC)]
    t_st = [P.dtok("st0"), P.dtok("st1")]
    all_oT = [t for r in t_oT for t in r]
    big0_toks = all_hT + t_A + all_oT + [t for r in t_yT for t in r] + [t for r in t_h2T for t in r]
    mg_toks = [t for r in t_mg for t in r] + t_gslot
    scr_toks = t_xs + t_xn + [t_WmT, t_bsb, t_Cg] + t_tmp + t_sqb + [t_posi, t_angf, t_targ] + t_QnT + t_Qr + t_pt \
        + t_KnT + [t_Vh, t_rec, t_rt1] + t_sqe + t_rse + t_hb + t_yb + [t_sg] + t_gvs

    wp = WPool(P, nc, 3, KC * 512)
    rot = {"n": 0, "lo": 4, "cnt": 4}

    def gbank():
        b = rot["lo"] + rot["n"] % rot["cnt"]
        rot["n"] += 1
        return b

    prot = {"n": 0}

    def pbank():
        b = 1 + prot["n"] % 7
        prot["n"] += 1
        return b

    def fence(toks):
        P.op("pool", lambda e: e.memset(fdummy[:, 0:1], 0.0), writes=list(toks) + [t_fd])

    def dump(name, src_ap, toks):
        if name in dbg_d:
            d = P.tok()
            P.dma("sp", lambda e: e.dma_start(out=dbg_d[name], in_=src_ap), reads=list(toks), writes=[d], sem_tok=t_dbg)
            d_out.append(d)

    def wsrc(w_d, c0, ncols):
        return w_d[:, c0:c0 + ncols].rearrange("(k p) o -> p k o", p=128)

    def consts(plan):
        if plan:
            return
        P.op("pool", lambda e: e.memset(epsc[:, :], EPS), writes=[t_epsc])
        P.op("pool", lambda e: e.memset(ones[:, :], 1.0), writes=[t_ones])
        P.dma("sp", lambda e: e.dma_start(out=cols[:, :], in_=cols_d), writes=[t_cols])
        P.op("pool", lambda e: e.memset(identf[:, :], 1.0), writes=[t_ident])
        P.op("pool", lambda e: e.affine_select(out=identf[:, :], in_=identf[:, :], pattern=[[-1, 128]],
                                                compare_op=ALU.is_equal, fill=0.0, base=0, channel_multiplier=1),
             reads=[t_ident], writes=[t_ident])
        P.op("pool", lambda e: e.tensor_copy(out=ident[:, :], in_=identf[:, :]), reads=[t_ident], writes=[t_ident])
        P.op("pool", lambda e: e.memset(masktmp[:, :], 0.0), writes=[t_mask])
        P.op("pool", lambda e: e.affine_select(out=masktmp[:, :], in_=masktmp[:, :], pattern=[[1, 128]],
                                                compare_op=ALU.is_ge, fill=-30000.0, base=0, channel_multiplier=-1),
             reads=[t_mask], writes=[t_mask])
        P.op("pool", lambda e: e.tensor_copy(out=maskneg[:, :], in_=masktmp[:, :]), reads=[t_mask], writes=[t_mask])

    MOD_PARTS = ((0, 8), (8, 12), (12, 24))

    def mod_blocks(plan, j0, j1):
        pmod = ps[0]
        for j in range(j0, j1):
            if plan:
                wp.declare(("ada", j), wsrc(w_ada_d, j * 512, 512), KC, 512)
                continue
            wv, wt = wp.next(("ada", j))

            def mm(e, wv=wv, j=j):
                for oc in range(4):
                    col = j * 4 + oc
                    for kc in range(KC):
                        ins = e.matmul(pmod[:, col:col + 1], lhsT=wv[:, kc, oc * 128:(oc + 1) * 128],
                                       rhs=scb[:, kc:kc + 1], start=(kc == 0), stop=(kc == KC - 1))
                return ins
            P.op("pe", mm, reads=[wt, t_scb], writes=[t_pmod])

    def phase_mod(plan, part, stream=True):
        j0, j1 = MOD_PARTS[part]
        pmod = ps[0]
        if not plan and part == 0:
            P.op("act", lambda e: e.activation(out=scb[:, :], in_=cols[:, C_C:C_C + KC], func=AF.Silu),
                 reads=[t_cols], writes=[t_scb])
        if stream:
            mod_blocks(plan, j0, j1)
        if plan:
            return
        c0, c1 = j0 * 4, j1 * 4
        P.op("dve", lambda e: e.tensor_tensor(out=modc[:, c0:c1], in0=pmod[:, c0:c1], in1=cols[:, C_BADA + c0:C_BADA + c1],
                                              op=ALU.add), reads=[t_pmod, t_cols], writes=[t_modc])
        def gs_sh(sl, c_sh, c_sc, c_g):
            P.op("dve", lambda e: e.scalar_tensor_tensor(out=prm[:, 3 * sl + 0, :], in0=modc[:, c_sc:c_sc + KC], scalar=1.0,
                                                         in1=cols[:, c_g:c_g + KC], op0=ALU.add, op1=ALU.mult),
                 reads=[t_modc, t_cols], writes=[t_prm])
            P.op("dve", lambda e: e.tensor_copy(out=prm[:, 3 * sl + 1, :], in_=modc[:, c_sh:c_sh + KC]),
                 reads=[t_modc], writes=[t_prm])

        def gg(sl, c_ga, c_gp):
            P.op("dve", lambda e: e.tensor_tensor(out=prm[:, 3 * sl + 2, :], in0=modc[:, c_ga:c_ga + KC],
                                                  in1=cols[:, c_gp:c_gp + KC], op=ALU.mult),
                 reads=[t_modc, t_cols], writes=[t_prm])
        if part == 0:
            gs_sh(0, 0, 16, C_G1)
        elif part == 1:
            gg(0, 32, C_GP1)
        else:
            gs_sh(1, 48, 64, C_G2)
            gg(1, 80, C_GP2)
            dump("modc", modc[:, :], [t_modc])

    def rstd_from(sq, rs, t_sq, t_rs, n):
        P.op("act", lambda e: e.activation(out=rs, in_=sq, func=AF.Sqrt, scale=1.0 / n, bias=epsc[:, 0:1]),
             reads=[t_sq, t_epsc], writes=[t_rs])
        P.op("dve", lambda e: e.reciprocal(out=rs, in_=rs), reads=[t_rs], writes=[t_rs])

    def prologue(plan, src_d, T0, ntiles, sl, dst, t_dst, src_toks=None, scaled=True):
        if plan:
            return
        fence(scr_toks)

        def stage_a(i):
            b = i % 2
            r0 = T0 + i * 128
            P.dma("sp", lambda e, b=b, r0=r0: e.dma_start(out=xs[b], in_=src_d[r0:r0 + 128, :]),
                  reads=([src_toks[r0 // 128]] if src_toks else []), writes=[t_xs[b]])
            sq, rs = stat[:, 2 * b:2 * b + 1], stat[:, 2 * b + 1:2 * b + 2]
            P.op("act", lambda e, b=b, sq=sq: e.activation(out=xn[b], in_=xs[b], func=AF.Square, accum_out=sq),
                 reads=[t_xs[b]], writes=[t_xn[b], t_stat[2 * b]])
            rstd_from(sq, rs, t_stat[2 * b], t_stat[2 * b + 1], D)
            P.op("dve", lambda e, b=b, rs=rs: e.tensor_scalar(out=xn[b], in0=xs[b], scalar1=rs, scalar2=None, op0=ALU.mult),
                 reads=[t_xs[b], t_stat[2 * b + 1]], writes=[t_xn[b]])

        def stage_b(i):
            b = i % 2
            for half in range(2):
                pb = 2 * b + half
                pv = ps[pb][:, :].bitcast(BF16)

                def tr(e, b=b, half=half, pv=pv):
                    for k in range(8):
                        kc = half * 8 + k
                        ins = e.transpose(out=pv[:, k * 128:(k + 1) * 128], in_=xn[b][:, kc * 128:(kc + 1) * 128],
                                          identity=ident[:, :])
                    return ins
                P.op("pe", tr, reads=[t_xn[b], t_ident], writes=[t_ps[pb]])
                for k in range(8):
                    kc = half * 8 + k
                    if not scaled:
                        if half == 0:
                            P.op("act", lambda e, kc=kc, k=k, pv=pv, i=i: e.activation(
                                out=dst[:, kc, i * 128:(i + 1) * 128], in_=pv[:, k * 128:(k + 1) * 128], func=AF.Copy),
                                reads=[t_ps[pb]], writes=[t_dst[kc][i]])
                        else:
                            P.op("dve", lambda e, kc=kc, k=k, pv=pv, i=i: e.tensor_copy(
                                out=dst[:, kc, i * 128:(i + 1) * 128], in_=pv[:, k * 128:(k + 1) * 128]),
                                reads=[t_ps[pb]], writes=[t_dst[kc][i]])
                        continue
                    if half == 0:
                        P.op("act", lambda e, kc=kc, k=k, pv=pv, i=i: e.activation(
                            out=dst[:, kc, i * 128:(i + 1) * 128], in_=pv[:, k * 128:(k + 1) * 128], func=AF.Identity,
                            scale=prm[:, 3 * sl + 0, kc:kc + 1], bias=prm[:, 3 * sl + 1, kc:kc + 1]),
                            reads=[t_ps[pb], t_prm], writes=[t_dst[kc][i]])
                    else:
                        P.op("dve", lambda e, kc=kc, k=k, pv=pv, i=i: e.tensor_scalar(
                            out=dst[:, kc, i * 128:(i + 1) * 128], in0=pv[:, k * 128:(k + 1) * 128],
                            scalar1=prm[:, 3 * sl + 0, kc:kc + 1], scalar2=prm[:, 3 * sl + 1, kc:kc + 1],
                            op0=ALU.mult, op1=ALU.add), reads=[t_ps[pb], t_prm], writes=[t_dst[kc][i]])

        stage_a(0)
        for i in range(ntiles):
            if i + 1 < ntiles:
                stage_a(i + 1)
            stage_b(i)

    def scale_shift_inplace(plan, sl, dst, t_dst, ntiles):
        if plan:
            return
        for kc in range(KC):
            v = dst[:, kc, 0:ntiles * 128]
            tk = t_dst[kc][0:ntiles]
            if kc % 2 == 0:
                P.op("dve", lambda e, v=v, kc=kc: e.tensor_scalar(out=v, in0=v, scalar1=prm[:, 3 * sl + 0, kc:kc + 1],
                                                                   scalar2=prm[:, 3 * sl + 1, kc:kc + 1], op0=ALU.mult, op1=ALU.add),
                     reads=[t_prm], writes=tk)
            else:
                P.op("act", lambda e, v=v, kc=kc: e.activation(out=v, in_=v, func=AF.Identity, scale=prm[:, 3 * sl + 0, kc:kc + 1],
                                                                bias=prm[:, 3 * sl + 1, kc:kc + 1]), reads=[t_prm], writes=tk)

    def gmlp_setup(plan):
        if plan:
            return
        fence(scr_toks)
        P.dma("pool", lambda e: e.dma_start(out=WmT, in_=wsT_d), writes=[t_WmT])
        P.op("pool", lambda e: e.affine_select(out=WmT, in_=WmT, pattern=[[0, 16], [1, 128]], compare_op=ALU.is_ge,
                                                fill=0.0, base=0, channel_multiplier=-1), reads=[t_WmT], writes=[t_WmT])
        P.dma("sp", lambda e: e.dma_start(out=bsb.rearrange("p g q -> p (g q)"), in_=bs_d.partition_broadcast(128)),
              writes=[t_bsb])
        for q4 in range(4):
            def mm(e, q4=q4):
                for k in range(4):
                    g = q4 * 4 + k
                    ins = e.matmul(ps[q4][:, k * 128:(k + 1) * 128], lhsT=ones[:, :], rhs=WmT[:, g, :], start=True, stop=True)
                return ins
            P.op("pe", mm, reads=[t_ones, t_WmT], writes=[t_ps[q4]])
            for k in range(4):
                g = q4 * 4 + k
                P.op("dve", lambda e, q4=q4, k=k, g=g: e.scalar_tensor_tensor(
                    out=Cg[:, g, :], in0=ps[q4][:, k * 128:(k + 1) * 128], scalar=cols[:, C_LNB + g:C_LNB + g + 1],
                    in1=bsb[:, g, :], op0=ALU.mult, op1=ALU.add), reads=[t_ps[q4], t_cols, t_bsb], writes=[t_Cg])

    def vphase(plan, blk):
        for j in range(4):
            if plan:
                wp.declare(("v", blk, j), wsrc(w_in_d, O_V + j * 512, 512), KC, 512)
                continue
            wv, wt = wp.next(("v", blk, j))
            for n in range(8):
                pb = gbank()

                def mm(e, wv=wv, n=n, pb=pb):
                    for kc in range(KC):
                        ins = e.matmul(ps[pb][:, :], lhsT=hT[:, kc, n * 128:(n + 1) * 128], rhs=wv[:, kc, :],
                                       start=(kc == 0), stop=(kc == KC - 1))
                    return ins
                P.op("pe", mm, reads=[wt] + [t_hT[kc][n] for kc in range(KC)], writes=[t_ps[pb]])
                P.op("act", lambda e, n=n, j=j, pb=pb: e.activation(out=A3[:, n, j * 512:(j + 1) * 512], in_=ps[pb][:, :],
                                                                      func=AF.Gelu_apprx_tanh),
                     reads=[t_ps[pb]], writes=[t_A[n]])

    def mixing(plan):
        if plan:
            return

        def cols6(n):
            c = 32 + 6 * (n % 3)
            return [stat[:, c + k:c + k + 1] for k in range(6)], t_stat[c:c + 6]

        def a_act(n):
            (s1, s2, mu, var, rsd, nb), ts = cols6(n)
            P.op("act", lambda e: e.activation(out=junkv, in_=A3[:, n, :], func=AF.Identity, accum_out=s1),
                 reads=[t_A[n]], writes=[t_tmp[0], t_tmp[1], ts[0]])
            P.op("act", lambda e: e.activation(out=junkv, in_=A3[:, n, :], func=AF.Square, accum_out=s2),
                 reads=[t_A[n]], writes=[t_tmp[0], t_tmp[1], ts[1]])

        def a_dve1(n):
            (s1, s2, mu, var, rsd, nb), ts = cols6(n)
            P.op("dve", lambda e: e.tensor_scalar(out=mu, in0=s1, scalar1=1.0 / 2048, scalar2=None, op0=ALU.mult),
                 reads=[ts[0]], writes=[ts[2]])
            P.op("dve", lambda e: e.tensor_tensor(out=var, in0=mu, in1=mu, op=ALU.mult), reads=[ts[2]], writes=[ts[3]])
            P.op("dve", lambda e: e.scalar_tensor_tensor(out=var, in0=s2, scalar=1.0 / 2048, in1=var, op0=ALU.mult,
                                                         op1=ALU.subtract), reads=[ts[1], ts[3]], writes=[ts[3]])

        def a_sqrt(n):
            (s1, s2, mu, var, rsd, nb), ts = cols6(n)
            P.op("act", lambda e: e.activation(out=rsd, in_=var, func=AF.Sqrt, scale=1.0, bias=epsc[:, 0:1]),
                 reads=[ts[3], t_epsc], writes=[ts[4]])

        def a_dve2(n):
            (s1, s2, mu, var, rsd, nb), ts = cols6(n)
            P.op("dve", lambda e: e.reciprocal(out=rsd, in_=rsd), reads=[ts[4]], writes=[ts[4]])
            P.op("dve", lambda e: e.scalar_tensor_tensor(out=nb, in0=mu, scalar=-1.0, in1=rsd, op0=ALU.mult, op1=ALU.mult),
                 reads=[ts[2], ts[4]], writes=[ts[5]])

        def b_norm_mm(n):
            (s1, s2, mu, var, rsd, nb), ts = cols6(n)
            P.op("dve", lambda e: e.tensor_scalar(out=A3[:, n, :], in0=A3[:, n, :], scalar1=rsd, scalar2=nb,
                                                  op0=ALU.mult, op1=ALU.add), reads=[ts[4], ts[5]], writes=[t_A[n]])
            for q4 in range(4):
                pb = 4 * (n % 2) + q4

                def mm(e, q4=q4, pb=pb):
                    for k in range(4):
                        g = q4 * 4 + k
                        ins = e.matmul(ps[pb][:, k * 128:(k + 1) * 128], lhsT=A3[:, n, g * 128:(g + 1) * 128],
                                       rhs=WmT[:, g, :], start=True, stop=True)
                    return ins
                P.op("pe", mm, reads=[t_A[n], t_WmT], writes=[t_ps[pb]])

        def b_ev(n):
            for q4 in range(4):
                pb = 4 * (n % 2) + q4
                P.op("dve", lambda e, pb=pb, q4=q4: e.tensor_copy(
                    out=A4[:, n, 4 * q4:4 * q4 + 4, :].rearrange("p g q -> p (g q)"), in_=ps[pb][:, :]),
                    reads=[t_ps[pb]], writes=[t_A[n]])

        a_act(0); a_dve1(0); a_sqrt(0); a_dve2(0); b_norm_mm(0)
        a_act(1); a_dve1(1)
        for k in range(8):
            if k + 1 < 8:
                a_sqrt(k + 1)
                a_dve2(k + 1)
                b_norm_mm(k + 1)
            b_ev(k)
            if k + 2 < 8:
                a_act(k + 2)
                a_dve1(k + 2)

    def uphase(plan, blk):
        for j in range(4):
            if plan:
                wp.declare(("u", blk, j), wsrc(w_in_d, O_U + j * 512, 512), KC, 512)
                continue
            wv, wt = wp.next(("u", blk, j))
            for oc in range(4):
                g = j * 4 + oc
                for tt in range(2):
                    pb = gbank()
                    sl_ = (oc * 2 + tt) % 2

                    def mm(e, wv=wv, oc=oc, tt=tt, pb=pb):
                        for kc in range(KC):
                            ins = e.matmul(ps[pb][:, :], lhsT=wv[:, kc, oc * 128:(oc + 1) * 128],
                                           rhs=hT[:, kc, tt * 512:(tt + 1) * 512], start=(kc == 0), stop=(kc == KC - 1))
                        return ins
                    P.op("pe", mm, reads=[wt] + [t_hT[kc][n] for kc in range(KC) for n in range(4 * tt, 4 * tt + 4)],
                         writes=[t_ps[pb]])
                    P.op("act", lambda e, pb=pb, sl_=sl_: e.activation(out=tmpb[sl_], in_=ps[pb][:, :], func=AF.Gelu_apprx_tanh),
                         reads=[t_ps[pb]], writes=[t_tmp[sl_]])
                    av = A4[:, 4 * tt:4 * tt + 4, g, :]
                    P.op("dve", lambda e, av=av, g=g: e.scalar_tensor_tensor(
                        out=av, in0=av, scalar=cols[:, C_LNG + g:C_LNG + g + 1], in1=Cg[:, g:g + 1, :].broadcast_to([128, 4, 128]),
                        op0=ALU.mult, op1=ALU.add), reads=[t_cols, t_Cg], writes=t_A[4 * tt:4 * tt + 4])
                    P.op("dve", lambda e, av=av, sl_=sl_: e.tensor_tensor(
                        out=av, in0=tmpb[sl_].rearrange("p (n q) -> p n q", n=4), in1=av, op=ALU.mult),
                        reads=[t_tmp[sl_]], writes=t_A[4 * tt:4 * tt + 4])

    def branch_a(plan, blk):
        for j in range(8):
            if plan:
                wp.declare(("ag", blk, j), wsrc(w_ag_d, j * 512, 512), KC, 512)
                continue
            if j == 0:
                fence(scr_toks)
            wv, wt = wp.next(("ag", blk, j))
            for oc in range(2):
                o = j * 2 + oc
                for tt in range(2):
                    pg, py = gbank(), gbank()
                    sl_ = (oc * 2 + tt) % 2

                    def mmg(e, wv=wv, oc=oc, tt=tt, pg=pg):
                        for kc in range(KC):
                            ins = e.matmul(ps[pg][:, :], lhsT=wv[:, kc, 256 + oc * 128:256 + (oc + 1) * 128],
                                           rhs=hT[:, kc, tt * 512:(tt + 1) * 512], start=(kc == 0), stop=(kc == KC - 1))
                        return ins
                    P.op("pe", mmg, reads=[wt] + [t_hT[kc][n] for kc in range(KC) for n in range(4 * tt, 4 * tt + 4)],
                         writes=[t_ps[pg]])
                    P.op("act", lambda e, pg=pg, sl_=sl_: e.activation(out=tmpf[sl_], in_=ps[pg][:, :], func=AF.Sigmoid),
                         reads=[t_ps[pg]], writes=[t_tmp[sl_]])

                    def mmy(e, wv=wv, oc=oc, tt=tt, py=py):
                        for g in range(16):
                            ins = e.matmul(ps[py][:, :], lhsT=wv[:, g, oc * 128:(oc + 1) * 128],
                                           rhs=A4[:, 4 * tt:4 * tt + 4, g, :], start=(g == 0), stop=(g == 15))
                        return ins
                    P.op("pe", mmy, reads=[wt] + t_A[4 * tt:4 * tt + 4], writes=[t_ps[py]])
                    P.op("dve", lambda e, o=o, tt=tt, py=py, sl_=sl_: e.tensor_tensor(
                        out=mergedT[:, o, tt * 512:(tt + 1) * 512], in0=ps[py][:, :], in1=tmpf[sl_], op=ALU.mult),
                        reads=[t_ps[py], t_tmp[sl_]], writes=[t_mg[o][tt]])

    SCALE = float(192 ** -0.5)
    PI = float(np.pi)

    def latents(plan, blk):
        T0 = blk * NT
        wblk = []
        wp.live = 2
        for j in range(2):
            if plan:
                wp.declare(("lat", blk, j), wsrc(w_lat_d, j * 512, 512), KC, 512)
            else:
                wblk.append(wp.next(("lat", blk, j)))
        if plan:
            wp.live = 1
            return
        fence(scr_toks)
        def rope_tables():
            P.dma("sp", lambda e: e.dma_start(out=posi, in_=pos_d[:, T0:T0 + NT].partition_broadcast(128)), writes=[t_posi])
            P.op("dve", lambda e: e.tensor_copy(out=angf, in_=posi), reads=[t_posi], writes=[t_angf])
            P.op("dve", lambda e: e.tensor_scalar(out=angf, in0=angf, scalar1=cols[:, C_INV:C_INV + 1], scalar2=None, op0=ALU.mult),
                 reads=[t_angf, t_cols], writes=[t_angf])
            HI = 6.28125
            LO = float(2 * np.pi - 6.28125)
            P.op("dve", lambda e: e.tensor_scalar(out=targ, in0=angf, scalar1=float(1 / (2 * np.pi)), scalar2=None, op0=ALU.mult),
                 reads=[t_angf], writes=[t_targ])
            P.op("dve", lambda e: e.tensor_copy(out=posi, in_=targ), reads=[t_targ], writes=[t_posi])
            P.op("dve", lambda e: e.tensor_copy(out=targ, in_=posi), reads=[t_posi], writes=[t_targ])
            P.op("dve", lambda e: e.scalar_tensor_tensor(out=angf, in0=targ, scalar=-HI, in1=angf, op0=ALU.mult, op1=ALU.add),
                 reads=[t_targ, t_angf], writes=[t_angf])
            P.op("dve", lambda e: e.scalar_tensor_tensor(out=angf, in0=targ, scalar=-LO, in1=angf, op0=ALU.mult, op1=ALU.add),
                 reads=[t_targ, t_angf], writes=[t_angf])
            P.op("dve", lambda e: e.tensor_scalar(out=angf, in0=angf, scalar1=-PI, scalar2=PI, op0=ALU.max, op1=ALU.min),
                 reads=[t_angf], writes=[t_angf])
            P.op("act", lambda e: e.activation(out=sin_s[:, :], in_=angf, func=AF.Sin, scale=cols[:, C_SGN:C_SGN + 1]),
                 reads=[t_angf, t_cols], writes=[t_sin])
            P.op("dve", lambda e: e.tensor_scalar(out=targ, in0=angf, scalar1=PI / 2, scalar2=2 * PI, op0=ALU.is_gt, op1=ALU.mult),
                 reads=[t_angf], writes=[t_targ])
            P.op("dve", lambda e: e.scalar_tensor_tensor(out=targ, in0=angf, scalar=PI / 2, in1=targ, op0=ALU.add, op1=ALU.subtract),
                 reads=[t_angf, t_targ], writes=[t_targ])
            P.op("dve", lambda e: e.tensor_scalar(out=targ, in0=targ, scalar1=-PI, scalar2=PI, op0=ALU.max, op1=ALU.min),
                 reads=[t_targ], writes=[t_targ])
            P.op("act", lambda e: e.activation(out=cos2[:, :], in_=targ, func=AF.Sin), reads=[t_targ], writes=[t_cos])
        (wq, wqt), (wk, wkt) = wblk
        lvl = {"lat0": 0, "lat1": 1, "lat2": 2}.get(stop_after, 3)
        if lvl < 3:
            rope_tables()
        for tt in range(2 if lvl > 0 else 0):
            hts = [t_hT[kc][n] for kc in range(KC) for n in range(4 * tt, 4 * tt + 4)]
            tsl = slice(tt * 512, (tt + 1) * 512)
            asl = slice(T0 + tt * 512, T0 + (tt + 1) * 512)
            at = (T0 // 512) + tt
            pq = 2

            def ones_q(c, pq=pq):
                P.op("pe", lambda e: e.matmul(ps[pq][:, :], lhsT=ones[:, :], rhs=sqb[c % 2], start=(c == 0), stop=(c == 3)),
                     reads=[t_ones, t_sqb[c % 2]], writes=[t_ps[pq]])
            for c in range(4):
                pb = gbank()

                def mm(e, c=c, pb=pb, tsl=tsl):
                    for kc in range(KC):
                        ins = e.matmul(ps[pb][:, :], lhsT=wq[:, kc, c * 128:(c + 1) * 128], rhs=hT[:, kc, tsl],
                                       start=(kc == 0), stop=(kc == KC - 1))
                    return ins
                P.op("pe", mm, reads=[wqt] + hts, writes=[t_ps[pb]])
                P.op("act", lambda e, c=c, pb=pb: e.activation(out=sqb[c % 2], in_=ps[pb][:, :], func=AF.Square),
                     reads=[t_ps[pb]], writes=[t_sqb[c % 2]])
                P.op("act", lambda e, c=c, pb=pb, tsl=tsl: e.activation(out=qgT[:, c, tsl], in_=ps[pb][:, :], func=AF.Copy,
                                                                        scale=cols[:, C_QG + c:C_QG + c + 1]),
                     reads=[t_ps[pb], t_cols], writes=[t_qg[c][tt]])
                if c >= 1:
                    ones_q(c - 1)
            ones_q(3)
            P.op("act", lambda e, tsl=tsl: e.activation(out=tmpf[0], in_=ps[pq][:, :], func=AF.Sqrt, scale=1.0 / 512,
                                                        bias=epsc[:, 0:1]), reads=[t_ps[pq], t_epsc], writes=[t_tmp[0]])
            P.op("dve", lambda e: e.reciprocal(out=tmpf[0], in_=tmpf[0]), reads=[t_tmp[0]], writes=[t_tmp[0]])
            for c in range(4):
                P.op("dve", lambda e, c=c, tsl=tsl: e.tensor_tensor(out=qgT[:, c, tsl], in0=qgT[:, c, tsl], in1=tmpf[0], op=ALU.mult),
                     reads=[t_tmp[0]], writes=[t_qg[c][tt]])
            if lvl < 2:
                continue
            pk = 3

            def ones_k(c, pk=pk):
                P.op("pe", lambda e: e.matmul(ps[pk][:, :], lhsT=ones[:, :], rhs=sqb[c], start=(c == 0), stop=(c == 1)),
                     reads=[t_ones, t_sqb[c]], writes=[t_ps[pk]])
            for c in range(2):
                pb = gbank()

                def mm(e, c=c, pb=pb, tsl=tsl):
                    for kc in range(KC):
                        ins = e.matmul(ps[pb][:, :], lhsT=wk[:, kc, c * 128:(c + 1) * 128], rhs=hT[:, kc, tsl],
                                       start=(kc == 0), stop=(kc == KC - 1))
                    return ins
                P.op("pe", mm, reads=[wkt] + hts, writes=[t_ps[pb]])
                P.op("act", lambda e, c=c, pb=pb: e.activation(out=sqb[c], in_=ps[pb][:, :], func=AF.Square),
                     reads=[t_ps[pb]], writes=[t_sqb[c]])
                P.op("act", lambda e, c=c, pb=pb, asl=asl: e.activation(out=kvgT[:, c, asl], in_=ps[pb][:, :], func=AF.Copy,
                                                                        scale=cols[:, C_KVG + c:C_KVG + c + 1]),
                     reads=[t_ps[pb], t_cols], writes=[t_kvg[c][at]])
                if c >= 1:
                    ones_k(c - 1)
            ones_k(1)
            P.op("act", lambda e: e.activation(out=tmpf[1], in_=ps[pk][:, :], func=AF.Sqrt, scale=1.0 / 256, bias=epsc[:, 0:1]),
                 reads=[t_ps[pk], t_epsc], writes=[t_tmp[1]])
            P.op("dve", lambda e: e.reciprocal(out=tmpf[1], in_=tmpf[1]), reads=[t_tmp[1]], writes=[t_tmp[1]])
            for c in range(2):
                P.op("dve", lambda e, c=c, asl=asl: e.tensor_tensor(out=kvgT[:, c, asl], in0=kvgT[:, c, asl], in1=tmpf[1], op=ALU.mult),
                     reads=[t_tmp[1]], writes=[t_kvg[c][at]])
            if lvl < 3:
                continue
            if tt == 0:
                rope_tables()
            pr_, psw = gbank(), gbank()
            for pbx, c0 in ((pr_, 256), (psw, 384)):
                def mm(e, pbx=pbx, c0=c0, tsl=tsl):
                    for kc in range(KC):
                        ins = e.matmul(ps[pbx][:, :], lhsT=wk[:, kc, c0:c0 + 128], rhs=hT[:, kc, tsl],
                                       start=(kc == 0), stop=(kc == KC - 1))
                    return ins
                P.op("pe", mm, reads=[wkt] + hts, writes=[t_ps[pbx]])
            P.op("dve", lambda e, tsl=tsl, pr_=pr_: e.tensor_tensor(out=tmpf[0], in0=ps[pr_][:, :], in1=cos2[:, tsl], op=ALU.mult),
                 reads=[t_ps[pr_], t_cos], writes=[t_tmp[0]])
            P.op("dve", lambda e, tsl=tsl, psw=psw: e.tensor_tensor(out=tmpf[1], in0=ps[psw][:, :], in1=sin_s[:, tsl], op=ALU.mult),
                 reads=[t_ps[psw], t_sin], writes=[t_tmp[1]])
            P.op("dve", lambda e, asl=asl: e.tensor_tensor(out=kpe_dup[:, asl], in0=tmpf[0], in1=tmpf[1], op=ALU.add),
                 reads=t_tmp, writes=[t_kpe[at]])

    def attention(plan, blk):
        wp.live = 1
        T0 = blk * NT
        nk512 = (T0 + NT) // 512
        ADA2 = (12, 14, 16, 18, 20, 21, 22, 23, 24)
        for pr in range(8):
            if plan:
                wp.declare(("pair", blk, pr), w_pair_d[pr].rearrange("p (k o) -> p k o", k=1), 1, 3072)
                if blk == 0:
                    mod_blocks(True, ADA2[pr], ADA2[pr + 1])
                continue
            rot["lo"], rot["cnt"] = 5, 3
            if pr == 0:
                fence(scr_toks + t_A)
                P.op("pool", lambda e: e.memset(Qr[0][64:128, :], 0.0), writes=[t_Qr[0]])
                P.op("pool", lambda e: e.memset(Qr[1][0:64, :], 0.0), writes=[t_Qr[1]])
            wv_, wt = wp.next(("pair", blk, pr))
            w = wv_[:, 0, :]
            qn = w[:, 0:1024].rearrange("p (k o) -> p k o", k=4)
            qr = w[:, 1024:1536].rearrange("p (k o) -> p k o", k=4)
            qs = w[:, 1536:2048].rearrange("p (k o) -> p k o", k=4)
            kn = w[:, 2048:2560].rearrange("p (k o) -> p k o", k=2)
            vv = w[:, 2560:3072].rearrange("p (k o) -> p k o", k=2)
            qg_all = [t_qg[c][tt] for c in range(4) for tt in range(2)]
            for tt in range(2):
                tsl = slice(tt * 512, (tt + 1) * 512)
                for hd in range(2):
                    pb = pbank()

                    def mm(e, hd=hd, pb=pb, tsl=tsl, qn=qn):
                        for c in range(4):
                            ins = e.matmul(ps[pb][:, :], lhsT=qn[:, c, hd * 128:(hd + 1) * 128], rhs=qgT[:, c, tsl],
                                           start=(c == 0), stop=(c == 3))
                        return ins
                    P.op("pe", mm, reads=[wt] + qg_all, writes=[t_ps[pb]])
                    P.op("dve", lambda e, hd=hd, pb=pb, tsl=tsl: e.tensor_copy(out=QnT[hd][:, tsl], in_=ps[pb][:, :]),
                         reads=[t_ps[pb]], writes=[t_QnT[hd]])
                pr_, psw = pbank(), pbank()
                for pbx, wsel in ((pr_, qr), (psw, qs)):
                    def mm(e, pbx=pbx, wsel=wsel, tsl=tsl):
                        for c in range(4):
                            ins = e.matmul(ps[pbx][:, :], lhsT=wsel[:, c, :], rhs=qgT[:, c, tsl], start=(c == 0), stop=(c == 3))
                        return ins
                    P.op("pe", mm, reads=[wt] + qg_all, writes=[t_ps[pbx]])
                P.op("dve", lambda e, tsl=tsl, pr_=pr_: e.tensor_tensor(out=rt1, in0=ps[pr_][:, :], in1=cos2[:, tsl], op=ALU.mult),
                     reads=[t_ps[pr_], t_cos], writes=[t_rt1])
                P.op("dve", lambda e, tsl=tsl, psw=psw: e.tensor_tensor(out=rec, in0=ps[psw][:, :], in1=sin_s[:, tsl], op=ALU.mult),
                     reads=[t_ps[psw], t_sin], writes=[t_rec])
                P.op("dve", lambda e, tsl=tsl: e.tensor_tensor(out=Qr[0][0:64, tsl], in0=rt1[0:64, :], in1=rec[0:64, :], op=ALU.add),
                     reads=[t_rt1, t_rec], writes=[t_Qr[0]])
                P.op("dve", lambda e, tsl=tsl: e.tensor_tensor(out=Qr[1][64:128, tsl], in0=rt1[64:128, :], in1=rec[64:128, :], op=ALU.add),
                     reads=[t_rt1, t_rec], writes=[t_Qr[1]])
            for hd in range(2):
                KnT_h, Vh_h, t_KnT_h, t_Vh_h = KV[hd]
                for kt in range(nk512):
                    ksl = slice(kt * 512, (kt + 1) * 512)
                    pb = pbank()

                    def mm(e, hd=hd, pb=pb, ksl=ksl, kn=kn):
                        for c in range(2):
                            ins = e.matmul(ps[pb][:, :], lhsT=kn[:, c, hd * 128:(hd + 1) * 128], rhs=kvgT[:, c, ksl],
                                           start=(c == 0), stop=(c == 1))
                        return ins
                    P.op("pe", mm, reads=[wt, t_kvg[0][kt], t_kvg[1][kt]], writes=[t_ps[pb]])
                    P.op("dve", lambda e, pb=pb, ksl=ksl, KnT_h=KnT_h: e.tensor_copy(out=KnT_h[:, ksl], in_=ps[pb][:, :]),
                         reads=[t_ps[pb]], writes=[t_KnT_h[kt]])
            for kt in range(nk512):
                for half in range(2):
                    pb2 = pbank()

                    def mmv(e, pb2=pb2, kt=kt, half=half, vv=vv):
                        for i in range(2):
                            kb = kt * 4 + half * 2 + i
                            for c in range(2):
                                ins = e.matmul(ps[pb2][:, i * 256:(i + 1) * 256], lhsT=kvgT[:, c, kb * 128:(kb + 1) * 128],
                                               rhs=vv[:, c, :], start=(c == 0), stop=(c == 1))
                        return ins
                    P.op("pe", mmv, reads=[wt, t_kvg[0][kt], t_kvg[1][kt]], writes=[t_ps[pb2]])
                    for hd in range(2):
                        Vdst, t_Vdst = KV[hd][1], KV[hd][3]
                        kb0 = kt * 4 + half * 2
                        P.op("act", lambda e, pb2=pb2, hd=hd, kb0=kb0, Vdst=Vdst: e.activation(
                            out=Vdst[:, kb0:kb0 + 2, :],
                            in_=ps[pb2][:, :].rearrange("p (i h d) -> p i h d", i=2, h=2)[:, :, hd, :], func=AF.Copy),
                            reads=[t_ps[pb2]], writes=[t_Vdst])
            for hd in range(2):
                h = 2 * pr + hd
                KnT_h, Vh_h, t_KnT_h, t_Vh_h = KV[hd]
                LOOK = 2
                pend = []
                cnt = 0

                def finalize(tt, po, psm, h=h):
                    P.op("dve", lambda e: e.reciprocal(out=rec, in_=ps[psm][:, :]), reads=[t_ps[psm]], writes=[t_rec])
                    P.op("dve", lambda e: e.tensor_tensor(out=oT[:, h, tt * 512:(tt + 1) * 512], in0=ps[po][:, :], in1=rec,
                                                          op=ALU.mult), reads=[t_ps[po], t_rec], writes=[t_oT[h][tt]])

                def emit_pend(ent, t_Vh_h=t_Vh_h):
                    f, k_, po_, psm_, fin = ent
                    P.op("pe", f, reads=[t_Vh_h, t_pt[k_], t_ones], writes=[t_ps[po_], t_ps[psm_]])
                    if fin is not None:
                        finalize(fin, po_, psm_)

                for tt in range(2):
                    qa = (T0 + 512 * tt) // 128
                    nkb = qa + 4
                    po, psm = ((2, 3), (1, 4))[tt]
                    for kb in range(nkb):
                        i = kb - qa
                        c0 = 128 * i if i > 0 else 0
                        qsl = slice(tt * 512 + c0, (tt + 1) * 512)
                        pb = gbank()
                        sl_ = cnt % 4
                        cnt += 1

                        def mms(e, hd=hd, kb=kb, i=i, c0=c0, qsl=qsl, pb=pb, KnT_h=KnT_h):
                            e.matmul(ps[pb][:, c0:512], lhsT=KnT_h[:, kb * 128:(kb + 1) * 128], rhs=QnT[hd][:, qsl],
                                     start=True, stop=False)
                            ins = e.matmul(ps[pb][:, c0:512], lhsT=kpe_dup[:, kb * 128:(kb + 1) * 128], rhs=Qr[hd][:, qsl],
                                           start=False, stop=(i < 0))
                            if i >= 0:
                                ins = e.matmul(ps[pb][:, c0:c0 + 128], lhsT=ident[:, :], rhs=maskneg[:, :], start=False, stop=True)
                            return ins
                        P.op("pe", mms, reads=[t_KnT_h[kb // 4], t_QnT[hd], t_Qr[hd], t_kpe[kb // 4], t_ident, t_mask],
                             writes=[t_ps[pb]])
                        P.op("act", lambda e, pb=pb, c0=c0, sl_=sl_: e.activation(out=ptile[sl_][:, c0:512], in_=ps[pb][:, c0:512],
                                                                                   func=AF.Exp, scale=SCALE),
                             reads=[t_ps[pb]], writes=[t_pt[sl_]])

                        def mmo(e, kb=kb, c0=c0, sl_=sl_, nkb=nkb, po=po, psm=psm, Vh_h=Vh_h):
                            e.matmul(ps[po][:, c0:512], lhsT=Vh_h[:, kb, :], rhs=ptile[sl_][:, c0:512], start=(kb == 0),
                                     stop=(kb == nkb - 1))
                            return e.matmul(ps[psm][:, c0:512], lhsT=ones[:, :], rhs=ptile[sl_][:, c0:512], start=(kb == 0),
                                            stop=(kb == nkb - 1))
                        pend.append((mmo, sl_, po, psm, tt if kb == nkb - 1 else None))
                        if len(pend) > LOOK:
                            emit_pend(pend.pop(0))
                for ent in pend:
                    emit_pend(ent)
            if blk == 0:
                mod_blocks(False, ADA2[pr], ADA2[pr + 1])
            rot["lo"], rot["cnt"] = 4, 4

    def branch_b(plan, blk):
        for j in range(8):
            if plan:
                wp.declare(("bg", blk, j), wsrc(w_bg_d, j * 512, 512), KC, 512)
                continue
            if j == 0:
                fence(scr_toks)
            wv, wt = wp.next(("bg", blk, j))
            for oc in range(2):
                o = j * 2 + oc
                for tt in range(2):
                    pg, py = gbank(), gbank()
                    sl_ = (oc * 2 + tt) % 2

                    def mmg(e, wv=wv, oc=oc, tt=tt, pg=pg):
                        for kc in range(KC):
                            ins = e.matmul(ps[pg][:, :], lhsT=wv[:, kc, 256 + oc * 128:256 + (oc + 1) * 128],
                                           rhs=hT[:, kc, tt * 512:(tt + 1) * 512], start=(kc == 0), stop=(kc == KC - 1))
                        return ins
                    P.op("pe", mmg, reads=[wt] + [t_hT[kc][n] for kc in range(KC) for n in range(4 * tt, 4 * tt + 4)],
                         writes=[t_ps[pg]])
                    P.op("act", lambda e, pg=pg, sl_=sl_: e.activation(out=tmpf[sl_], in_=ps[pg][:, :], func=AF.Sigmoid),
                         reads=[t_ps[pg]], writes=[t_tmp[sl_]])

                    def mmy(e, wv=wv, oc=oc, tt=tt, py=py):
                        for g in range(16):
                            ins = e.matmul(ps[py][:, :], lhsT=wv[:, g, oc * 128:(oc + 1) * 128],
                                           rhs=oT[:, g, tt * 512:(tt + 1) * 512], start=(g == 0), stop=(g == 15))
                        return ins
                    P.op("pe", mmy, reads=[wt] + [t_oT[h][tt] for h in range(16)], writes=[t_ps[py]])
                    P.op("dve", lambda e, py=py, sl_=sl_: e.tensor_tensor(out=tmpf[sl_], in0=ps[py][:, :], in1=tmpf[sl_], op=ALU.mult),
                         reads=[t_ps[py], t_tmp[sl_]], writes=[t_tmp[sl_]])
                    P.op("dve", lambda e, o=o, tt=tt, sl_=sl_: e.tensor_tensor(
                        out=mergedT[:, o, tt * 512:(tt + 1) * 512], in0=mergedT[:, o, tt * 512:(tt + 1) * 512], in1=tmpf[sl_],
                        op=ALU.add), reads=[t_tmp[sl_], t_mg[o][tt]], writes=[t_mg[o][tt]])

    def wout_phase(plan, blk):
        for j in range(4):
            if plan:
                wp.declare(("wo", blk, j), wsrc(w_out_d, j * 512, 512), KC, 512)
                continue
            if j == 0:
                fence(scr_toks + big0_toks)
            wv, wt = wp.next(("wo", blk, j))
            for oc in range(4):
                o = j * 4 + oc
                for tt in range(2):
                    pb = gbank()

                    def mm(e, wv=wv, oc=oc, tt=tt, pb=pb):
                        for kc in range(KC):
                            ins = e.matmul(ps[pb][:, :], lhsT=wv[:, kc, oc * 128:(oc + 1) * 128],
                                           rhs=mergedT[:, kc, tt * 512:(tt + 1) * 512], start=(kc == 0), stop=(kc == KC - 1))
                        return ins
                    P.op("pe", mm, reads=[wt] + [t_mg[kc][tt] for kc in range(KC)], writes=[t_ps[pb]])
                    P.op("act", lambda e, o=o, tt=tt, pb=pb: e.activation(out=yT[:, o, tt * 512:(tt + 1) * 512], in_=ps[pb][:, :],
                                                                           func=AF.Copy), reads=[t_ps[pb]], writes=[t_yT[o][tt]])

    def epilogue(plan, T0, sl, res_d, res_toks):
        if plan:
            return
        fence(scr_toks)
        psc = ps[3]
        nt = NT // 128
        for o in range(KC):
            k = o % 2
            P.op("act", lambda e, o=o, k=k: e.activation(out=sqw[k], in_=yT[:, o, :], func=AF.Square),
                 reads=[t_yT[o][0], t_yT[o][1]], writes=[t_sqe[k]])

            def mmc(e, o=o, k=k):
                for n in range(nt):
                    ins = e.matmul(psc[:, n:n + 1], lhsT=sqw[k][:, n * 128:(n + 1) * 128], rhs=ones[:, 0:1],
                                   start=(o == 0 and n == 0), stop=(o == KC - 1 and n == nt - 1))
                return ins
            P.op("pe", mmc, reads=[t_ones, t_sqe[k]], writes=[t_ps[3]])
            P.op("dve", lambda e, o=o: e.tensor_scalar(out=yT[:, o, :], in0=yT[:, o, :], scalar1=prm[:, 3 * sl + 2, o:o + 1],
                                                        scalar2=None, op0=ALU.mult),
                 reads=[t_prm, t_sqe[k]], writes=[t_yT[o][0], t_yT[o][1]])
        rcol = stat[:, 24:24 + nt]
        P.op("act", lambda e: e.activation(out=rcol, in_=psc[:, 0:nt], func=AF.Sqrt, scale=1.0 / D, bias=epsc[:, 0:1]),
             reads=[t_ps[3], t_epsc], writes=[t_stat[24]])
        P.op("dve", lambda e: e.reciprocal(out=rcol, in_=rcol), reads=[t_stat[24]], writes=[t_stat[24]])
        hb_ = 0

        def load(n):
            b, r0 = n % 2, T0 + n * 128
            P.dma("sp", lambda e: e.dma_start(out=xs[b], in_=res_d[r0:r0 + 128, :]),
                  reads=([res_toks[r0 // 128]] if res_toks else []), writes=[t_xs[b]])

        load(0)
        load(1)
        for n in range(nt):
            b = n % 2
            r0 = T0 + n * 128
            tt = n // 4
            for q4 in range(4):
                pb = 4 + hb_ % 4
                hb_ += 1

                def tr(e, n=n, q4=q4, pb=pb):
                    for k in range(4):
                        o = q4 * 4 + k
                        ins = e.transpose(out=ps[pb][:, k * 128:(k + 1) * 128], in_=yT[:, o, n * 128:(n + 1) * 128],
                                          identity=identf[:, :])
                    return ins
                P.op("pe", tr, reads=[t_yT[q4 * 4 + k][tt] for k in range(4)] + [t_ident], writes=[t_ps[pb]])
                P.op("dve", lambda e, b=b, q4=q4, pb=pb, n=n: e.scalar_tensor_tensor(
                    out=xs[b][:, q4 * 512:(q4 + 1) * 512], in0=ps[pb][:, :], scalar=rcol[:, n:n + 1],
                    in1=xs[b][:, q4 * 512:(q4 + 1) * 512], op0=ALU.mult, op1=ALU.add),
                    reads=[t_ps[pb], t_stat[24]], writes=[t_xs[b]])
            P.dma("sp", lambda e, b=b, r0=r0: e.dma_start(out=out_d[r0:r0 + 128, :], in_=xs[b]),
                  reads=[t_xs[b]], writes=[d_x1[r0 // 128]], sem_tok=t_st[b])
            if n + 2 < nt:
                load(n + 2)

    def ffn_up(plan):
        for j in range(FC // 2):
            if plan:
                wp.declare(("up", j), wsrc(w_up2_d, j * 512, 512), KC, 512)
                continue
            if j == 0:
                fence(scr_toks)
            wv, wt = wp.next(("up", j))
            for f2 in range(2):
                fc = 2 * j + f2
                for tt in range(4):
                    tsl = slice(tt * 512, (tt + 1) * 512)
                    hts = [t_h2T[kc][n] for kc in range(KC) for n in range(4 * tt, 4 * tt + 4)]
                    for gvsel in range(2):
                        pb = gbank()
                        ch = fc + FC * gvsel
                        c0 = f2 * 256 + gvsel * 128

                        def mm(e, wv=wv, c0=c0, tsl=tsl, pb=pb):
                            for kc in range(KC):
                                ins = e.matmul(ps[pb][:, :], lhsT=wv[:, kc, c0:c0 + 128], rhs=h2T[:, kc, tsl],
                                               start=(kc == 0), stop=(kc == KC - 1))
                            return ins
                        P.op("pe", mm, reads=[wt] + hts, writes=[t_ps[pb]])
                        hb, yb = hbuf[gvsel], ybuf[gvsel]
                        if tt == 0:
                            P.op("dve", lambda e, hb=hb: e.memset(hb[:, 0:2], 0.0), writes=[t_hb[gvsel]])
                        P.op("act", lambda e, hb=hb, pb=pb: e.activation(out=hb[:, 2:514], in_=ps[pb][:, :], func=AF.Copy),
                             reads=[t_ps[pb]], writes=[t_hb[gvsel]])
                        P.op("act", lambda e, yb=yb, pb=pb, ch=ch: e.activation(
                            out=yb, in_=ps[pb][:, :], func=AF.Identity, scale=cols[:, C_CW + 176 + ch:C_CW + 176 + ch + 1],
                            bias=cols[:, C_CB + ch:C_CB + ch + 1]), reads=[t_ps[pb], t_cols], writes=[t_yb[gvsel]])
                        for tap, off in ((1, 1), (0, 0)):
                            P.op("dve", lambda e, hb=hb, yb=yb, ch=ch, tap=tap, off=off: e.scalar_tensor_tensor(
                                out=yb, in0=hb[:, off:off + 512], scalar=cols[:, C_CW + 88 * tap + ch:C_CW + 88 * tap + ch + 1],
                                in1=yb, op0=ALU.mult, op1=ALU.add), reads=[t_hb[gvsel], t_cols], writes=[t_yb[gvsel]])
                        P.op("dve", lambda e, hb=hb: e.tensor_copy(out=hb[:, 0:2], in_=hb[:, 512:514]),
                             reads=[t_yb[gvsel]], writes=[t_hb[gvsel]])
                    k = tt % 2
                    P.op("act", lambda e: e.activation(out=sgb, in_=ybuf[0], func=AF.Silu), reads=[t_yb[0]], writes=[t_sg])
                    P.op("dve", lambda e, k=k: e.tensor_tensor(out=gvs[k], in0=sgb, in1=ybuf[1], op=ALU.mult),
                         reads=[t_sg, t_yb[1]], writes=[t_gvs[k]])
                    P.dma("sp", lambda e, k=k, fc=fc, tsl=tsl: e.dma_start(out=gv_d[fc, :, tsl], in_=gvs[k]),
                          reads=[t_gvs[k]], writes=[d_gv[fc][tt]], sem_tok=t_gvst[k])

    def ffn_down(plan):
        NG = FC // 4
        gi = 0
        for th in range(2):
            for oq in range(4):
                for g in range(NG):
                    if plan:
                        wp.declare(("dn", th, oq, g), w_down_d[g * 512:(g + 1) * 512, oq * 512:(oq + 1) * 512]
                                   .rearrange("(k p) o -> p k o", p=128), 4, 512)
                        continue
                    if th == 0 and oq == 0 and g == 0:
                        fence(scr_toks + big0_toks + mg_toks)
                    wv, wt = wp.next(("dn", th, oq, g))
                    sl_ = gi % 3
                    gi += 1
                    P.dma("sp", lambda e, sl_=sl_, g=g, th=th: e.dma_start(
                        out=gslot[sl_], in_=gv_d[4 * g:4 * g + 4, :, th * 1024:(th + 1) * 1024].rearrange("f p t -> p f t")),
                        reads=[d_gv[4 * g + f][2 * th + t2] for f in range(4) for t2 in range(2)], writes=[t_gslot[sl_]])

                    def mm(e, wv=wv, sl_=sl_, g=g):
                        for f in range(4):
                            for oc in range(4):
                                for t2 in range(2):
                                    ins = e.matmul(ps[oc * 2 + t2][:, :], lhsT=wv[:, f, oc * 128:(oc + 1) * 128],
                                                   rhs=gslot[sl_][:, f, t2 * 512:(t2 + 1) * 512],
                                                   start=(g == 0 and f == 0), stop=(g == NG - 1 and f == 3))
                        return ins
                    P.op("pe", mm, reads=[wt, t_gslot[sl_]], writes=t_ps)
                if plan:
                    continue
                for oc in range(4):
                    for t2 in range(2):
                        o = oq * 4 + oc
                        if t2 == 0:
                            P.op("act", lambda e, o=o, oc=oc, t2=t2: e.activation(out=yT[:, o, t2 * 512:(t2 + 1) * 512],
                                                                                   in_=ps[oc * 2 + t2][:, :], func=AF.Copy),
                                 reads=[t_ps[oc * 2 + t2]], writes=[t_yT[o][t2]])
                        else:
                            P.op("dve", lambda e, o=o, oc=oc, t2=t2: e.tensor_copy(out=yT[:, o, t2 * 512:(t2 + 1) * 512],
                                                                                    in_=ps[oc * 2 + t2][:, :]),
                                 reads=[t_ps[oc * 2 + t2]], writes=[t_yT[o][t2]])
            epilogue(plan, th * 1024, 1, out_d, d_x1)

    def all_phases(plan):
        consts(plan)
        for blk in range(NB if stop_after == "all" else 1):
            if not plan:
                fence(big0_toks + mg_toks)
            if blk == 0:
                prologue(plan, x_d, 0, NT // 128, 0, hT, t_hT, scaled=False)
                phase_mod(plan, 0)
                scale_shift_inplace(plan, 0, hT, t_hT, NT // 128)
            else:
                prologue(plan, x_d, blk * NT, NT // 128, 0, hT, t_hT)
            if not plan:
                dump("hT", hT, all_hT)
            if stop_after == "p1":
                return
            gmlp_setup(plan)
            vphase(plan, blk)
            mixing(plan)
            uphase(plan, blk)
            if not plan:
                dump("aT", big0[:, 16384:32768], t_A)
            if stop_after == "gmlp":
                return
            if blk == 0:
                phase_mod(plan, 1)
            branch_a(plan, blk)
            if not plan:
                dump("mgA", mergedT[:, :, :], [t for r in t_mg for t in r])
            if stop_after == "brA":
                return
            latents(plan, blk)
            if not plan:
                dump("kpe", kpe_dup[:, :], t_kpe)
                dump("kvg", kvgT[:, :, :], [t for r in t_kvg for t in r])
                dump("qg", qgT[:, :, :], [t for r in t_qg for t in r])
            if not plan:
                dump("cos", cos2[:, :], [t_cos])
                dump("sin", sin_s[:, :], [t_sin])
            if stop_after in ("lat", "lat0", "lat1", "lat2"):
                return
            attention(plan, blk)
            if blk == 0:
                phase_mod(plan, 2, stream=False)
            if not plan:
                dump("oT", big0[:, 16384:32768], all_oT)
            if stop_after == "att":
                return
            branch_b(plan, blk)
            if not plan:
                dump("mg", mergedT[:, :, :], [t for r in t_mg for t in r])
            if stop_after == "brB":
                return
            wout_phase(plan, blk)
            epilogue(plan, blk * NT, 0, x_d, None)
            if stop_after == "x1":
                break
        if stop_after == "x1":
            return
        if not plan:
            fence(big0_toks + mg_toks)
        prologue(plan, out_d, 0, S // 128, 1, h2T, t_h2T, src_toks=d_x1)
        ffn_up(plan)
        ffn_down(plan)

    all_phases(True)
    all_phases(False)

    done_rows = {"all": S, "x1": NT}.get(stop_after, 0)
    fence(scr_toks)
    t_o = P.dtok("o")
    for i in range(done_rows // 128, S // 128):
        b = i % 2
        P.dma("sp", lambda e, b=b, i=i: e.dma_start(out=xs[b], in_=x_d[i * 128:(i + 1) * 128, :]), writes=[t_xs[b]])
        P.dma("sp", lambda e, b=b, i=i: e.dma_start(out=out_d[i * 128:(i + 1) * 128, :], in_=xs[b]),
              reads=[t_xs[b]], writes=[d_x1[i]], sem_tok=t_o)
    d_out.extend(d_x1)
    P.wait_all("sp", d_out)
    P.emit()
    return nc


def _col(v):
    v = np.asarray(v, dtype=np.float32).reshape(-1, 128)
    return np.ascontiguousarray(v.T)


def make_in_maps(inp, cores):
    f32 = lambda k: np.ascontiguousarray(inp[k], dtype=np.float32)
    w_in = f32("w_in")
    kpe = w_in[:, O_KPE:O_KPE + 64]
    swp = np.concatenate([kpe[:, 32:64], kpe[:, 0:32]], axis=1)
    w_lat = np.ascontiguousarray(np.concatenate([w_in[:, O_QL:O_QL + 768], kpe, kpe, swp, swp], axis=1))
    w_uq, w_ukv = f32("w_uq"), f32("w_ukv")
    w_pair = np.zeros((8, 128, 3072), np.float32)
    chunked = lambda a: a.reshape(-1, 128, a.shape[1]).transpose(1, 0, 2)
    for pr in range(8):
        parts = []
        hs = (2 * pr, 2 * pr + 1)
        parts.append(np.concatenate([w_uq[:, h * 192:h * 192 + 128] for h in hs], axis=1))
        parts.append(np.concatenate([w_uq[:, h * 192 + 128:h * 192 + 192] for h in hs], axis=1))
        parts.append(np.concatenate([np.concatenate([w_uq[:, h * 192 + 160:h * 192 + 192],
                                                     w_uq[:, h * 192 + 128:h * 192 + 160]], axis=1) for h in hs], axis=1))
        parts.append(np.concatenate([w_ukv[:, h * 256:h * 256 + 128] for h in hs], axis=1))
        parts.append(np.concatenate([w_ukv[:, h * 256 + 128:h * 256 + 256] for h in hs], axis=1))
        w_pair[pr] = np.concatenate([chunked(a).reshape(128, -1) for a in parts], axis=1)
    def fuse(wb, g0):
        return np.ascontiguousarray(np.concatenate(
            [np.concatenate([wb[:, j * 256:(j + 1) * 256], w_in[:, g0 + j * 256:g0 + (j + 1) * 256]], axis=1) for j in range(8)], axis=1))
    shared = {"w_ada": f32("w_ada"), "w_in": w_in, "w_ag": fuse(f32("w_branch_a"), O_GA), "w_bg": fuse(f32("w_branch_b"), O_GB),
              "w_lat": w_lat, "w_pair": w_pair, "w_out": f32("w_out"), "w_down": f32("w_down"),
              "w_up2": np.ascontiguousarray(np.asarray(inp["w_up"], np.float32).reshape(D, 2, FC, 128).transpose(0, 2, 1, 3)
                                            .reshape(D, 2 * D_FF)),
              "wsT": np.ascontiguousarray(np.transpose(np.asarray(inp["gm_w_s"], np.float32), (2, 0, 1))),
              "bs": np.ascontiguousarray(np.asarray(inp["gm_b_s"], np.float32).reshape(1, 2048))}
    maps = []
    for b in cores:
        cols = np.zeros((128, NCOL), np.float32)
        cols[:, C_BADA:C_BADA + 96] = _col(inp["b_ada"])
        cols[:, C_G1:C_G1 + 16] = _col(inp["pre_norm1_g"])
        cols[:, C_G2:C_G2 + 16] = _col(inp["pre_norm2_g"])
        cols[:, C_GP1:C_GP1 + 16] = _col(inp["post_norm1_g"])
        cols[:, C_GP2:C_GP2 + 16] = _col(inp["post_norm2_g"])
        cols[:, C_LNG:C_LNG + 16] = _col(inp["gm_ln_g"])
        cols[:, C_LNB:C_LNB + 16] = _col(inp["gm_ln_b"])
        cols[:, C_QG:C_QG + 4] = _col(inp["q_norm_g"])
        cols[:, C_KVG:C_KVG + 2] = _col(inp["kv_norm_g"])
        cols[:, C_C:C_C + 16] = _col(inp["c"][b])
        for k in range(3):
            cols[:, C_CW + 88 * k:C_CW + 88 * (k + 1)] = _col(inp["conv_w"][k])
        cols[:, C_CB:C_CB + 88] = _col(inp["conv_b"])
        pidx = np.arange(128)
        cols[:, C_INV] = (10000.0 ** (-(2.0 * (pidx % 32)) / 64.0)).astype(np.float32)
        sgn = np.where((pidx % 64) < 32, -1.0, 1.0).astype(np.float32)
        cols[:, C_SGN] = sgn
        cols[:, C_NPI] = np.float32(-np.pi)
        cols[:, C_NPS] = (np.float32(-np.pi) * sgn).astype(np.float32)
        m = dict(shared)
        m["x"] = np.ascontiguousarray(inp["x"][b], dtype=np.float32)
        m["pos"] = np.ascontiguousarray(inp["positions"][b], dtype=np.int32).reshape(1, S)
        m["cols"] = cols
        maps.append(m)
    return maps


def kernel(**inputs):
    nc = build()
    maps = make_in_maps(inputs, list(range(8)))
    res = run_bass_kernel_spmd(nc, maps, core_ids=list(range(8)))
    return np.stack([np.asarray(r["out"], dtype=np.float32) for r in res.results], axis=0)
```

```python
import numpy as np
import ml_dtypes
import concourse.bass as bass
import concourse.mybir as mybir
from concourse.bass_utils import run_bass_kernel_spmd

F32 = mybir.dt.float32
BF16 = mybir.dt.bfloat16
I32 = mybir.dt.int32
AF = mybir.ActivationFunctionType
ALU = mybir.AluOpType
AX = mybir.AxisListType

D = 2048
S = 2048
KC = D // 128
NB = 2
NT = S // NB
EPS = 1e-6
D_FF = 5632
FC = D_FF // 128
N_HEADS = 16

C_BADA, C_G1, C_G2, C_GP1, C_GP2, C_LNG, C_LNB, C_QG, C_KVG, C_C, C_CW, C_CB = (
    0, 96, 112, 128, 144, 160, 176, 192, 196, 198, 214, 478)
C_INV, C_SGN, C_NPI, C_NPS = 566, 567, 568, 569
NCOL = 570


class Ev:
    __slots__ = ("sem", "val")

    def __init__(self, sem, val):
        self.sem = sem
        self.val = val


class Tok:
    __slots__ = ("w", "r", "dsem", "dcnt", "name", "excl")

    def __init__(self, name=""):
        self.excl = False
        self.w = None
        self.r = {}
        self.dsem = None
        self.dcnt = 0
        self.name = name


class Prog:
    ENG = ("pe", "act", "dve", "pool", "sp")

    def __init__(self, nc):
        self.nc = nc
        self.ops = {e: [] for e in self.ENG}
        self.sems = {e: nc.alloc_semaphore("s_" + e) for e in self.ENG}
        self.cnt = {e: 0 for e in self.ENG}
        self.waited = {e: {} for e in self.ENG}
        self.semobj = {self.sems[e].num: self.sems[e] for e in self.ENG}
        self.ntok = 0

    def tok(self, name=""):
        return Tok(name)

    def toks(self, n):
        return [Tok() for _ in range(n)]

    def dtok(self, name=""):
        t = Tok(name)
        self.ntok += 1
        t.dsem = self.nc.alloc_semaphore("d%d_%s" % (self.ntok, name))
        self.semobj[t.dsem.num] = t.dsem
        return t

    def _deps(self, eng, reads, writes, excl_own=None):
        need = {}
        for t in reads:
            if t.w is not None:
                need[t.w.sem] = max(need.get(t.w.sem, 0), t.w.val)
            if t.excl:
                for s, v in t.r.items():
                    if s != excl_own:
                        need[s] = max(need.get(s, 0), v)
        for t in writes:
            if t.w is not None:
                need[t.w.sem] = max(need.get(t.w.sem, 0), t.w.val)
            for s, v in t.r.items():
                need[s] = max(need.get(s, 0), v)
        w = self.waited[eng]
        waits = []
        if eng == "pe":
            need.pop(self.sems["pe"].num, None)
        for s, v in need.items():
            if w.get(s, 0) < v:
                waits.append((s, v))
                w[s] = v
        return waits

    def _mark(self, ev, reads, writes):
        for t in reads:
            t.r[ev.sem] = max(t.r.get(ev.sem, 0), ev.val)
        for t in writes:
            t.w = ev
            t.r = {}

    def op(self, eng, fn, reads=(), writes=()):
        waits = self._deps(eng, reads, writes, excl_own=self.sems[eng].num)
        self.cnt[eng] += 1
        ev = Ev(self.sems[eng].num, self.cnt[eng])
        self.ops[eng].append((waits, fn, (self.sems[eng].num, 1)))
        self._mark(ev, reads, writes)
        return ev

    def dma(self, eng, fn, reads=(), writes=(), sem_tok=None):
        st = sem_tok if sem_tok is not None else writes[0]
        assert st.dsem is not None
        waits = self._deps(eng, reads, writes)
        st.dcnt += 16
        ev = Ev(st.dsem.num, st.dcnt)
        self.ops[eng].append((waits, fn, (st.dsem.num, 16)))
        self._mark(ev, reads, writes)
        return ev

    def wait_all(self, eng, toks):
        waits = self._deps(eng, [], toks)
        self.ops[eng].append((waits, None, None))

    def emit(self):
        nc, ops, semobj = self.nc, self.ops, self.semobj

        def run(e, lst):
            for waits, fn, inc in lst:
                for s, v in waits:
                    e.wait_ge(semobj[s], v)
                if fn is None:
                    continue
                ins = fn(e)
                if inc is not None:
                    ins.then_inc(semobj[inc[0]], inc[1])

        with nc.Block() as block:
            @block.tensor
            def _(e):
                run(e, ops["pe"])

            @block.scalar
            def _(e):
                run(e, ops["act"])

            @block.vector
            def _(e):
                run(e, ops["dve"])

            @block.gpsimd
            def _(e):
                run(e, ops["pool"])

            @block.sync
            def _(e):
                run(e, ops["sp"])


class WPool:
    def __init__(self, P, nc, nslots, elems):
        self.P = P
        self.slots = [nc.alloc_sbuf_tensor("wslot%d" % i, [128, elems], BF16) for i in range(nslots)]
        self.toks = [P.dtok("w%d" % i) for i in range(nslots)]
        self.plan = []
        self.issued = 0
        self.cur = 0
        self.live = 1

    def declare(self, tag, src, kc, ncols):
        self.plan.append((tag, src, kc, ncols))

    def _issue(self, i):
        tag, src, kc, ncols = self.plan[i]
        s = i % len(self.slots)
        dst = self.slots[s][:, 0:kc * ncols].rearrange("p (k o) -> p k o", k=kc)
        self.P.dma("pool", lambda e, dst=dst, src=src: e.dma_start(out=dst, in_=src), writes=[self.toks[s]])

    def next(self, tag):
        i = self.cur
        assert self.plan[i][0] == tag, (self.plan[i][0], tag)
        while self.issued < min(len(self.plan), i + 1 + len(self.slots) - self.live):
            self._issue(self.issued)
            self.issued += 1
        self.cur += 1
        _, _, kc, ncols = self.plan[i]
        s = i % len(self.slots)
        return self.slots[s][:, 0:kc * ncols].rearrange("p (k o) -> p k o", k=kc), self.toks[s]


O_U, O_V, O_QL, O_KVL, O_KPE, O_GA, O_GB = 0, 2048, 4096, 4608, 4864, 4928, 6976


def build(stop_after="all", dbg=()):
    nc = bass.Bass("TRN2", target_bir_lowering=False)
    P = Prog(nc)
    dt = lambda name, shape, ty, kind: nc.dram_tensor(name, shape, ty, kind=kind).ap()
    x_d = dt("x", [S, D], F32, "ExternalInput")
    pos_d = dt("pos", [1, S], I32, "ExternalInput")
    cols_d = dt("cols", [128, NCOL], F32, "ExternalInput")
    w_ada_d = dt("w_ada", [D, 6 * D], F32, "ExternalInput")
    w_in_d = dt("w_in", [D, 9024], F32, "ExternalInput")
    wsT_d = dt("wsT", [128, 16, 128], F32, "ExternalInput")
    bs_d = dt("bs", [1, 2048], F32, "ExternalInput")
    w_ag_d = dt("w_ag", [D, 2 * D], F32, "ExternalInput")
    w_bg_d = dt("w_bg", [D, 2 * D], F32, "ExternalInput")
    w_lat_d = dt("w_lat", [D, 1024], F32, "ExternalInput")
    w_pair_d = dt("w_pair", [8, 128, 3072], F32, "ExternalInput")
    w_out_d = dt("w_out", [D, D], F32, "ExternalInput")
    w_up2_d = dt("w_up2", [D, 2 * D_FF], F32, "ExternalInput")
    w_down_d = dt("w_down", [D_FF, D], F32, "ExternalInput")
    out_d = dt("out", [S, D], F32, "ExternalOutput")
    gv_d = nc.dram_tensor("gv_scr", [FC, 128, S], BF16).ap()
    dbg_d = {}
    for name, shape, ty in dbg:
        dbg_d[name] = dt("dbg_" + name, shape, ty, "ExternalOutput")

    sb = lambda name, shape, ty: nc.alloc_sbuf_tensor("sb_" + name, shape, ty)
    cols = sb("cols", [128, NCOL], F32)
    modc = sb("modc", [128, 96], F32)
    prm = sb("prm", [128, 6, KC], F32)
    scb = sb("scb", [128, KC], BF16)
    ident = sb("ident", [128, 128], BF16)
    identf = sb("identf", [128, 128], F32)
    ones = sb("ones", [128, 128], BF16)
    stat = sb("stat", [128, 64], F32)
    epsc = sb("epsc", [128, 1], F32)
    fdummy = sb("fdummy", [128, 2], F32)
    big0 = sb("big0", [128, 32768], BF16)
    hT = big0[:, 0:16384].rearrange("p (k t) -> p k t", k=KC)
    A3 = big0[:, 16384:32768].rearrange("p (n c) -> p n c", n=8)
    A4 = big0[:, 16384:32768].rearrange("p (n g q) -> p n g q", n=8, g=16)
    mergedT = sb("mergedT", [128, KC, NT], BF16)
    scr = sb("scr", [128, 12288], BF16)
    xs = [scr[:, 0:4096].bitcast(F32), scr[:, 4096:8192].bitcast(F32)]
    xn = [scr[:, 8192:10240], scr[:, 10240:12288]]
    WmT = scr[:, 0:2048].rearrange("p (g q) -> p g q", g=16)
    bsb = scr[:, 2048:6144].bitcast(F32).rearrange("p (g q) -> p g q", g=16)
    Cg = scr[:, 6144:10240].bitcast(F32).rearrange("p (g q) -> p g q", g=16)
    tmpb = [scr[:, 10240:10752], scr[:, 11264:11776]]
    tmpf = [scr[:, 10240:11264].bitcast(F32), scr[:, 11264:12288].bitcast(F32)]
    junkv = scr[:, 10240:12288]
    oT = big0[:, 16384:32768].rearrange("p (h t) -> p h t", h=16)
    qgT = sb("qgT", [128, 4, NT], BF16)
    kvgT = sb("kvgT", [128, 2, S], BF16)
    kpe_dup = sb("kpe_dup", [128, S], BF16)
    cos2 = sb("cos2", [128, NT], BF16)
    sin_s = sb("sin_s", [128, NT], BF16)
    maskneg = sb("maskneg", [128, 128], BF16)
    KnT_b = sb("KnT_b", [128, S], BF16)
    Vh_b = sb("Vh_b", [128, 16, 128], BF16)
    masktmp = sb("masktmp", [128, 128], F32)
    sqb = [scr[:, 0:512], scr[:, 512:1024]]
    posi = scr[:, 2048:4096].bitcast(I32)
    angf = scr[:, 4096:6144].bitcast(F32)
    targ = scr[:, 6144:8192].bitcast(F32)
    QnT = [scr[:, 0:1024], scr[:, 1024:2048]]
    Qr = [scr[:, 2048:3072], scr[:, 3072:4096]]
    KnT = scr[:, 4096:6144]
    Vh = scr[:, 6144:8192].rearrange("p (k d) -> p k d", k=16)
    ptile = [scr[:, 8192 + 512 * k:8192 + 512 * (k + 1)] for k in range(4)]
    rec = scr[:, 10240:11264].bitcast(F32)
    rt1 = scr[:, 11264:12288].bitcast(F32)
    yT = big0[:, :].bitcast(F32).rearrange("p (k t) -> p k t", k=KC)
    h2T = big0[:, :].rearrange("p (k t) -> p k t", k=KC)
    sqe = [scr[:, 8192:8704], scr[:, 8704:9216]]
    sqw = [scr[:, 8192:9216], scr[:, 9216:10240]]
    rstd_e = scr[:, 10240:12288].bitcast(F32)
    hbuf = [scr[:, 0:1028].bitcast(F32), scr[:, 1056:2084].bitcast(F32)]
    ybuf = [scr[:, 2112:3136].bitcast(F32), scr[:, 3136:4160].bitcast(F32)]
    sgb = scr[:, 4160:4672]
    gvs = [scr[:, 4672:5184], scr[:, 5184:5696]]
    gslot = [mergedT[:, :, :].rearrange("p k t -> p (k t)")[:, 4096 * i:4096 * (i + 1)].rearrange("p (f t) -> p f t", f=4)
             for i in range(3)]
    ps = [nc.alloc_psum_tensor("ps%d" % i, [128, 512], F32) for i in range(8)]
    t_ps = P.toks(8)
    for t in t_ps:
        t.excl = True
    t_pmod = t_ps[0]

    t_cols = P.dtok("cols")
    t_modc, t_prm, t_scb, t_ident, t_epsc, t_ones, t_fd = P.toks(7)
    t_stat = P.toks(64)
    t_xs = [P.dtok("xs0"), P.dtok("xs1")]
    t_xn = P.toks(2)
    t_hT = [[P.tok() for _ in range(NT // 128)] for _ in range(KC)]
    t_A = P.toks(8)
    t_mg = [[P.tok() for _ in range(2)] for _ in range(KC)]
    t_WmT = P.dtok("wmT")
    t_bsb = P.dtok("bsb")
    t_Cg, = P.toks(1)
    t_tmp = P.toks(2)
    t_dbg = P.dtok("dbg")
    d_out = []
    all_hT = [t for row in t_hT for t in row]
    t_qg = [[P.tok() for _ in range(2)] for _ in range(4)]
    t_kvg = [[P.tok() for _ in range(4)] for _ in range(2)]
    t_rq = P.toks(2)
    t_rkv = P.toks(4)
    t_rkvc, t_mask, t_cos, t_sin, t_posi, t_angf, t_targ = P.toks(7)
    t_posi = P.dtok("posi")
    t_kpe = P.toks(4)
    t_sqb = P.toks(2)
    t_QnT, t_Qr, t_pt = P.toks(2), P.toks(2), P.toks(4)
    t_KnT = P.toks(4)
    t_Vh, t_rec, t_rt1 = P.toks(3)
    t_oT = [[P.tok() for _ in range(2)] for _ in range(16)]
    t_KnT_b = P.toks(4)
    t_Vh_b = P.tok()
    KV = ((KnT, Vh, t_KnT, t_Vh), (KnT_b, Vh_b, t_KnT_b, t_Vh_b))
    t_yT = [[P.tok() for _ in range(2)] for _ in range(KC)]
    t_h2T = [[P.tok() for _ in range(S // 128)] for _ in range(KC)]
    t_sqe = P.toks(2)
    t_rse = P.toks(2)
    t_hb, t_yb = P.toks(2), P.toks(2)
    t_sg, = P.toks(1)
    t_gvs = P.toks(2)
    t_gslot = [P.dtok("gs%d" % i) for i in range(3)]
    t_gvst = [P.dtok("gvst%d" % i) for i in range(2)]
    d_x1 = P.toks(S // 128)
    d_gv = [[P.tok() for _ in range(4)] for _ in range(FC)]
    t_st = [P.dtok("st0"), P.dtok("st1")]
    all_oT = [t for r in t_oT for t in r]
    big0_toks = all_hT + t_A + all_oT + [t for r in t_yT for t in r] + [t for r in t_h2T for t in r]
    mg_toks = [t for r in t_mg for t in r] + t_gslot
    scr_toks = t_xs + t_xn + [t_WmT, t_bsb, t_Cg] + t_tmp + t_sqb + [t_posi, t_angf, t_targ] + t_QnT + t_Qr + t_pt \
        + t_KnT + [t_Vh, t_rec, t_rt1] + t_sqe + t_rse + t_hb + t_yb + [t_sg] + t_gvs

    wp = WPool(P, nc, 3, KC * 512)
    rot = {"n": 0, "lo": 4, "cnt": 4}

    def gbank():
        b = rot["lo"] + rot["n"] % rot["cnt"]
        rot["n"] += 1
        return b

    prot = {"n": 0}

    def pbank():
        b = 1 + prot["n"] % 7
        prot["n"] += 1
        return b

    def fence(toks):
        P.op("pool", lambda e: e.memset(fdummy[:, 0:1], 0.0), writes=list(toks) + [t_fd])

    def dump(name, src_ap, toks):
        if name in dbg_d:
            d = P.tok()
            P.dma("sp", lambda e: e.dma_start(out=dbg_d[name], in_=src_ap), reads=list(toks), writes=[d], sem_tok=t_dbg)
            d_out.append(d)

    def wsrc(w_d, c0, ncols):
        return w_d[:, c0:c0 + ncols].rearrange("(k p) o -> p k o", p=128)

    def consts(plan):
        if plan:
            return
        P.op("pool", lambda e: e.memset(epsc[:, :], EPS), writes=[t_epsc])
        P.op("pool", lambda e: e.memset(ones[:, :], 1.0), writes=[t_ones])
        P.dma("sp", lambda e: e.dma_start(out=cols[:, :], in_=cols_d), writes=[t_cols])
        P.op("pool", lambda e: e.memset(identf[:, :], 1.0), writes=[t_ident])
        P.op("pool", lambda e: e.affine_select(out=identf[:, :], in_=identf[:, :], pattern=[[-1, 128]],
                                                compare_op=ALU.is_equal, fill=0.0, base=0, channel_multiplier=1),
             reads=[t_ident], writes=[t_ident])
        P.op("pool", lambda e: e.tensor_copy(out=ident[:, :], in_=identf[:, :]), reads=[t_ident], writes=[t_ident])
        P.op("pool", lambda e: e.memset(masktmp[:, :], 0.0), writes=[t_mask])
        P.op("pool", lambda e: e.affine_select(out=masktmp[:, :], in_=masktmp[:, :], pattern=[[1, 128]],
                                                compare_op=ALU.is_ge, fill=-30000.0, base=0, channel_multiplier=-1),
             reads=[t_mask], writes=[t_mask])
        P.op("pool", lambda e: e.tensor_copy(out=maskneg[:, :], in_=masktmp[:, :]), reads=[t_mask], writes=[t_mask])

    MOD_PARTS = ((0, 8), (8, 12), (12, 24))

    def mod_blocks(plan, j0, j1):
        pmod = ps[0]
        for j in range(j0, j1):
            if plan:
                wp.declare(("ada", j), wsrc(w_ada_d, j * 512, 512), KC, 512)
                continue
            wv, wt = wp.next(("ada", j))

            def mm(e, wv=wv, j=j):
                for oc in range(4):
                    col = j * 4 + oc
                    for kc in range(KC):
                        ins = e.matmul(pmod[:, col:col + 1], lhsT=wv[:, kc, oc * 128:(oc + 1) * 128],
                                       rhs=scb[:, kc:kc + 1], start=(kc == 0), stop=(kc == KC - 1))
                return ins
            P.op("pe", mm, reads=[wt, t_scb], writes=[t_pmod])

    def phase_mod(plan, part, stream=True):
        j0, j1 = MOD_PARTS[part]
        pmod = ps[0]
        if not plan and part == 0:
            P.op("act", lambda e: e.activation(out=scb[:, :], in_=cols[:, C_C:C_C + KC], func=AF.Silu),
                 reads=[t_cols], writes=[t_scb])
        if stream:
            mod_blocks(plan, j0, j1)
        if plan:
            return
        c0, c1 = j0 * 4, j1 * 4
        P.op("dve", lambda e: e.tensor_tensor(out=modc[:, c0:c1], in0=pmod[:, c0:c1], in1=cols[:, C_BADA + c0:C_BADA + c1],
                                              op=ALU.add), reads=[t_pmod, t_cols], writes=[t_modc])
        def gs_sh(sl, c_sh, c_sc, c_g):
            P.op("dve", lambda e: e.scalar_tensor_tensor(out=prm[:, 3 * sl + 0, :], in0=modc[:, c_sc:c_sc + KC], scalar=1.0,
                                                         in1=cols[:, c_g:c_g + KC], op0=ALU.add, op1=ALU.mult),
                 reads=[t_modc, t_cols], writes=[t_prm])
            P.op("dve", lambda e: e.tensor_copy(out=prm[:, 3 * sl + 1, :], in_=modc[:, c_sh:c_sh + KC]),
                 reads=[t_modc], writes=[t_prm])

        def gg(sl, c_ga, c_gp):
            P.op("dve", lambda e: e.tensor_tensor(out=prm[:, 3 * sl + 2, :], in0=modc[:, c_ga:c_ga + KC],
                                                  in1=cols[:, c_gp:c_gp + KC], op=ALU.mult),
                 reads=[t_modc, t_cols], writes=[t_prm])
        if part == 0:
            gs_sh(0, 0, 16, C_G1)
        elif part == 1:
            gg(0, 32, C_GP1)
        else:
            gs_sh(1, 48, 64, C_G2)
            gg(1, 80, C_GP2)
            dump("modc", modc[:, :], [t_modc])

    def rstd_from(sq, rs, t_sq, t_rs, n):
        P.op("act", lambda e: e.activation(out=rs, in_=sq, func=AF.Sqrt, scale=1.0 / n, bias=epsc[:, 0:1]),
             reads=[t_sq, t_epsc], writes=[t_rs])
        P.op("dve", lambda e: e.reciprocal(out=rs, in_=rs), reads=[t_rs], writes=[t_rs])

    def prologue(plan, src_d, T0, ntiles, sl, dst, t_dst, src_toks=None):
        if plan:
            return
        fence(scr_toks)

        def stage_a(i):
            b = i % 2
            r0 = T0 + i * 128
            P.dma("sp", lambda e, b=b, r0=r0: e.dma_start(out=xs[b], in_=src_d[r0:r0 + 128, :]),
                  reads=([src_toks[r0 // 128]] if src_toks else []), writes=[t_xs[b]])
            sq, rs = stat[:, 2 * b:2 * b + 1], stat[:, 2 * b + 1:2 * b + 2]
            P.op("act", lambda e, b=b, sq=sq: e.activation(out=xn[b], in_=xs[b], func=AF.Square, accum_out=sq),
                 reads=[t_xs[b]], writes=[t_xn[b], t_stat[2 * b]])
            rstd_from(sq, rs, t_stat[2 * b], t_stat[2 * b + 1], D)
            P.op("dve", lambda e, b=b, rs=rs: e.tensor_scalar(out=xn[b], in0=xs[b], scalar1=rs, scalar2=None, op0=ALU.mult),
                 reads=[t_xs[b], t_stat[2 * b + 1]], writes=[t_xn[b]])

        def stage_b(i):
            b = i % 2
            for half in range(2):
                pb = 2 * b + half
                pv = ps[pb][:, :].bitcast(BF16)

                def tr(e, b=b, half=half, pv=pv):
                    for k in range(8):
                        kc = half * 8 + k
                        ins = e.transpose(out=pv[:, k * 128:(k + 1) * 128], in_=xn[b][:, kc * 128:(kc + 1) * 128],
                                          identity=ident[:, :])
                    return ins
                P.op("pe", tr, reads=[t_xn[b], t_ident], writes=[t_ps[pb]])
                for k in range(8):
                    kc = half * 8 + k
                    if half == 0:
                        P.op("act", lambda e, kc=kc, k=k, pv=pv, i=i: e.activation(
                            out=dst[:, kc, i * 128:(i + 1) * 128], in_=pv[:, k * 128:(k + 1) * 128], func=AF.Identity,
                            scale=prm[:, 3 * sl + 0, kc:kc + 1], bias=prm[:, 3 * sl + 1, kc:kc + 1]),
                            reads=[t_ps[pb], t_prm], writes=[t_dst[kc][i]])
                    else:
                        P.op("dve", lambda e, kc=kc, k=k, pv=pv, i=i: e.tensor_scalar(
                            out=dst[:, kc, i * 128:(i + 1) * 128], in0=pv[:, k * 128:(k + 1) * 128],
                            scalar1=prm[:, 3 * sl + 0, kc:kc + 1], scalar2=prm[:, 3 * sl + 1, kc:kc + 1],
                            op0=ALU.mult, op1=ALU.add), reads=[t_ps[pb], t_prm], writes=[t_dst[kc][i]])

        stage_a(0)
        for i in range(ntiles):
            if i + 1 < ntiles:
                stage_a(i + 1)
            stage_b(i)

    def gmlp_setup(plan):
        if plan:
            return
        fence(scr_toks)
        P.dma("pool", lambda e: e.dma_start(out=WmT, in_=wsT_d), writes=[t_WmT])
        P.op("pool", lambda e: e.affine_select(out=WmT, in_=WmT, pattern=[[0, 16], [1, 128]], compare_op=ALU.is_ge,
                                                fill=0.0, base=0, channel_multiplier=-1), reads=[t_WmT], writes=[t_WmT])
        P.dma("sp", lambda e: e.dma_start(out=bsb.rearrange("p g q -> p (g q)"), in_=bs_d.partition_broadcast(128)),
              writes=[t_bsb])
        for q4 in range(4):
            def mm(e, q4=q4):
                for k in range(4):
                    g = q4 * 4 + k
                    ins = e.matmul(ps[q4][:, k * 128:(k + 1) * 128], lhsT=ones[:, :], rhs=WmT[:, g, :], start=True, stop=True)
                return ins
            P.op("pe", mm, reads=[t_ones, t_WmT], writes=[t_ps[q4]])
            for k in range(4):
                g = q4 * 4 + k
                P.op("dve", lambda e, q4=q4, k=k, g=g: e.scalar_tensor_tensor(
                    out=Cg[:, g, :], in0=ps[q4][:, k * 128:(k + 1) * 128], scalar=cols[:, C_LNB + g:C_LNB + g + 1],
                    in1=bsb[:, g, :], op0=ALU.mult, op1=ALU.add), reads=[t_ps[q4], t_cols, t_bsb], writes=[t_Cg])

    def vphase(plan, blk):
        for j in range(4):
            if plan:
                wp.declare(("v", blk, j), wsrc(w_in_d, O_V + j * 512, 512), KC, 512)
                continue
            wv, wt = wp.next(("v", blk, j))
            for n in range(8):
                pb = gbank()

                def mm(e, wv=wv, n=n, pb=pb):
                    for kc in range(KC):
                        ins = e.matmul(ps[pb][:, :], lhsT=hT[:, kc, n * 128:(n + 1) * 128], rhs=wv[:, kc, :],
                                       start=(kc == 0), stop=(kc == KC - 1))
                    return ins
                P.op("pe", mm, reads=[wt] + [t_hT[kc][n] for kc in range(KC)], writes=[t_ps[pb]])
                P.op("act", lambda e, n=n, j=j, pb=pb: e.activation(out=A3[:, n, j * 512:(j + 1) * 512], in_=ps[pb][:, :],
                                                                      func=AF.Gelu_apprx_tanh),
                     reads=[t_ps[pb]], writes=[t_A[n]])

    def mixing(plan):
        if plan:
            return

        def cols6(n):
            c = 32 + 6 * (n % 3)
            return [stat[:, c + k:c + k + 1] for k in range(6)], t_stat[c:c + 6]

        def a_act(n):
            (s1, s2, mu, var, rsd, nb), ts = cols6(n)
            P.op("act", lambda e: e.activation(out=junkv, in_=A3[:, n, :], func=AF.Identity, accum_out=s1),
                 reads=[t_A[n]], writes=[t_tmp[0], t_tmp[1], ts[0]])
            P.op("act", lambda e: e.activation(out=junkv, in_=A3[:, n, :], func=AF.Square, accum_out=s2),
                 reads=[t_A[n]], writes=[t_tmp[0], t_tmp[1], ts[1]])

        def a_dve1(n):
            (s1, s2, mu, var, rsd, nb), ts = cols6(n)
            P.op("dve", lambda e: e.tensor_scalar(out=mu, in0=s1, scalar1=1.0 / 2048, scalar2=None, op0=ALU.mult),
                 reads=[ts[0]], writes=[ts[2]])
            P.op("dve", lambda e: e.tensor_tensor(out=var, in0=mu, in1=mu, op=ALU.mult), reads=[ts[2]], writes=[ts[3]])
            P.op("dve", lambda e: e.scalar_tensor_tensor(out=var, in0=s2, scalar=1.0 / 2048, in1=var, op0=ALU.mult,
                                                         op1=ALU.subtract), reads=[ts[1], ts[3]], writes=[ts[3]])

        def a_sqrt(n):
            (s1, s2, mu, var, rsd, nb), ts = cols6(n)
            P.op("act", lambda e: e.activation(out=rsd, in_=var, func=AF.Sqrt, scale=1.0, bias=epsc[:, 0:1]),
                 reads=[ts[3], t_epsc], writes=[ts[4]])

        def a_dve2(n):
            (s1, s2, mu, var, rsd, nb), ts = cols6(n)
            P.op("dve", lambda e: e.reciprocal(out=rsd, in_=rsd), reads=[ts[4]], writes=[ts[4]])
            P.op("dve", lambda e: e.scalar_tensor_tensor(out=nb, in0=mu, scalar=-1.0, in1=rsd, op0=ALU.mult, op1=ALU.mult),
                 reads=[ts[2], ts[4]], writes=[ts[5]])

        def b_norm_mm(n):
            (s1, s2, mu, var, rsd, nb), ts = cols6(n)
            P.op("dve", lambda e: e.tensor_scalar(out=A3[:, n, :], in0=A3[:, n, :], scalar1=rsd, scalar2=nb,
                                                  op0=ALU.mult, op1=ALU.add), reads=[ts[4], ts[5]], writes=[t_A[n]])
            for q4 in range(4):
                pb = 4 * (n % 2) + q4

                def mm(e, q4=q4, pb=pb):
                    for k in range(4):
                        g = q4 * 4 + k
                        ins = e.matmul(ps[pb][:, k * 128:(k + 1) * 128], lhsT=A3[:, n, g * 128:(g + 1) * 128],
                                       rhs=WmT[:, g, :], start=True, stop=True)
                    return ins
                P.op("pe", mm, reads=[t_A[n], t_WmT], writes=[t_ps[pb]])

        def b_ev(n):
            for q4 in range(4):
                pb = 4 * (n % 2) + q4
                P.op("dve", lambda e, pb=pb, q4=q4: e.tensor_copy(
                    out=A4[:, n, 4 * q4:4 * q4 + 4, :].rearrange("p g q -> p (g q)"), in_=ps[pb][:, :]),
                    reads=[t_ps[pb]], writes=[t_A[n]])

        a_act(0); a_dve1(0); a_sqrt(0); a_dve2(0); b_norm_mm(0)
        a_act(1); a_dve1(1)
        for k in range(8):
            if k + 1 < 8:
                a_sqrt(k + 1)
                a_dve2(k + 1)
                b_norm_mm(k + 1)
            b_ev(k)
            if k + 2 < 8:
                a_act(k + 2)
                a_dve1(k + 2)

    def uphase(plan, blk):
        for j in range(4):
            if plan:
                wp.declare(("u", blk, j), wsrc(w_in_d, O_U + j * 512, 512), KC, 512)
                continue
            wv, wt = wp.next(("u", blk, j))
            for oc in range(4):
                g = j * 4 + oc
                for tt in range(2):
                    pb = gbank()
                    sl_ = (oc * 2 + tt) % 2

                    def mm(e, wv=wv, oc=oc, tt=tt, pb=pb):
                        for kc in range(KC):
                            ins = e.matmul(ps[pb][:, :], lhsT=wv[:, kc, oc * 128:(oc + 1) * 128],
                                           rhs=hT[:, kc, tt * 512:(tt + 1) * 512], start=(kc == 0), stop=(kc == KC - 1))
                        return ins
                    P.op("pe", mm, reads=[wt] + [t_hT[kc][n] for kc in range(KC) for n in range(4 * tt, 4 * tt + 4)],
                         writes=[t_ps[pb]])
                    P.op("act", lambda e, pb=pb, sl_=sl_: e.activation(out=tmpb[sl_], in_=ps[pb][:, :], func=AF.Gelu_apprx_tanh),
                         reads=[t_ps[pb]], writes=[t_tmp[sl_]])
                    av = A4[:, 4 * tt:4 * tt + 4, g, :]
                    P.op("dve", lambda e, av=av, g=g: e.scalar_tensor_tensor(
                        out=av, in0=av, scalar=cols[:, C_LNG + g:C_LNG + g + 1], in1=Cg[:, g:g + 1, :].broadcast_to([128, 4, 128]),
                        op0=ALU.mult, op1=ALU.add), reads=[t_cols, t_Cg], writes=t_A[4 * tt:4 * tt + 4])
                    P.op("dve", lambda e, av=av, sl_=sl_: e.tensor_tensor(
                        out=av, in0=tmpb[sl_].rearrange("p (n q) -> p n q", n=4), in1=av, op=ALU.mult),
                        reads=[t_tmp[sl_]], writes=t_A[4 * tt:4 * tt + 4])

    def branch_a(plan, blk):
        for j in range(8):
            if plan:
                wp.declare(("ag", blk, j), wsrc(w_ag_d, j * 512, 512), KC, 512)
                continue
            if j == 0:
                fence(scr_toks)
            wv, wt = wp.next(("ag", blk, j))
            for oc in range(2):
                o = j * 2 + oc
                for tt in range(2):
                    pg, py = gbank(), gbank()
                    sl_ = (oc * 2 + tt) % 2

                    def mmg(e, wv=wv, oc=oc, tt=tt, pg=pg):
                        for kc in range(KC):
                            ins = e.matmul(ps[pg][:, :], lhsT=wv[:, kc, 256 + oc * 128:256 + (oc + 1) * 128],
                                           rhs=hT[:, kc, tt * 512:(tt + 1) * 512], start=(kc == 0), stop=(kc == KC - 1))
                        return ins
                    P.op("pe", mmg, reads=[wt] + [t_hT[kc][n] for kc in range(KC) for n in range(4 * tt, 4 * tt + 4)],
                         writes=[t_ps[pg]])
                    P.op("act", lambda e, pg=pg, sl_=sl_: e.activation(out=tmpf[sl_], in_=ps[pg][:, :], func=AF.Sigmoid),
                         reads=[t_ps[pg]], writes=[t_tmp[sl_]])

                    def mmy(e, wv=wv, oc=oc, tt=tt, py=py):
                        for g in range(16):
                            ins = e.matmul(ps[py][:, :], lhsT=wv[:, g, oc * 128:(oc + 1) * 128],
                                           rhs=A4[:, 4 * tt:4 * tt + 4, g, :], start=(g == 0), stop=(g == 15))
                        return ins
                    P.op("pe", mmy, reads=[wt] + t_A[4 * tt:4 * tt + 4], writes=[t_ps[py]])
                    P.op("dve", lambda e, o=o, tt=tt, py=py, sl_=sl_: e.tensor_tensor(
                        out=mergedT[:, o, tt * 512:(tt + 1) * 512], in0=ps[py][:, :], in1=tmpf[sl_], op=ALU.mult),
                        reads=[t_ps[py], t_tmp[sl_]], writes=[t_mg[o][tt]])

    SCALE = float(192 ** -0.5)
    PI = float(np.pi)

    def latents(plan, blk):
        T0 = blk * NT
        wblk = []
        wp.live = 2
        for j in range(2):
            if plan:
                wp.declare(("lat", blk, j), wsrc(w_lat_d, j * 512, 512), KC, 512)
            else:
                wblk.append(wp.next(("lat", blk, j)))
        if plan:
            wp.live = 1
            return
        fence(scr_toks)
        P.dma("sp", lambda e: e.dma_start(out=posi, in_=pos_d[:, T0:T0 + NT].partition_broadcast(128)), writes=[t_posi])
        P.op("dve", lambda e: e.tensor_copy(out=angf, in_=posi), reads=[t_posi], writes=[t_angf])
        P.op("dve", lambda e: e.tensor_scalar(out=angf, in0=angf, scalar1=cols[:, C_INV:C_INV + 1], scalar2=None, op0=ALU.mult),
             reads=[t_angf, t_cols], writes=[t_angf])
        HI = 6.28125
        LO = float(2 * np.pi - 6.28125)
        P.op("dve", lambda e: e.tensor_scalar(out=targ, in0=angf, scalar1=float(1 / (2 * np.pi)), scalar2=None, op0=ALU.mult),
             reads=[t_angf], writes=[t_targ])
        P.op("dve", lambda e: e.tensor_copy(out=posi, in_=targ), reads=[t_targ], writes=[t_posi])
        P.op("dve", lambda e: e.tensor_copy(out=targ, in_=posi), reads=[t_posi], writes=[t_targ])
        P.op("dve", lambda e: e.scalar_tensor_tensor(out=angf, in0=targ, scalar=-HI, in1=angf, op0=ALU.mult, op1=ALU.add),
             reads=[t_targ, t_angf], writes=[t_angf])
        P.op("dve", lambda e: e.scalar_tensor_tensor(out=angf, in0=targ, scalar=-LO, in1=angf, op0=ALU.mult, op1=ALU.add),
             reads=[t_targ, t_angf], writes=[t_angf])
        P.op("dve", lambda e: e.tensor_scalar(out=angf, in0=angf, scalar1=-PI, scalar2=PI, op0=ALU.max, op1=ALU.min),
             reads=[t_angf], writes=[t_angf])
        P.op("act", lambda e: e.activation(out=sin_s[:, :], in_=angf, func=AF.Sin, scale=cols[:, C_SGN:C_SGN + 1]),
             reads=[t_angf, t_cols], writes=[t_sin])
        P.op("dve", lambda e: e.tensor_scalar(out=targ, in0=angf, scalar1=PI / 2, scalar2=2 * PI, op0=ALU.is_gt, op1=ALU.mult),
             reads=[t_angf], writes=[t_targ])
        P.op("dve", lambda e: e.scalar_tensor_tensor(out=targ, in0=angf, scalar=PI / 2, in1=targ, op0=ALU.add, op1=ALU.subtract),
             reads=[t_angf, t_targ], writes=[t_targ])
        P.op("dve", lambda e: e.tensor_scalar(out=targ, in0=targ, scalar1=-PI, scalar2=PI, op0=ALU.max, op1=ALU.min),
             reads=[t_targ], writes=[t_targ])
        P.op("act", lambda e: e.activation(out=cos2[:, :], in_=targ, func=AF.Sin), reads=[t_targ], writes=[t_cos])
        (wq, wqt), (wk, wkt) = wblk
        lvl = {"lat0": 0, "lat1": 1, "lat2": 2}.get(stop_after, 3)
        for tt in range(2 if lvl > 0 else 0):
            hts = [t_hT[kc][n] for kc in range(KC) for n in range(4 * tt, 4 * tt + 4)]
            tsl = slice(tt * 512, (tt + 1) * 512)
            asl = slice(T0 + tt * 512, T0 + (tt + 1) * 512)
            at = (T0 // 512) + tt
            pq = 2
            for c in range(4):
                pb = gbank()

                def mm(e, c=c, pb=pb, tsl=tsl):
                    for kc in range(KC):
                        ins = e.matmul(ps[pb][:, :], lhsT=wq[:, kc, c * 128:(c + 1) * 128], rhs=hT[:, kc, tsl],
                                       start=(kc == 0), stop=(kc == KC - 1))
                    return ins
                P.op("pe", mm, reads=[wqt] + hts, writes=[t_ps[pb]])
                P.op("act", lambda e, c=c, pb=pb: e.activation(out=sqb[c % 2], in_=ps[pb][:, :], func=AF.Square),
                     reads=[t_ps[pb]], writes=[t_sqb[c % 2]])
                P.op("act", lambda e, c=c, pb=pb, tsl=tsl: e.activation(out=qgT[:, c, tsl], in_=ps[pb][:, :], func=AF.Copy,
                                                                        scale=cols[:, C_QG + c:C_QG + c + 1]),
                     reads=[t_ps[pb], t_cols], writes=[t_qg[c][tt]])
                P.op("pe", lambda e, c=c: e.matmul(ps[pq][:, :], lhsT=ones[:, :], rhs=sqb[c % 2], start=(c == 0), stop=(c == 3)),
                     reads=[t_ones, t_sqb[c % 2]], writes=[t_ps[pq]])
            P.op("act", lambda e, tsl=tsl: e.activation(out=tmpf[0], in_=ps[pq][:, :], func=AF.Sqrt, scale=1.0 / 512,
                                                        bias=epsc[:, 0:1]), reads=[t_ps[pq], t_epsc], writes=[t_tmp[0]])
            P.op("dve", lambda e: e.reciprocal(out=tmpf[0], in_=tmpf[0]), reads=[t_tmp[0]], writes=[t_tmp[0]])
            for c in range(4):
                P.op("dve", lambda e, c=c, tsl=tsl: e.tensor_tensor(out=qgT[:, c, tsl], in0=qgT[:, c, tsl], in1=tmpf[0], op=ALU.mult),
                     reads=[t_tmp[0]], writes=[t_qg[c][tt]])
            if lvl < 2:
                continue
            pk = 3
            for c in range(2):
                pb = gbank()

                def mm(e, c=c, pb=pb, tsl=tsl):
                    for kc in range(KC):
                        ins = e.matmul(ps[pb][:, :], lhsT=wk[:, kc, c * 128:(c + 1) * 128], rhs=hT[:, kc, tsl],
                                       start=(kc == 0), stop=(kc == KC - 1))
                    return ins
                P.op("pe", mm, reads=[wkt] + hts, writes=[t_ps[pb]])
                P.op("act", lambda e, c=c, pb=pb: e.activation(out=sqb[c], in_=ps[pb][:, :], func=AF.Square),
                     reads=[t_ps[pb]], writes=[t_sqb[c]])
                P.op("act", lambda e, c=c, pb=pb, asl=asl: e.activation(out=kvgT[:, c, asl], in_=ps[pb][:, :], func=AF.Copy,
                                                                        scale=cols[:, C_KVG + c:C_KVG + c + 1]),
                     reads=[t_ps[pb], t_cols], writes=[t_kvg[c][at]])
                P.op("pe", lambda e, c=c: e.matmul(ps[pk][:, :], lhsT=ones[:, :], rhs=sqb[c], start=(c == 0), stop=(c == 1)),
                     reads=[t_ones, t_sqb[c]], writes=[t_ps[pk]])
            P.op("act", lambda e: e.activation(out=tmpf[1], in_=ps[pk][:, :], func=AF.Sqrt, scale=1.0 / 256, bias=epsc[:, 0:1]),
                 reads=[t_ps[pk], t_epsc], writes=[t_tmp[1]])
            P.op("dve", lambda e: e.reciprocal(out=tmpf[1], in_=tmpf[1]), reads=[t_tmp[1]], writes=[t_tmp[1]])
            for c in range(2):
                P.op("dve", lambda e, c=c, asl=asl: e.tensor_tensor(out=kvgT[:, c, asl], in0=kvgT[:, c, asl], in1=tmpf[1], op=ALU.mult),
                     reads=[t_tmp[1]], writes=[t_kvg[c][at]])
            if lvl < 3:
                continue
            pr_, psw = gbank(), gbank()
            for pbx, c0 in ((pr_, 256), (psw, 384)):
                def mm(e, pbx=pbx, c0=c0, tsl=tsl):
                    for kc in range(KC):
                        ins = e.matmul(ps[pbx][:, :], lhsT=wk[:, kc, c0:c0 + 128], rhs=hT[:, kc, tsl],
                                       start=(kc == 0), stop=(kc == KC - 1))
                    return ins
                P.op("pe", mm, reads=[wkt] + hts, writes=[t_ps[pbx]])
            P.op("dve", lambda e, tsl=tsl, pr_=pr_: e.tensor_tensor(out=tmpf[0], in0=ps[pr_][:, :], in1=cos2[:, tsl], op=ALU.mult),
                 reads=[t_ps[pr_], t_cos], writes=[t_tmp[0]])
            P.op("dve", lambda e, tsl=tsl, psw=psw: e.tensor_tensor(out=tmpf[1], in0=ps[psw][:, :], in1=sin_s[:, tsl], op=ALU.mult),
                 reads=[t_ps[psw], t_sin], writes=[t_tmp[1]])
            P.op("dve", lambda e, asl=asl: e.tensor_tensor(out=kpe_dup[:, asl], in0=tmpf[0], in1=tmpf[1], op=ALU.add),
                 reads=t_tmp, writes=[t_kpe[at]])

    def attention(plan, blk):
        wp.live = 1
        T0 = blk * NT
        nk512 = (T0 + NT) // 512
        ADA2 = (12, 14, 16, 18, 20, 21, 22, 23, 24)
        for pr in range(8):
            if plan:
                wp.declare(("pair", blk, pr), w_pair_d[pr].rearrange("p (k o) -> p k o", k=1), 1, 3072)
                if blk == 0:
                    mod_blocks(True, ADA2[pr], ADA2[pr + 1])
                continue
            rot["lo"], rot["cnt"] = 5, 3
            if pr == 0:
                fence(scr_toks + t_A)
                P.op("pool", lambda e: e.memset(Qr[0][64:128, :], 0.0), writes=[t_Qr[0]])
                P.op("pool", lambda e: e.memset(Qr[1][0:64, :], 0.0), writes=[t_Qr[1]])
            wv_, wt = wp.next(("pair", blk, pr))
            w = wv_[:, 0, :]
            qn = w[:, 0:1024].rearrange("p (k o) -> p k o", k=4)
            qr = w[:, 1024:1536].rearrange("p (k o) -> p k o", k=4)
            qs = w[:, 1536:2048].rearrange("p (k o) -> p k o", k=4)
            kn = w[:, 2048:2560].rearrange("p (k o) -> p k o", k=2)
            vv = w[:, 2560:3072].rearrange("p (k o) -> p k o", k=2)
            qg_all = [t_qg[c][tt] for c in range(4) for tt in range(2)]
            for tt in range(2):
                tsl = slice(tt * 512, (tt + 1) * 512)
                for hd in range(2):
                    pb = pbank()

                    def mm(e, hd=hd, pb=pb, tsl=tsl, qn=qn):
                        for c in range(4):
                            ins = e.matmul(ps[pb][:, :], lhsT=qn[:, c, hd * 128:(hd + 1) * 128], rhs=qgT[:, c, tsl],
                                           start=(c == 0), stop=(c == 3))
                        return ins
                    P.op("pe", mm, reads=[wt] + qg_all, writes=[t_ps[pb]])
                    P.op("dve", lambda e, hd=hd, pb=pb, tsl=tsl: e.tensor_copy(out=QnT[hd][:, tsl], in_=ps[pb][:, :]),
                         reads=[t_ps[pb]], writes=[t_QnT[hd]])
                pr_, psw = pbank(), pbank()
                for pbx, wsel in ((pr_, qr), (psw, qs)):
                    def mm(e, pbx=pbx, wsel=wsel, tsl=tsl):
                        for c in range(4):
                            ins = e.matmul(ps[pbx][:, :], lhsT=wsel[:, c, :], rhs=qgT[:, c, tsl], start=(c == 0), stop=(c == 3))
                        return ins
                    P.op("pe", mm, reads=[wt] + qg_all, writes=[t_ps[pbx]])
                P.op("dve", lambda e, tsl=tsl, pr_=pr_: e.tensor_tensor(out=rt1, in0=ps[pr_][:, :], in1=cos2[:, tsl], op=ALU.mult),
                     reads=[t_ps[pr_], t_cos], writes=[t_rt1])
                P.op("dve", lambda e, tsl=tsl, psw=psw: e.tensor_tensor(out=rec, in0=ps[psw][:, :], in1=sin_s[:, tsl], op=ALU.mult),
                     reads=[t_ps[psw], t_sin], writes=[t_rec])
                P.op("dve", lambda e, tsl=tsl: e.tensor_tensor(out=Qr[0][0:64, tsl], in0=rt1[0:64, :], in1=rec[0:64, :], op=ALU.add),
                     reads=[t_rt1, t_rec], writes=[t_Qr[0]])
                P.op("dve", lambda e, tsl=tsl: e.tensor_tensor(out=Qr[1][64:128, tsl], in0=rt1[64:128, :], in1=rec[64:128, :], op=ALU.add),
                     reads=[t_rt1, t_rec], writes=[t_Qr[1]])
            for hd in range(2):
                KnT_h, Vh_h, t_KnT_h, t_Vh_h = KV[hd]
                for kt in range(nk512):
                    ksl = slice(kt * 512, (kt + 1) * 512)
                    pb = pbank()

                    def mm(e, hd=hd, pb=pb, ksl=ksl, kn=kn):
                        for c in range(2):
                            ins = e.matmul(ps[pb][:, :], lhsT=kn[:, c, hd * 128:(hd + 1) * 128], rhs=kvgT[:, c, ksl],
                                           start=(c == 0), stop=(c == 1))
                        return ins
                    P.op("pe", mm, reads=[wt, t_kvg[0][kt], t_kvg[1][kt]], writes=[t_ps[pb]])
                    P.op("dve", lambda e, pb=pb, ksl=ksl, KnT_h=KnT_h: e.tensor_copy(out=KnT_h[:, ksl], in_=ps[pb][:, :]),
                         reads=[t_ps[pb]], writes=[t_KnT_h[kt]])
            for kt in range(nk512):
                for half in range(2):
                    pb2 = pbank()

                    def mmv(e, pb2=pb2, kt=kt, half=half, vv=vv):
                        for i in range(2):
                            kb = kt * 4 + half * 2 + i
                            for c in range(2):
                                ins = e.matmul(ps[pb2][:, i * 256:(i + 1) * 256], lhsT=kvgT[:, c, kb * 128:(kb + 1) * 128],
                                               rhs=vv[:, c, :], start=(c == 0), stop=(c == 1))
                        return ins
                    P.op("pe", mmv, reads=[wt, t_kvg[0][kt], t_kvg[1][kt]], writes=[t_ps[pb2]])
                    for hd in range(2):
                        Vdst, t_Vdst = KV[hd][1], KV[hd][3]
                        kb0 = kt * 4 + half * 2
                        P.op("act", lambda e, pb2=pb2, hd=hd, kb0=kb0, Vdst=Vdst: e.activation(
                            out=Vdst[:, kb0:kb0 + 2, :],
                            in_=ps[pb2][:, :].rearrange("p (i h d) -> p i h d", i=2, h=2)[:, :, hd, :], func=AF.Copy),
                            reads=[t_ps[pb2]], writes=[t_Vdst])
            for hd in range(2):
                h = 2 * pr + hd
                KnT_h, Vh_h, t_KnT_h, t_Vh_h = KV[hd]
                LOOK = 2
                pend = []
                cnt = 0

                def finalize(tt, po, psm, h=h):
                    P.op("dve", lambda e: e.reciprocal(out=rec, in_=ps[psm][:, :]), reads=[t_ps[psm]], writes=[t_rec])
                    P.op("dve", lambda e: e.tensor_tensor(out=oT[:, h, tt * 512:(tt + 1) * 512], in0=ps[po][:, :], in1=rec,
                                                          op=ALU.mult), reads=[t_ps[po], t_rec], writes=[t_oT[h][tt]])

                def emit_pend(ent, t_Vh_h=t_Vh_h):
                    f, k_, po_, psm_, fin = ent
                    P.op("pe", f, reads=[t_Vh_h, t_pt[k_], t_ones], writes=[t_ps[po_], t_ps[psm_]])
                    if fin is not None:
                        finalize(fin, po_, psm_)

                for tt in range(2):
                    qa = (T0 + 512 * tt) // 128
                    nkb = qa + 4
                    po, psm = ((2, 3), (1, 4))[tt]
                    for kb in range(nkb):
                        i = kb - qa
                        c0 = 128 * i if i > 0 else 0
                        qsl = slice(tt * 512 + c0, (tt + 1) * 512)
                        pb = gbank()
                        sl_ = cnt % 4
                        cnt += 1

                        def mms(e, hd=hd, kb=kb, i=i, c0=c0, qsl=qsl, pb=pb, KnT_h=KnT_h):
                            e.matmul(ps[pb][:, c0:512], lhsT=KnT_h[:, kb * 128:(kb + 1) * 128], rhs=QnT[hd][:, qsl],
                                     start=True, stop=False)
                            ins = e.matmul(ps[pb][:, c0:512], lhsT=kpe_dup[:, kb * 128:(kb + 1) * 128], rhs=Qr[hd][:, qsl],
                                           start=False, stop=(i < 0))
                            if i >= 0:
                                ins = e.matmul(ps[pb][:, c0:c0 + 128], lhsT=ident[:, :], rhs=maskneg[:, :], start=False, stop=True)
                            return ins
                        P.op("pe", mms, reads=[t_KnT_h[kb // 4], t_QnT[hd], t_Qr[hd], t_kpe[kb // 4], t_ident, t_mask],
                             writes=[t_ps[pb]])
                        P.op("act", lambda e, pb=pb, c0=c0, sl_=sl_: e.activation(out=ptile[sl_][:, c0:512], in_=ps[pb][:, c0:512],
                                                                                   func=AF.Exp, scale=SCALE),
                             reads=[t_ps[pb]], writes=[t_pt[sl_]])

                        def mmo(e, kb=kb, c0=c0, sl_=sl_, nkb=nkb, po=po, psm=psm, Vh_h=Vh_h):
                            e.matmul(ps[po][:, c0:512], lhsT=Vh_h[:, kb, :], rhs=ptile[sl_][:, c0:512], start=(kb == 0),
                                     stop=(kb == nkb - 1))
                            return e.matmul(ps[psm][:, c0:512], lhsT=ones[:, :], rhs=ptile[sl_][:, c0:512], start=(kb == 0),
                                            stop=(kb == nkb - 1))
                        pend.append((mmo, sl_, po, psm, tt if kb == nkb - 1 else None))
                        if len(pend) > LOOK:
                            emit_pend(pend.pop(0))
                for ent in pend:
                    emit_pend(ent)
            if blk == 0:
                mod_blocks(False, ADA2[pr], ADA2[pr + 1])
            rot["lo"], rot["cnt"] = 4, 4

    def branch_b(plan, blk):
        for j in range(8):
            if plan:
                wp.declare(("bg", blk, j), wsrc(w_bg_d, j * 512, 512), KC, 512)
                continue
            if j == 0:
                fence(scr_toks)
            wv, wt = wp.next(("bg", blk, j))
            for oc in range(2):
                o = j * 2 + oc
                for tt in range(2):
                    pg, py = gbank(), gbank()
                    sl_ = (oc * 2 + tt) % 2

                    def mmg(e, wv=wv, oc=oc, tt=tt, pg=pg):
                        for kc in range(KC):
                            ins = e.matmul(ps[pg][:, :], lhsT=wv[:, kc, 256 + oc * 128:256 + (oc + 1) * 128],
                                           rhs=hT[:, kc, tt * 512:(tt + 1) * 512], start=(kc == 0), stop=(kc == KC - 1))
                        return ins
                    P.op("pe", mmg, reads=[wt] + [t_hT[kc][n] for kc in range(KC) for n in range(4 * tt, 4 * tt + 4)],
                         writes=[t_ps[pg]])
                    P.op("act", lambda e, pg=pg, sl_=sl_: e.activation(out=tmpf[sl_], in_=ps[pg][:, :], func=AF.Sigmoid),
                         reads=[t_ps[pg]], writes=[t_tmp[sl_]])

                    def mmy(e, wv=wv, oc=oc, tt=tt, py=py):
                        for g in range(16):
                            ins = e.matmul(ps[py][:, :], lhsT=wv[:, g, oc * 128:(oc + 1) * 128],
                                           rhs=oT[:, g, tt * 512:(tt + 1) * 512], start=(g == 0), stop=(g == 15))
                        return ins
                    P.op("pe", mmy, reads=[wt] + [t_oT[h][tt] for h in range(16)], writes=[t_ps[py]])
                    P.op("dve", lambda e, py=py, sl_=sl_: e.tensor_tensor(out=tmpf[sl_], in0=ps[py][:, :], in1=tmpf[sl_], op=ALU.mult),
                         reads=[t_ps[py], t_tmp[sl_]], writes=[t_tmp[sl_]])
                    P.op("dve", lambda e, o=o, tt=tt, sl_=sl_: e.tensor_tensor(
                        out=mergedT[:, o, tt * 512:(tt + 1) * 512], in0=mergedT[:, o, tt * 512:(tt + 1) * 512], in1=tmpf[sl_],
                        op=ALU.add), reads=[t_tmp[sl_], t_mg[o][tt]], writes=[t_mg[o][tt]])

    def wout_phase(plan, blk):
        for j in range(4):
            if plan:
                wp.declare(("wo", blk, j), wsrc(w_out_d, j * 512, 512), KC, 512)
                continue
            if j == 0:
                fence(scr_toks + big0_toks)
            wv, wt = wp.next(("wo", blk, j))
            for oc in range(4):
                o = j * 4 + oc
                for tt in range(2):
                    pb = gbank()

                    def mm(e, wv=wv, oc=oc, tt=tt, pb=pb):
                        for kc in range(KC):
                            ins = e.matmul(ps[pb][:, :], lhsT=wv[:, kc, oc * 128:(oc + 1) * 128],
                                           rhs=mergedT[:, kc, tt * 512:(tt + 1) * 512], start=(kc == 0), stop=(kc == KC - 1))
                        return ins
                    P.op("pe", mm, reads=[wt] + [t_mg[kc][tt] for kc in range(KC)], writes=[t_ps[pb]])
                    P.op("act", lambda e, o=o, tt=tt, pb=pb: e.activation(out=yT[:, o, tt * 512:(tt + 1) * 512], in_=ps[pb][:, :],
                                                                           func=AF.Copy), reads=[t_ps[pb]], writes=[t_yT[o][tt]])
                epi_chunk_stats(0, o)

    def epi_chunk_stats(sl, o):
        psc = ps[3]
        nt = NT // 128
        k = o % 2
        P.op("act", lambda e: e.activation(out=sqw[k], in_=yT[:, o, :], func=AF.Square),
             reads=[t_yT[o][0], t_yT[o][1]], writes=[t_sqe[k]])

        def mmc(e):
            for n in range(nt):
                ins = e.matmul(psc[:, n:n + 1], lhsT=sqw[k][:, n * 128:(n + 1) * 128], rhs=ones[:, 0:1],
                               start=(o == 0 and n == 0), stop=(o == KC - 1 and n == nt - 1))
            return ins
        P.op("pe", mmc, reads=[t_ones, t_sqe[k]], writes=[t_ps[3]])
        P.op("dve", lambda e: e.tensor_scalar(out=yT[:, o, :], in0=yT[:, o, :], scalar1=prm[:, 3 * sl + 2, o:o + 1],
                                              scalar2=None, op0=ALU.mult),
             reads=[t_prm, t_sqe[k]], writes=[t_yT[o][0], t_yT[o][1]])

    def epilogue(plan, T0, sl, res_d, res_toks, stats_done=False):
        if plan:
            return
        psc = ps[3]
        nt = NT // 128
        if not stats_done:
            fence(scr_toks)
            for o in range(KC):
                epi_chunk_stats(sl, o)
        rcol = stat[:, 24:24 + nt]
        P.op("act", lambda e: e.activation(out=rcol, in_=psc[:, 0:nt], func=AF.Sqrt, scale=1.0 / D, bias=epsc[:, 0:1]),
             reads=[t_ps[3], t_epsc], writes=[t_stat[24]])
        P.op("dve", lambda e: e.reciprocal(out=rcol, in_=rcol), reads=[t_stat[24]], writes=[t_stat[24]])
        hb_ = 0

        def load(n):
            b, r0 = n % 2, T0 + n * 128
            P.dma("sp", lambda e: e.dma_start(out=xs[b], in_=res_d[r0:r0 + 128, :]),
                  reads=([res_toks[r0 // 128]] if res_toks else []), writes=[t_xs[b]])

        load(0)
        load(1)
        for n in range(nt):
            b = n % 2
            r0 = T0 + n * 128
            tt = n // 4
            for q4 in range(4):
                pb = 4 + hb_ % 4
                hb_ += 1

                def tr(e, n=n, q4=q4, pb=pb):
                    for k in range(4):
                        o = q4 * 4 + k
                        ins = e.transpose(out=ps[pb][:, k * 128:(k + 1) * 128], in_=yT[:, o, n * 128:(n + 1) * 128],
                                          identity=identf[:, :])
                    return ins
                P.op("pe", tr, reads=[t_yT[q4 * 4 + k][tt] for k in range(4)] + [t_ident], writes=[t_ps[pb]])
                P.op("dve", lambda e, b=b, q4=q4, pb=pb, n=n: e.scalar_tensor_tensor(
                    out=xs[b][:, q4 * 512:(q4 + 1) * 512], in0=ps[pb][:, :], scalar=rcol[:, n:n + 1],
                    in1=xs[b][:, q4 * 512:(q4 + 1) * 512], op0=ALU.mult, op1=ALU.add),
                    reads=[t_ps[pb], t_stat[24]], writes=[t_xs[b]])
            P.dma("sp", lambda e, b=b, r0=r0: e.dma_start(out=out_d[r0:r0 + 128, :], in_=xs[b]),
                  reads=[t_xs[b]], writes=[d_x1[r0 // 128]], sem_tok=t_st[b])
            if n + 2 < nt:
                load(n + 2)

    def ffn_up(plan):
        for j in range(FC // 2):
            if plan:
                wp.declare(("up", j), wsrc(w_up2_d, j * 512, 512), KC, 512)
                continue
            if j == 0:
                fence(scr_toks)
            wv, wt = wp.next(("up", j))
            for f2 in range(2):
                fc = 2 * j + f2
                for tt in range(4):
                    tsl = slice(tt * 512, (tt + 1) * 512)
                    hts = [t_h2T[kc][n] for kc in range(KC) for n in range(4 * tt, 4 * tt + 4)]
                    for gvsel in range(2):
                        pb = gbank()
                        ch = fc + FC * gvsel
                        c0 = f2 * 256 + gvsel * 128

                        def mm(e, wv=wv, c0=c0, tsl=tsl, pb=pb):
                            for kc in range(KC):
                                ins = e.matmul(ps[pb][:, :], lhsT=wv[:, kc, c0:c0 + 128], rhs=h2T[:, kc, tsl],
                                               start=(kc == 0), stop=(kc == KC - 1))
                            return ins
                        P.op("pe", mm, reads=[wt] + hts, writes=[t_ps[pb]])
                        hb, yb = hbuf[gvsel], ybuf[gvsel]
                        if tt == 0:
                            P.op("dve", lambda e, hb=hb: e.memset(hb[:, 0:2], 0.0), writes=[t_hb[gvsel]])
                        P.op("act", lambda e, hb=hb, pb=pb: e.activation(out=hb[:, 2:514], in_=ps[pb][:, :], func=AF.Copy),
                             reads=[t_ps[pb]], writes=[t_hb[gvsel]])
                        P.op("act", lambda e, yb=yb, pb=pb, ch=ch: e.activation(
                            out=yb, in_=ps[pb][:, :], func=AF.Identity, scale=cols[:, C_CW + 176 + ch:C_CW + 176 + ch + 1],
                            bias=cols[:, C_CB + ch:C_CB + ch + 1]), reads=[t_ps[pb], t_cols], writes=[t_yb[gvsel]])
                        for tap, off in ((1, 1), (0, 0)):
                            P.op("dve", lambda e, hb=hb, yb=yb, ch=ch, tap=tap, off=off: e.scalar_tensor_tensor(
                                out=yb, in0=hb[:, off:off + 512], scalar=cols[:, C_CW + 88 * tap + ch:C_CW + 88 * tap + ch + 1],
                                in1=yb, op0=ALU.mult, op1=ALU.add), reads=[t_hb[gvsel], t_cols], writes=[t_yb[gvsel]])
                        P.op("dve", lambda e, hb=hb: e.tensor_copy(out=hb[:, 0:2], in_=hb[:, 512:514]),
                             reads=[t_yb[gvsel]], writes=[t_hb[gvsel]])
                    k = tt % 2
                    P.op("act", lambda e: e.activation(out=sgb, in_=ybuf[0], func=AF.Silu), reads=[t_yb[0]], writes=[t_sg])
                    P.op("dve", lambda e, k=k: e.tensor_tensor(out=gvs[k], in0=sgb, in1=ybuf[1], op=ALU.mult),
                         reads=[t_sg, t_yb[1]], writes=[t_gvs[k]])
                    P.dma("sp", lambda e, k=k, fc=fc, tsl=tsl: e.dma_start(out=gv_d[fc, :, tsl], in_=gvs[k]),
                          reads=[t_gvs[k]], writes=[d_gv[fc][tt]], sem_tok=t_gvst[k])

    def ffn_down(plan):
        NG = FC // 4
        gi = 0
        for th in range(2):
            for oq in range(4):
                for g in range(NG):
                    if plan:
                        wp.declare(("dn", th, oq, g), w_down_d[g * 512:(g + 1) * 512, oq * 512:(oq + 1) * 512]
                                   .rearrange("(k p) o -> p k o", p=128), 4, 512)
                        continue
                    if th == 0 and oq == 0 and g == 0:
                        fence(scr_toks + big0_toks + mg_toks)
                    wv, wt = wp.next(("dn", th, oq, g))
                    sl_ = gi % 3
                    gi += 1
                    P.dma("sp", lambda e, sl_=sl_, g=g, th=th: e.dma_start(
                        out=gslot[sl_], in_=gv_d[4 * g:4 * g + 4, :, th * 1024:(th + 1) * 1024].rearrange("f p t -> p f t")),
                        reads=[d_gv[4 * g + f][2 * th + t2] for f in range(4) for t2 in range(2)], writes=[t_gslot[sl_]])

                    def mm(e, wv=wv, sl_=sl_, g=g):
                        for f in range(4):
                            for oc in range(4):
                                for t2 in range(2):
                                    ins = e.matmul(ps[oc * 2 + t2][:, :], lhsT=wv[:, f, oc * 128:(oc + 1) * 128],
                                                   rhs=gslot[sl_][:, f, t2 * 512:(t2 + 1) * 512],
                                                   start=(g == 0 and f == 0), stop=(g == NG - 1 and f == 3))
                        return ins
                    P.op("pe", mm, reads=[wt, t_gslot[sl_]], writes=t_ps)
                if plan:
                    continue
                for oc in range(4):
                    for t2 in range(2):
                        o = oq * 4 + oc
                        if t2 == 0:
                            P.op("act", lambda e, o=o, oc=oc, t2=t2: e.activation(out=yT[:, o, t2 * 512:(t2 + 1) * 512],
                                                                                   in_=ps[oc * 2 + t2][:, :], func=AF.Copy),
                                 reads=[t_ps[oc * 2 + t2]], writes=[t_yT[o][t2]])
                        else:
                            P.op("dve", lambda e, o=o, oc=oc, t2=t2: e.tensor_copy(out=yT[:, o, t2 * 512:(t2 + 1) * 512],
                                                                                    in_=ps[oc * 2 + t2][:, :]),
                                 reads=[t_ps[oc * 2 + t2]], writes=[t_yT[o][t2]])
            epilogue(plan, th * 1024, 1, out_d, d_x1)

    def all_phases(plan):
        consts(plan)
        phase_mod(plan, 0)
        for blk in range(NB if stop_after == "all" else 1):
            if not plan:
                fence(big0_toks + mg_toks)
            prologue(plan, x_d, blk * NT, NT // 128, 0, hT, t_hT)
            if not plan:
                dump("hT", hT, all_hT)
            if stop_after == "p1":
                return
            gmlp_setup(plan)
            vphase(plan, blk)
            mixing(plan)
            uphase(plan, blk)
            if not plan:
                dump("aT", big0[:, 16384:32768], t_A)
            if stop_after == "gmlp":
                return
            if blk == 0:
                phase_mod(plan, 1)
            branch_a(plan, blk)
            if not plan:
                dump("mgA", mergedT[:, :, :], [t for r in t_mg for t in r])
            if stop_after == "brA":
                return
            latents(plan, blk)
            if not plan:
                dump("kpe", kpe_dup[:, :], t_kpe)
                dump("kvg", kvgT[:, :, :], [t for r in t_kvg for t in r])
                dump("qg", qgT[:, :, :], [t for r in t_qg for t in r])
            if not plan:
                dump("cos", cos2[:, :], [t_cos])
                dump("sin", sin_s[:, :], [t_sin])
            if stop_after in ("lat", "lat0", "lat1", "lat2"):
                return
            attention(plan, blk)
            if blk == 0:
                phase_mod(plan, 2, stream=False)
            if not plan:
                dump("oT", big0[:, 16384:32768], all_oT)
            if stop_after == "att":
                return
            branch_b(plan, blk)
            if not plan:
                dump("mg", mergedT[:, :, :], [t for r in t_mg for t in r])
            if stop_after == "brB":
                return
            wout_phase(plan, blk)
            epilogue(plan, blk * NT, 0, x_d, None, stats_done=True)
            if stop_after == "x1":
                break
        if stop_after == "x1":
            return
        if not plan:
            fence(big0_toks + mg_toks)
        prologue(plan, out_d, 0, S // 128, 1, h2T, t_h2T, src_toks=d_x1)
        ffn_up(plan)
        ffn_down(plan)

    all_phases(True)
    all_phases(False)

    done_rows = {"all": S, "x1": NT}.get(stop_after, 0)
    fence(scr_toks)
    t_o = P.dtok("o")
    for i in range(done_rows // 128, S // 128):
        b = i % 2
        P.dma("sp", lambda e, b=b, i=i: e.dma_start(out=xs[b], in_=x_d[i * 128:(i + 1) * 128, :]), writes=[t_xs[b]])
        P.dma("sp", lambda e, b=b, i=i: e.dma_start(out=out_d[i * 128:(i + 1) * 128, :], in_=xs[b]),
              reads=[t_xs[b]], writes=[d_x1[i]], sem_tok=t_o)
    d_out.extend(d_x1)
    P.wait_all("sp", d_out)
    P.emit()
    return nc


def _col(v):
    v = np.asarray(v, dtype=np.float32).reshape(-1, 128)
    return np.ascontiguousarray(v.T)


def make_in_maps(inp, cores):
    f32 = lambda k: np.ascontiguousarray(inp[k], dtype=np.float32)
    w_in = f32("w_in")
    kpe = w_in[:, O_KPE:O_KPE + 64]
    swp = np.concatenate([kpe[:, 32:64], kpe[:, 0:32]], axis=1)
    w_lat = np.ascontiguousarray(np.concatenate([w_in[:, O_QL:O_QL + 768], kpe, kpe, swp, swp], axis=1))
    w_uq, w_ukv = f32("w_uq"), f32("w_ukv")
    w_pair = np.zeros((8, 128, 3072), np.float32)
    chunked = lambda a: a.reshape(-1, 128, a.shape[1]).transpose(1, 0, 2)
    for pr in range(8):
        parts = []
        hs = (2 * pr, 2 * pr + 1)
        parts.append(np.concatenate([w_uq[:, h * 192:h * 192 + 128] for h in hs], axis=1))
        parts.append(np.concatenate([w_uq[:, h * 192 + 128:h * 192 + 192] for h in hs], axis=1))
        parts.append(np.concatenate([np.concatenate([w_uq[:, h * 192 + 160:h * 192 + 192],
                                                     w_uq[:, h * 192 + 128:h * 192 + 160]], axis=1) for h in hs], axis=1))
        parts.append(np.concatenate([w_ukv[:, h * 256:h * 256 + 128] for h in hs], axis=1))
        parts.append(np.concatenate([w_ukv[:, h * 256 + 128:h * 256 + 256] for h in hs], axis=1))
        w_pair[pr] = np.concatenate([chunked(a).reshape(128, -1) for a in parts], axis=1)
    def fuse(wb, g0):
        return np.ascontiguousarray(np.concatenate(
            [np.concatenate([wb[:, j * 256:(j + 1) * 256], w_in[:, g0 + j * 256:g0 + (j + 1) * 256]], axis=1) for j in range(8)], axis=1))
    shared = {"w_ada": f32("w_ada"), "w_in": w_in, "w_ag": fuse(f32("w_branch_a"), O_GA), "w_bg": fuse(f32("w_branch_b"), O_GB),
              "w_lat": w_lat, "w_pair": w_pair, "w_out": f32("w_out"), "w_down": f32("w_down"),
              "w_up2": np.ascontiguousarray(np.asarray(inp["w_up"], np.float32).reshape(D, 2, FC, 128).transpose(0, 2, 1, 3)
                                            .reshape(D, 2 * D_FF)),
              "wsT": np.ascontiguousarray(np.transpose(np.asarray(inp["gm_w_s"], np.float32), (2, 0, 1))),
              "bs": np.ascontiguousarray(np.asarray(inp["gm_b_s"], np.float32).reshape(1, 2048))}
    maps = []
    for b in cores:
        cols = np.zeros((128, NCOL), np.float32)
        cols[:, C_BADA:C_BADA + 96] = _col(inp["b_ada"])
        cols[:, C_G1:C_G1 + 16] = _col(inp["pre_norm1_g"])
        cols[:, C_G2:C_G2 + 16] = _col(inp["pre_norm2_g"])
        cols[:, C_GP1:C_GP1 + 16] = _col(inp["post_norm1_g"])
        cols[:, C_GP2:C_GP2 + 16] = _col(inp["post_norm2_g"])
        cols[:, C_LNG:C_LNG + 16] = _col(inp["gm_ln_g"])
        cols[:, C_LNB:C_LNB + 16] = _col(inp["gm_ln_b"])
        cols[:, C_QG:C_QG + 4] = _col(inp["q_norm_g"])
        cols[:, C_KVG:C_KVG + 2] = _col(inp["kv_norm_g"])
        cols[:, C_C:C_C + 16] = _col(inp["c"][b])
        for k in range(3):
            cols[:, C_CW + 88 * k:C_CW + 88 * (k + 1)] = _col(inp["conv_w"][k])
        cols[:, C_CB:C_CB + 88] = _col(inp["conv_b"])
        pidx = np.arange(128)
        cols[:, C_INV] = (10000.0 ** (-(2.0 * (pidx % 32)) / 64.0)).astype(np.float32)
        sgn = np.where((pidx % 64) < 32, -1.0, 1.0).astype(np.float32)
        cols[:, C_SGN] = sgn
        cols[:, C_NPI] = np.float32(-np.pi)
        cols[:, C_NPS] = (np.float32(-np.pi) * sgn).astype(np.float32)
        m = dict(shared)
        m["x"] = np.ascontiguousarray(inp["x"][b], dtype=np.float32)
        m["pos"] = np.ascontiguousarray(inp["positions"][b], dtype=np.int32).reshape(1, S)
        m["cols"] = cols
        maps.append(m)
    return maps


def kernel(**inputs):
    nc = build()
    maps = make_in_maps(inputs, list(range(8)))
    res = run_bass_kernel_spmd(nc, maps, core_ids=list(range(8)))
    return np.stack([np.asarray(r["out"], dtype=np.float32) for r in res.results], axis=0)
```

```python
import numpy as np
import ml_dtypes
import concourse.bass as bass
import concourse.mybir as mybir
from concourse.bass_utils import run_bass_kernel_spmd

F32 = mybir.dt.float32
BF16 = mybir.dt.bfloat16
I32 = mybir.dt.int32
AF = mybir.ActivationFunctionType
ALU = mybir.AluOpType
AX = mybir.AxisListType

D = 2048
S = 2048
KC = D // 128
NB = 2
NT = S // NB
EPS = 1e-6
D_FF = 5632
FC = D_FF // 128
N_HEADS = 16

C_BADA, C_G1, C_G2, C_GP1, C_GP2, C_LNG, C_LNB, C_QG, C_KVG, C_C, C_CW, C_CB = (
    0, 96, 112, 128, 144, 160, 176, 192, 196, 198, 214, 478)
C_INV, C_SGN, C_NPI, C_NPS = 566, 567, 568, 569
NCOL = 570


class Ev:
    __slots__ = ("sem", "val")

    def __init__(self, sem, val):
        self.sem = sem
        self.val = val


class Tok:
    __slots__ = ("w", "r", "dsem", "dcnt", "name", "excl")

    def __init__(self, name=""):
        self.excl = False
        self.w = None
        self.r = {}
        self.dsem = None
        self.dcnt = 0
        self.name = name


class Prog:
    ENG = ("pe", "act", "dve", "pool", "sp")

    def __init__(self, nc):
        self.nc = nc
        self.ops = {e: [] for e in self.ENG}
        self.sems = {e: nc.alloc_semaphore("s_" + e) for e in self.ENG}
        self.cnt = {e: 0 for e in self.ENG}
        self.waited = {e: {} for e in self.ENG}
        self.semobj = {self.sems[e].num: self.sems[e] for e in self.ENG}
        self.ntok = 0

    def tok(self, name=""):
        return Tok(name)

    def toks(self, n):
        return [Tok() for _ in range(n)]

    def dtok(self, name=""):
        t = Tok(name)
        self.ntok += 1
        t.dsem = self.nc.alloc_semaphore("d%d_%s" % (self.ntok, name))
        self.semobj[t.dsem.num] = t.dsem
        return t

    def _deps(self, eng, reads, writes, excl_own=None):
        need = {}
        for t in reads:
            if t.w is not None:
                need[t.w.sem] = max(need.get(t.w.sem, 0), t.w.val)
            if t.excl:
                for s, v in t.r.items():
                    if s != excl_own:
                        need[s] = max(need.get(s, 0), v)
        for t in writes:
            if t.w is not None:
                need[t.w.sem] = max(need.get(t.w.sem, 0), t.w.val)
            for s, v in t.r.items():
                need[s] = max(need.get(s, 0), v)
        w = self.waited[eng]
        waits = []
        if eng == "pe":
            need.pop(self.sems["pe"].num, None)
        for s, v in need.items():
            if w.get(s, 0) < v:
                waits.append((s, v))
                w[s] = v
        return waits

    def _mark(self, ev, reads, writes):
        for t in reads:
            t.r[ev.sem] = max(t.r.get(ev.sem, 0), ev.val)
        for t in writes:
            t.w = ev
            t.r = {}

    def op(self, eng, fn, reads=(), writes=()):
        waits = self._deps(eng, reads, writes, excl_own=self.sems[eng].num)
        self.cnt[eng] += 1
        ev = Ev(self.sems[eng].num, self.cnt[eng])
        self.ops[eng].append((waits, fn, (self.sems[eng].num, 1)))
        self._mark(ev, reads, writes)
        return ev

    def dma(self, eng, fn, reads=(), writes=(), sem_tok=None):
        st = sem_tok if sem_tok is not None else writes[0]
        assert st.dsem is not None
        waits = self._deps(eng, reads, writes)
        st.dcnt += 16
        ev = Ev(st.dsem.num, st.dcnt)
        self.ops[eng].append((waits, fn, (st.dsem.num, 16)))
        self._mark(ev, reads, writes)
        return ev

    def wait_all(self, eng, toks):
        waits = self._deps(eng, [], toks)
        self.ops[eng].append((waits, None, None))

    def emit(self):
        nc, ops, semobj = self.nc, self.ops, self.semobj

        def run(e, lst):
            for waits, fn, inc in lst:
                for s, v in waits:
                    e.wait_ge(semobj[s], v)
                if fn is None:
                    continue
                ins = fn(e)
                if inc is not None:
                    ins.then_inc(semobj[inc[0]], inc[1])

        with nc.Block() as block:
            @block.tensor
            def _(e):
                run(e, ops["pe"])

            @block.scalar
            def _(e):
                run(e, ops["act"])

            @block.vector
            def _(e):
                run(e, ops["dve"])

            @block.gpsimd
            def _(e):
                run(e, ops["pool"])

            @block.sync
            def _(e):
                run(e, ops["sp"])


class WPool:
    def __init__(self, P, nc, nslots, elems):
        self.P = P
        self.slots = [nc.alloc_sbuf_tensor("wslot%d" % i, [128, elems], BF16) for i in range(nslots)]
        self.toks = [P.dtok("w%d" % i) for i in range(nslots)]
        self.plan = []
        self.issued = 0
        self.cur = 0
        self.live = 1

    def declare(self, tag, src, kc, ncols):
        self.plan.append((tag, src, kc, ncols))

    def _issue(self, i):
        tag, src, kc, ncols = self.plan[i]
        s = i % len(self.slots)
        dst = self.slots[s][:, 0:kc * ncols].rearrange("p (k o) -> p k o", k=kc)
        self.P.dma("pool", lambda e, dst=dst, src=src: e.dma_start(out=dst, in_=src), writes=[self.toks[s]])

    def next(self, tag):
        i = self.cur
        assert self.plan[i][0] == tag, (self.plan[i][0], tag)
        while self.issued < min(len(self.plan), i + 1 + len(self.slots) - self.live):
            self._issue(self.issued)
            self.issued += 1
        self.cur += 1
        _, _, kc, ncols = self.plan[i]
        s = i % len(self.slots)
        return self.slots[s][:, 0:kc * ncols].rearrange("p (k o) -> p k o", k=kc), self.toks[s]


O_U, O_V, O_QL, O_KVL, O_KPE, O_GA, O_GB = 0, 2048, 4096, 4608, 4864, 4928, 6976


def build(stop_after="all", dbg=()):
    nc = bass.Bass("TRN2", target_bir_lowering=False)
    P = Prog(nc)
    dt = lambda name, shape, ty, kind: nc.dram_tensor(name, shape, ty, kind=kind).ap()
    x_d = dt("x", [S, D], F32, "ExternalInput")
    pos_d = dt("pos", [1, S], I32, "ExternalInput")
    cols_d = dt("cols", [128, NCOL], F32, "ExternalInput")
    w_ada_d = dt("w_ada", [D, 6 * D], F32, "ExternalInput")
    w_in_d = dt("w_in", [D, 9024], F32, "ExternalInput")
    wsT_d = dt("wsT", [128, 16, 128], F32, "ExternalInput")
    bs_d = dt("bs", [1, 2048], F32, "ExternalInput")
    w_ag_d = dt("w_ag", [D, 2 * D], F32, "ExternalInput")
    w_bg_d = dt("w_bg", [D, 2 * D], F32, "ExternalInput")
    w_lat_d = dt("w_lat", [D, 1024], F32, "ExternalInput")
    w_pair_d = dt("w_pair", [8, 128, 3072], F32, "ExternalInput")
    w_out_d = dt("w_out", [D, D], F32, "ExternalInput")
    w_up2_d = dt("w_up2", [D, 2 * D_FF], F32, "ExternalInput")
    w_down_d = dt("w_down", [D_FF, D], F32, "ExternalInput")
    out_d = dt("out", [S, D], F32, "ExternalOutput")
    gv_d = nc.dram_tensor("gv_scr", [FC, 128, S], BF16).ap()
    dbg_d = {}
    for name, shape, ty in dbg:
        dbg_d[name] = dt("dbg_" + name, shape, ty, "ExternalOutput")

    sb = lambda name, shape, ty: nc.alloc_sbuf_tensor("sb_" + name, shape, ty)
    cols = sb("cols", [128, NCOL], F32)
    modc = sb("modc", [128, 96], F32)
    prm = sb("prm", [128, 6, KC], F32)
    scb = sb("scb", [128, KC], BF16)
    ident = sb("ident", [128, 128], BF16)
    identf = sb("identf", [128, 128], F32)
    ones = sb("ones", [128, 128], BF16)
    stat = sb("stat", [128, 64], F32)
    epsc = sb("epsc", [128, 1], F32)
    fdummy = sb("fdummy", [128, 2], F32)
    big0 = sb("big0", [128, 32768], BF16)
    hT = big0[:, 0:16384].rearrange("p (k t) -> p k t", k=KC)
    A3 = big0[:, 16384:32768].rearrange("p (n c) -> p n c", n=8)
    A4 = big0[:, 16384:32768].rearrange("p (n g q) -> p n g q", n=8, g=16)
    mergedT = sb("mergedT", [128, KC, NT], BF16)
    scr = sb("scr", [128, 12288], BF16)
    xs = [scr[:, 0:4096].bitcast(F32), scr[:, 4096:8192].bitcast(F32)]
    xn = [scr[:, 8192:10240], scr[:, 10240:12288]]
    WmT = scr[:, 0:2048].rearrange("p (g q) -> p g q", g=16)
    bsb = scr[:, 2048:6144].bitcast(F32).rearrange("p (g q) -> p g q", g=16)
    Cg = scr[:, 6144:10240].bitcast(F32).rearrange("p (g q) -> p g q", g=16)
    tmpb = [scr[:, 10240:10752], scr[:, 11264:11776]]
    tmpf = [scr[:, 10240:11264].bitcast(F32), scr[:, 11264:12288].bitcast(F32)]
    junkv = scr[:, 10240:12288]
    oT = big0[:, 16384:32768].rearrange("p (h t) -> p h t", h=16)
    qgT = sb("qgT", [128, 4, NT], BF16)
    kvgT = sb("kvgT", [128, 2, S], BF16)
    kpe_dup = sb("kpe_dup", [128, S], BF16)
    cos2 = sb("cos2", [128, NT], BF16)
    sin_s = sb("sin_s", [128, NT], BF16)
    maskneg = sb("maskneg", [128, 128], BF16)
    KnT_b = sb("KnT_b", [128, S], BF16)
    Vh_b = sb("Vh_b", [128, 16, 128], BF16)
    masktmp = sb("masktmp", [128, 128], F32)
    sqb = [scr[:, 0:512], scr[:, 512:1024]]
    posi = scr[:, 2048:4096].bitcast(I32)
    angf = scr[:, 4096:6144].bitcast(F32)
    targ = scr[:, 6144:8192].bitcast(F32)
    QnT = [scr[:, 0:1024], scr[:, 1024:2048]]
    Qr = [scr[:, 2048:3072], scr[:, 3072:4096]]
    KnT = scr[:, 4096:6144]
    Vh = scr[:, 6144:8192].rearrange("p (k d) -> p k d", k=16)
    ptile = [scr[:, 8192 + 512 * k:8192 + 512 * (k + 1)] for k in range(4)]
    rec = scr[:, 10240:11264].bitcast(F32)
    rt1 = scr[:, 11264:12288].bitcast(F32)
    yT = big0[:, :].bitcast(F32).rearrange("p (k t) -> p k t", k=KC)
    h2T = big0[:, :].rearrange("p (k t) -> p k t", k=KC)
    sqe = [scr[:, 8192:8704], scr[:, 8704:9216]]
    sqw = [scr[:, 8192:9216], scr[:, 9216:10240]]
    rstd_e = scr[:, 10240:12288].bitcast(F32)
    hbuf = [scr[:, 0:1028].bitcast(F32), scr[:, 1056:2084].bitcast(F32)]
    ybuf = [scr[:, 2112:3136].bitcast(F32), scr[:, 3136:4160].bitcast(F32)]
    sgb = scr[:, 4160:4672]
    gvs = [scr[:, 4672:5184], scr[:, 5184:5696]]
    gslot = [mergedT[:, :, :].rearrange("p k t -> p (k t)")[:, 4096 * i:4096 * (i + 1)].rearrange("p (f t) -> p f t", f=4)
             for i in range(3)]
    ps = [nc.alloc_psum_tensor("ps%d" % i, [128, 512], F32) for i in range(8)]
    t_ps = P.toks(8)
    for t in t_ps:
        t.excl = True
    t_pmod = t_ps[0]

    t_cols = P.dtok("cols")
    t_modc, t_prm, t_scb, t_ident, t_epsc, t_ones, t_fd = P.toks(7)
    t_stat = P.toks(64)
    t_xs = [P.dtok("xs0"), P.dtok("xs1")]
    t_xn = P.toks(2)
    t_hT = [[P.tok() for _ in range(NT // 128)] for _ in range(KC)]
    t_A = P.toks(8)
    t_mg = [[P.tok() for _ in range(2)] for _ in range(KC)]
    t_WmT = P.dtok("wmT")
    t_bsb = P.dtok("bsb")
    t_Cg, = P.toks(1)
    t_tmp = P.toks(2)
    t_dbg = P.dtok("dbg")
    d_out = []
    all_hT = [t for row in t_hT for t in row]
    t_qg = [[P.tok() for _ in range(2)] for _ in range(4)]
    t_kvg = [[P.tok() for _ in range(4)] for _ in range(2)]
    t_rq = P.toks(2)
    t_rkv = P.toks(4)
    t_rkvc, t_mask, t_cos, t_sin, t_posi, t_angf, t_targ = P.toks(7)
    t_posi = P.dtok("posi")
    t_kpe = P.toks(4)
    t_sqb = P.toks(2)
    t_QnT, t_Qr, t_pt = P.toks(2), P.toks(2), P.toks(4)
    t_KnT = P.toks(4)
    t_Vh, t_rec, t_rt1 = P.toks(3)
    t_oT = [[P.tok() for _ in range(2)] for _ in range(16)]
    t_KnT_b = P.toks(4)
    t_Vh_b = P.tok()
    KV = ((KnT, Vh, t_KnT, t_Vh), (KnT_b, Vh_b, t_KnT_b, t_Vh_b))
    t_yT = [[P.tok() for _ in range(2)] for _ in range(KC)]
    t_h2T = [[P.tok() for _ in range(S // 128)] for _ in range(KC)]
    t_sqe = P.toks(2)
    t_rse = P.toks(2)
    t_hb, t_yb = P.toks(2), P.toks(2)
    t_sg, = P.toks(1)
    t_gvs = P.toks(2)
    t_gslot = [P.dtok("gs%d" % i) for i in range(3)]
    t_gvst = [P.dtok("gvst%d" % i) for i in range(2)]
    d_x1 = P.toks(S // 128)
    d_gv = [[P.tok() for _ in range(4)] for _ in range(FC)]
    t_st = [P.dtok("st0"), P.dtok("st1")]
    all_oT = [t for r in t_oT for t in r]
    big0_toks = all_hT + t_A + all_oT + [t for r in t_yT for t in r] + [t for r in t_h2T for t in r]
    mg_toks = [t for r in t_mg for t in r] + t_gslot
    scr_toks = t_xs + t_xn + [t_WmT, t_bsb, t_Cg] + t_tmp + t_sqb + [t_posi, t_angf, t_targ] + t_QnT + t_Qr + t_pt \
        + t_KnT + [t_Vh, t_rec, t_rt1] + t_sqe + t_rse + t_hb + t_yb + [t_sg] + t_gvs

    wp = WPool(P, nc, 3, KC * 512)
    rot = {"n": 0, "lo": 4, "cnt": 4}

    def gbank():
        b = rot["lo"] + rot["n"] % rot["cnt"]
        rot["n"] += 1
        return b

    prot = {"n": 0}

    def pbank():
        b = 1 + prot["n"] % 7
        prot["n"] += 1
        return b

    def fence(toks):
        P.op("pool", lambda e: e.memset(fdummy[:, 0:1], 0.0), writes=list(toks) + [t_fd])

    def dump(name, src_ap, toks):
        if name in dbg_d:
            d = P.tok()
            P.dma("sp", lambda e: e.dma_start(out=dbg_d[name], in_=src_ap), reads=list(toks), writes=[d], sem_tok=t_dbg)
            d_out.append(d)

    def wsrc(w_d, c0, ncols):
        return w_d[:, c0:c0 + ncols].rearrange("(k p) o -> p k o", p=128)

    def consts(plan):
        if plan:
            return
        P.op("pool", lambda e: e.memset(epsc[:, :], EPS), writes=[t_epsc])
        P.op("pool", lambda e: e.memset(ones[:, :], 1.0), writes=[t_ones])
        P.dma("sp", lambda e: e.dma_start(out=cols[:, :], in_=cols_d), writes=[t_cols])
        P.op("pool", lambda e: e.memset(identf[:, :], 1.0), writes=[t_ident])
        P.op("pool", lambda e: e.affine_select(out=identf[:, :], in_=identf[:, :], pattern=[[-1, 128]],
                                                compare_op=ALU.is_equal, fill=0.0, base=0, channel_multiplier=1),
             reads=[t_ident], writes=[t_ident])
        P.op("pool", lambda e: e.tensor_copy(out=ident[:, :], in_=identf[:, :]), reads=[t_ident], writes=[t_ident])
        P.op("pool", lambda e: e.memset(masktmp[:, :], 0.0), writes=[t_mask])
        P.op("pool", lambda e: e.affine_select(out=masktmp[:, :], in_=masktmp[:, :], pattern=[[1, 128]],
                                                compare_op=ALU.is_ge, fill=-30000.0, base=0, channel_multiplier=-1),
             reads=[t_mask], writes=[t_mask])
        P.op("pool", lambda e: e.tensor_copy(out=maskneg[:, :], in_=masktmp[:, :]), reads=[t_mask], writes=[t_mask])

    MOD_PARTS = ((0, 8), (8, 12), (12, 24))

    def mod_blocks(plan, j0, j1):
        pmod = ps[0]
        for j in range(j0, j1):
            if plan:
                wp.declare(("ada", j), wsrc(w_ada_d, j * 512, 512), KC, 512)
                continue
            wv, wt = wp.next(("ada", j))

            def mm(e, wv=wv, j=j):
                for oc in range(4):
                    col = j * 4 + oc
                    for kc in range(KC):
                        ins = e.matmul(pmod[:, col:col + 1], lhsT=wv[:, kc, oc * 128:(oc + 1) * 128],
                                       rhs=scb[:, kc:kc + 1], start=(kc == 0), stop=(kc == KC - 1))
                return ins
            P.op("pe", mm, reads=[wt, t_scb], writes=[t_pmod])

    def phase_mod(plan, part, stream=True):
        j0, j1 = MOD_PARTS[part]
        pmod = ps[0]
        if not plan and part == 0:
            P.op("act", lambda e: e.activation(out=scb[:, :], in_=cols[:, C_C:C_C + KC], func=AF.Silu),
                 reads=[t_cols], writes=[t_scb])
        if stream:
            mod_blocks(plan, j0, j1)
        if plan:
            return
        c0, c1 = j0 * 4, j1 * 4
        P.op("dve", lambda e: e.tensor_tensor(out=modc[:, c0:c1], in0=pmod[:, c0:c1], in1=cols[:, C_BADA + c0:C_BADA + c1],
                                              op=ALU.add), reads=[t_pmod, t_cols], writes=[t_modc])
        def gs_sh(sl, c_sh, c_sc, c_g):
            P.op("dve", lambda e: e.scalar_tensor_tensor(out=prm[:, 3 * sl + 0, :], in0=modc[:, c_sc:c_sc + KC], scalar=1.0,
                                                         in1=cols[:, c_g:c_g + KC], op0=ALU.add, op1=ALU.mult),
                 reads=[t_modc, t_cols], writes=[t_prm])
            P.op("dve", lambda e: e.tensor_copy(out=prm[:, 3 * sl + 1, :], in_=modc[:, c_sh:c_sh + KC]),
                 reads=[t_modc], writes=[t_prm])

        def gg(sl, c_ga, c_gp):
            P.op("dve", lambda e: e.tensor_tensor(out=prm[:, 3 * sl + 2, :], in0=modc[:, c_ga:c_ga + KC],
                                                  in1=cols[:, c_gp:c_gp + KC], op=ALU.mult),
                 reads=[t_modc, t_cols], writes=[t_prm])
        if part == 0:
            gs_sh(0, 0, 16, C_G1)
        elif part == 1:
            gg(0, 32, C_GP1)
        else:
            gs_sh(1, 48, 64, C_G2)
            gg(1, 80, C_GP2)
            dump("modc", modc[:, :], [t_modc])

    def rstd_from(sq, rs, t_sq, t_rs, n):
        P.op("act", lambda e: e.activation(out=rs, in_=sq, func=AF.Sqrt, scale=1.0 / n, bias=epsc[:, 0:1]),
             reads=[t_sq, t_epsc], writes=[t_rs])
        P.op("dve", lambda e: e.reciprocal(out=rs, in_=rs), reads=[t_rs], writes=[t_rs])

    def prologue(plan, src_d, T0, ntiles, sl, dst, t_dst, src_toks=None):
        if plan:
            return
        fence(scr_toks)

        def stage_a(i):
            b = i % 2
            r0 = T0 + i * 128
            P.dma("sp", lambda e, b=b, r0=r0: e.dma_start(out=xs[b], in_=src_d[r0:r0 + 128, :]),
                  reads=([src_toks[r0 // 128]] if src_toks else []), writes=[t_xs[b]])
            sq, rs = stat[:, 2 * b:2 * b + 1], stat[:, 2 * b + 1:2 * b + 2]
            P.op("act", lambda e, b=b, sq=sq: e.activation(out=xn[b], in_=xs[b], func=AF.Square, accum_out=sq),
                 reads=[t_xs[b]], writes=[t_xn[b], t_stat[2 * b]])
            rstd_from(sq, rs, t_stat[2 * b], t_stat[2 * b + 1], D)
            P.op("dve", lambda e, b=b, rs=rs: e.tensor_scalar(out=xn[b], in0=xs[b], scalar1=rs, scalar2=None, op0=ALU.mult),
                 reads=[t_xs[b], t_stat[2 * b + 1]], writes=[t_xn[b]])

        def stage_b(i):
            b = i % 2
            for half in range(2):
                pb = 2 * b + half
                pv = ps[pb][:, :].bitcast(BF16)

                def tr(e, b=b, half=half, pv=pv):
                    for k in range(8):
                        kc = half * 8 + k
                        ins = e.transpose(out=pv[:, k * 128:(k + 1) * 128], in_=xn[b][:, kc * 128:(kc + 1) * 128],
                                          identity=ident[:, :])
                    return ins
                P.op("pe", tr, reads=[t_xn[b], t_ident], writes=[t_ps[pb]])
                for k in range(8):
                    kc = half * 8 + k
                    if half == 0:
                        P.op("act", lambda e, kc=kc, k=k, pv=pv, i=i: e.activation(
                            out=dst[:, kc, i * 128:(i + 1) * 128], in_=pv[:, k * 128:(k + 1) * 128], func=AF.Identity,
                            scale=prm[:, 3 * sl + 0, kc:kc + 1], bias=prm[:, 3 * sl + 1, kc:kc + 1]),
                            reads=[t_ps[pb], t_prm], writes=[t_dst[kc][i]])
                    else:
                        P.op("dve", lambda e, kc=kc, k=k, pv=pv, i=i: e.tensor_scalar(
                            out=dst[:, kc, i * 128:(i + 1) * 128], in0=pv[:, k * 128:(k + 1) * 128],
                            scalar1=prm[:, 3 * sl + 0, kc:kc + 1], scalar2=prm[:, 3 * sl + 1, kc:kc + 1],
                            op0=ALU.mult, op1=ALU.add), reads=[t_ps[pb], t_prm], writes=[t_dst[kc][i]])

        stage_a(0)
        for i in range(ntiles):
            if i + 1 < ntiles:
                stage_a(i + 1)
            stage_b(i)

    def gmlp_setup(plan):
        if plan:
            return
        fence(scr_toks)
        P.dma("pool", lambda e: e.dma_start(out=WmT, in_=wsT_d), writes=[t_WmT])
        P.op("pool", lambda e: e.affine_select(out=WmT, in_=WmT, pattern=[[0, 16], [1, 128]], compare_op=ALU.is_ge,
                                                fill=0.0, base=0, channel_multiplier=-1), reads=[t_WmT], writes=[t_WmT])
        P.dma("sp", lambda e: e.dma_start(out=bsb.rearrange("p g q -> p (g q)"), in_=bs_d.partition_broadcast(128)),
              writes=[t_bsb])
        for q4 in range(4):
            def mm(e, q4=q4):
                for k in range(4):
                    g = q4 * 4 + k
                    ins = e.matmul(ps[q4][:, k * 128:(k + 1) * 128], lhsT=ones[:, :], rhs=WmT[:, g, :], start=True, stop=True)
                return ins
            P.op("pe", mm, reads=[t_ones, t_WmT], writes=[t_ps[q4]])
            for k in range(4):
                g = q4 * 4 + k
                P.op("dve", lambda e, q4=q4, k=k, g=g: e.scalar_tensor_tensor(
                    out=Cg[:, g, :], in0=ps[q4][:, k * 128:(k + 1) * 128], scalar=cols[:, C_LNB + g:C_LNB + g + 1],
                    in1=bsb[:, g, :], op0=ALU.mult, op1=ALU.add), reads=[t_ps[q4], t_cols, t_bsb], writes=[t_Cg])

    def vphase(plan, blk):
        for j in range(4):
            if plan:
                wp.declare(("v", blk, j), wsrc(w_in_d, O_V + j * 512, 512), KC, 512)
                continue
            wv, wt = wp.next(("v", blk, j))
            for n in range(8):
                pb = gbank()

                def mm(e, wv=wv, n=n, pb=pb):
                    for kc in range(KC):
                        ins = e.matmul(ps[pb][:, :], lhsT=hT[:, kc, n * 128:(n + 1) * 128], rhs=wv[:, kc, :],
                                       start=(kc == 0), stop=(kc == KC - 1))
                    return ins
                P.op("pe", mm, reads=[wt] + [t_hT[kc][n] for kc in range(KC)], writes=[t_ps[pb]])
                P.op("act", lambda e, n=n, j=j, pb=pb: e.activation(out=A3[:, n, j * 512:(j + 1) * 512], in_=ps[pb][:, :],
                                                                      func=AF.Gelu_apprx_tanh),
                     reads=[t_ps[pb]], writes=[t_A[n]])

    def mixing(plan):
        if plan:
            return

        def cols6(n):
            c = 32 + 6 * (n % 3)
            return [stat[:, c + k:c + k + 1] for k in range(6)], t_stat[c:c + 6]

        def a_act(n):
            (s1, s2, mu, var, rsd, nb), ts = cols6(n)
            P.op("act", lambda e: e.activation(out=junkv, in_=A3[:, n, :], func=AF.Identity, accum_out=s1),
                 reads=[t_A[n]], writes=[t_tmp[0], t_tmp[1], ts[0]])
            P.op("act", lambda e: e.activation(out=junkv, in_=A3[:, n, :], func=AF.Square, accum_out=s2),
                 reads=[t_A[n]], writes=[t_tmp[0], t_tmp[1], ts[1]])

        def a_dve1(n):
            (s1, s2, mu, var, rsd, nb), ts = cols6(n)
            P.op("dve", lambda e: e.tensor_scalar(out=mu, in0=s1, scalar1=1.0 / 2048, scalar2=None, op0=ALU.mult),
                 reads=[ts[0]], writes=[ts[2]])
            P.op("dve", lambda e: e.tensor_tensor(out=var, in0=mu, in1=mu, op=ALU.mult), reads=[ts[2]], writes=[ts[3]])
            P.op("dve", lambda e: e.scalar_tensor_tensor(out=var, in0=s2, scalar=1.0 / 2048, in1=var, op0=ALU.mult,
                                                         op1=ALU.subtract), reads=[ts[1], ts[3]], writes=[ts[3]])

        def a_sqrt(n):
            (s1, s2, mu, var, rsd, nb), ts = cols6(n)
            P.op("act", lambda e: e.activation(out=rsd, in_=var, func=AF.Sqrt, scale=1.0, bias=epsc[:, 0:1]),
                 reads=[ts[3], t_epsc], writes=[ts[4]])

        def a_dve2(n):
            (s1, s2, mu, var, rsd, nb), ts = cols6(n)
            P.op("dve", lambda e: e.reciprocal(out=rsd, in_=rsd), reads=[ts[4]], writes=[ts[4]])
            P.op("dve", lambda e: e.scalar_tensor_tensor(out=nb, in0=mu, scalar=-1.0, in1=rsd, op0=ALU.mult, op1=ALU.mult),
                 reads=[ts[2], ts[4]], writes=[ts[5]])

        def b_norm_mm(n):
            (s1, s2, mu, var, rsd, nb), ts = cols6(n)
            P.op("dve", lambda e: e.tensor_scalar(out=A3[:, n, :], in0=A3[:, n, :], scalar1=rsd, scalar2=nb,
                                                  op0=ALU.mult, op1=ALU.add), reads=[ts[4], ts[5]], writes=[t_A[n]])
            for q4 in range(4):
                pb = 4 * (n % 2) + q4

                def mm(e, q4=q4, pb=pb):
                    for k in range(4):
                        g = q4 * 4 + k
                        ins = e.matmul(ps[pb][:, k * 128:(k + 1) * 128], lhsT=A3[:, n, g * 128:(g + 1) * 128],
                                       rhs=WmT[:, g, :], start=True, stop=True)
                    return ins
                P.op("pe", mm, reads=[t_A[n], t_WmT], writes=[t_ps[pb]])

        def b_ev(n):
            for q4 in range(4):
                pb = 4 * (n % 2) + q4
                P.op("dve", lambda e, pb=pb, q4=q4: e.tensor_copy(
                    out=A4[:, n, 4 * q4:4 * q4 + 4, :].rearrange("p g q -> p (g q)"), in_=ps[pb][:, :]),
                    reads=[t_ps[pb]], writes=[t_A[n]])

        a_act(0); a_dve1(0); a_sqrt(0); a_dve2(0); b_norm_mm(0)
        a_act(1); a_dve1(1)
        for k in range(8):
            if k + 1 < 8:
                a_sqrt(k + 1)
                a_dve2(k + 1)
                b_norm_mm(k + 1)
            b_ev(k)
            if k + 2 < 8:
                a_act(k + 2)
                a_dve1(k + 2)

    def uphase(plan, blk):
        for j in range(4):
            if plan:
                wp.declare(("u", blk, j), wsrc(w_in_d, O_U + j * 512, 512), KC, 512)
                continue
            wv, wt = wp.next(("u", blk, j))
            for oc in range(4):
                g = j * 4 + oc
                for tt in range(2):
                    pb = gbank()
                    sl_ = (oc * 2 + tt) % 2

                    def mm(e, wv=wv, oc=oc, tt=tt, pb=pb):
                        for kc in range(KC):
                            ins = e.matmul(ps[pb][:, :], lhsT=wv[:, kc, oc * 128:(oc + 1) * 128],
                                           rhs=hT[:, kc, tt * 512:(tt + 1) * 512], start=(kc == 0), stop=(kc == KC - 1))
                        return ins
                    P.op("pe", mm, reads=[wt] + [t_hT[kc][n] for kc in range(KC) for n in range(4 * tt, 4 * tt + 4)],
                         writes=[t_ps[pb]])
                    P.op("act", lambda e, pb=pb, sl_=sl_: e.activation(out=tmpb[sl_], in_=ps[pb][:, :], func=AF.Gelu_apprx_tanh),
                         reads=[t_ps[pb]], writes=[t_tmp[sl_]])
                    av = A4[:, 4 * tt:4 * tt + 4, g, :]
                    P.op("dve", lambda e, av=av, g=g: e.scalar_tensor_tensor(
                        out=av, in0=av, scalar=cols[:, C_LNG + g:C_LNG + g + 1], in1=Cg[:, g:g + 1, :].broadcast_to([128, 4, 128]),
                        op0=ALU.mult, op1=ALU.add), reads=[t_cols, t_Cg], writes=t_A[4 * tt:4 * tt + 4])
                    P.op("dve", lambda e, av=av, sl_=sl_: e.tensor_tensor(
                        out=av, in0=tmpb[sl_].rearrange("p (n q) -> p n q", n=4), in1=av, op=ALU.mult),
                        reads=[t_tmp[sl_]], writes=t_A[4 * tt:4 * tt + 4])

    def branch_a(plan, blk):
        for j in range(8):
            if plan:
                wp.declare(("ag", blk, j), wsrc(w_ag_d, j * 512, 512), KC, 512)
                continue
            if j == 0:
                fence(scr_toks)
            wv, wt = wp.next(("ag", blk, j))
            for oc in range(2):
                o = j * 2 + oc
                for tt in range(2):
                    pg, py = gbank(), gbank()
                    sl_ = (oc * 2 + tt) % 2

                    def mmg(e, wv=wv, oc=oc, tt=tt, pg=pg):
                        for kc in range(KC):
                            ins = e.matmul(ps[pg][:, :], lhsT=wv[:, kc, 256 + oc * 128:256 + (oc + 1) * 128],
                                           rhs=hT[:, kc, tt * 512:(tt + 1) * 512], start=(kc == 0), stop=(kc == KC - 1))
                        return ins
                    P.op("pe", mmg, reads=[wt] + [t_hT[kc][n] for kc in range(KC) for n in range(4 * tt, 4 * tt + 4)],
                         writes=[t_ps[pg]])
                    P.op("act", lambda e, pg=pg, sl_=sl_: e.activation(out=tmpf[sl_], in_=ps[pg][:, :], func=AF.Sigmoid),
                         reads=[t_ps[pg]], writes=[t_tmp[sl_]])

                    def mmy(e, wv=wv, oc=oc, tt=tt, py=py):
                        for g in range(16):
                            ins = e.matmul(ps[py][:, :], lhsT=wv[:, g, oc * 128:(oc + 1) * 128],
                                           rhs=A4[:, 4 * tt:4 * tt + 4, g, :], start=(g == 0), stop=(g == 15))
                        return ins
                    P.op("pe", mmy, reads=[wt] + t_A[4 * tt:4 * tt + 4], writes=[t_ps[py]])
                    P.op("dve", lambda e, o=o, tt=tt, py=py, sl_=sl_: e.tensor_tensor(
                        out=mergedT[:, o, tt * 512:(tt + 1) * 512], in0=ps[py][:, :], in1=tmpf[sl_], op=ALU.mult),
                        reads=[t_ps[py], t_tmp[sl_]], writes=[t_mg[o][tt]])

    SCALE = float(192 ** -0.5)
    PI = float(np.pi)

    def latents(plan, blk):
        T0 = blk * NT
        wblk = []
        wp.live = 2
        for j in range(2):
            if plan:
                wp.declare(("lat", blk, j), wsrc(w_lat_d, j * 512, 512), KC, 512)
            else:
                wblk.append(wp.next(("lat", blk, j)))
        if plan:
            wp.live = 1
            return
        fence(scr_toks)
        P.dma("sp", lambda e: e.dma_start(out=posi, in_=pos_d[:, T0:T0 + NT].partition_broadcast(128)), writes=[t_posi])
        P.op("dve", lambda e: e.tensor_copy(out=angf, in_=posi), reads=[t_posi], writes=[t_angf])
        P.op("dve", lambda e: e.tensor_scalar(out=angf, in0=angf, scalar1=cols[:, C_INV:C_INV + 1], scalar2=None, op0=ALU.mult),
             reads=[t_angf, t_cols], writes=[t_angf])
        HI = 6.28125
        LO = float(2 * np.pi - 6.28125)
        P.op("dve", lambda e: e.tensor_scalar(out=targ, in0=angf, scalar1=float(1 / (2 * np.pi)), scalar2=None, op0=ALU.mult),
             reads=[t_angf], writes=[t_targ])
        P.op("dve", lambda e: e.tensor_copy(out=posi, in_=targ), reads=[t_targ], writes=[t_posi])
        P.op("dve", lambda e: e.tensor_copy(out=targ, in_=posi), reads=[t_posi], writes=[t_targ])
        P.op("dve", lambda e: e.scalar_tensor_tensor(out=angf, in0=targ, scalar=-HI, in1=angf, op0=ALU.mult, op1=ALU.add),
             reads=[t_targ, t_angf], writes=[t_angf])
        P.op("dve", lambda e: e.scalar_tensor_tensor(out=angf, in0=targ, scalar=-LO, in1=angf, op0=ALU.mult, op1=ALU.add),
             reads=[t_targ, t_angf], writes=[t_angf])
        P.op("dve", lambda e: e.tensor_scalar(out=angf, in0=angf, scalar1=-PI, scalar2=PI, op0=ALU.max, op1=ALU.min),
             reads=[t_angf], writes=[t_angf])
        P.op("act", lambda e: e.activation(out=sin_s[:, :], in_=angf, func=AF.Sin, scale=cols[:, C_SGN:C_SGN + 1]),
             reads=[t_angf, t_cols], writes=[t_sin])
        P.op("dve", lambda e: e.tensor_scalar(out=targ, in0=angf, scalar1=PI / 2, scalar2=2 * PI, op0=ALU.is_gt, op1=ALU.mult),
             reads=[t_angf], writes=[t_targ])
        P.op("dve", lambda e: e.scalar_tensor_tensor(out=targ, in0=angf, scalar=PI / 2, in1=targ, op0=ALU.add, op1=ALU.subtract),
             reads=[t_angf, t_targ], writes=[t_targ])
        P.op("dve", lambda e: e.tensor_scalar(out=targ, in0=targ, scalar1=-PI, scalar2=PI, op0=ALU.max, op1=ALU.min),
             reads=[t_targ], writes=[t_targ])
        P.op("act", lambda e: e.activation(out=cos2[:, :], in_=targ, func=AF.Sin), reads=[t_targ], writes=[t_cos])
        (wq, wqt), (wk, wkt) = wblk
        lvl = {"lat0": 0, "lat1": 1, "lat2": 2}.get(stop_after, 3)
        for tt in range(2 if lvl > 0 else 0):
            hts = [t_hT[kc][n] for kc in range(KC) for n in range(4 * tt, 4 * tt + 4)]
            tsl = slice(tt * 512, (tt + 1) * 512)
            asl = slice(T0 + tt * 512, T0 + (tt + 1) * 512)
            at = (T0 // 512) + tt
            pq = 2
            for c in range(4):
                pb = gbank()

                def mm(e, c=c, pb=pb, tsl=tsl):
                    for kc in range(KC):
                        ins = e.matmul(ps[pb][:, :], lhsT=wq[:, kc, c * 128:(c + 1) * 128], rhs=hT[:, kc, tsl],
                                       start=(kc == 0), stop=(kc == KC - 1))
                    return ins
                P.op("pe", mm, reads=[wqt] + hts, writes=[t_ps[pb]])
                P.op("act", lambda e, c=c, pb=pb: e.activation(out=sqb[c % 2], in_=ps[pb][:, :], func=AF.Square),
                     reads=[t_ps[pb]], writes=[t_sqb[c % 2]])
                P.op("act", lambda e, c=c, pb=pb, tsl=tsl: e.activation(out=qgT[:, c, tsl], in_=ps[pb][:, :], func=AF.Copy,
                                                                        scale=cols[:, C_QG + c:C_QG + c + 1]),
                     reads=[t_ps[pb], t_cols], writes=[t_qg[c][tt]])
                P.op("pe", lambda e, c=c: e.matmul(ps[pq][:, :], lhsT=ones[:, :], rhs=sqb[c % 2], start=(c == 0), stop=(c == 3)),
                     reads=[t_ones, t_sqb[c % 2]], writes=[t_ps[pq]])
            P.op("act", lambda e, tsl=tsl: e.activation(out=tmpf[0], in_=ps[pq][:, :], func=AF.Sqrt, scale=1.0 / 512,
                                                        bias=epsc[:, 0:1]), reads=[t_ps[pq], t_epsc], writes=[t_tmp[0]])
            P.op("dve", lambda e: e.reciprocal(out=tmpf[0], in_=tmpf[0]), reads=[t_tmp[0]], writes=[t_tmp[0]])
            for c in range(4):
                P.op("dve", lambda e, c=c, tsl=tsl: e.tensor_tensor(out=qgT[:, c, tsl], in0=qgT[:, c, tsl], in1=tmpf[0], op=ALU.mult),
                     reads=[t_tmp[0]], writes=[t_qg[c][tt]])
            if lvl < 2:
                continue
            pk = 3
            for c in range(2):
                pb = gbank()

                def mm(e, c=c, pb=pb, tsl=tsl):
                    for kc in range(KC):
                        ins = e.matmul(ps[pb][:, :], lhsT=wk[:, kc, c * 128:(c + 1) * 128], rhs=hT[:, kc, tsl],
                                       start=(kc == 0), stop=(kc == KC - 1))
                    return ins
                P.op("pe", mm, reads=[wkt] + hts, writes=[t_ps[pb]])
                P.op("act", lambda e, c=c, pb=pb: e.activation(out=sqb[c], in_=ps[pb][:, :], func=AF.Square),
                     reads=[t_ps[pb]], writes=[t_sqb[c]])
                P.op("act", lambda e, c=c, pb=pb, asl=asl: e.activation(out=kvgT[:, c, asl], in_=ps[pb][:, :], func=AF.Copy,
                                                                        scale=cols[:, C_KVG + c:C_KVG + c + 1]),
                     reads=[t_ps[pb], t_cols], writes=[t_kvg[c][at]])
                P.op("pe", lambda e, c=c: e.matmul(ps[pk][:, :], lhsT=ones[:, :], rhs=sqb[c], start=(c == 0), stop=(c == 1)),
                     reads=[t_ones, t_sqb[c]], writes=[t_ps[pk]])
            P.op("act", lambda e: e.activation(out=tmpf[1], in_=ps[pk][:, :], func=AF.Sqrt, scale=1.0 / 256, bias=epsc[:, 0:1]),
                 reads=[t_ps[pk], t_epsc], writes=[t_tmp[1]])
            P.op("dve", lambda e: e.reciprocal(out=tmpf[1], in_=tmpf[1]), reads=[t_tmp[1]], writes=[t_tmp[1]])
            for c in range(2):
                P.op("dve", lambda e, c=c, asl=asl: e.tensor_tensor(out=kvgT[:, c, asl], in0=kvgT[:, c, asl], in1=tmpf[1], op=ALU.mult),
                     reads=[t_tmp[1]], writes=[t_kvg[c][at]])
            if lvl < 3:
                continue
            pr_, psw = gbank(), gbank()
            for pbx, c0 in ((pr_, 256), (psw, 384)):
                def mm(e, pbx=pbx, c0=c0, tsl=tsl):
                    for kc in range(KC):
                        ins = e.matmul(ps[pbx][:, :], lhsT=wk[:, kc, c0:c0 + 128], rhs=hT[:, kc, tsl],
                                       start=(kc == 0), stop=(kc == KC - 1))
                    return ins
                P.op("pe", mm, reads=[wkt] + hts, writes=[t_ps[pbx]])
            P.op("dve", lambda e, tsl=tsl, pr_=pr_: e.tensor_tensor(out=tmpf[0], in0=ps[pr_][:, :], in1=cos2[:, tsl], op=ALU.mult),
                 reads=[t_ps[pr_], t_cos], writes=[t_tmp[0]])
            P.op("dve", lambda e, tsl=tsl, psw=psw: e.tensor_tensor(out=tmpf[1], in0=ps[psw][:, :], in1=sin_s[:, tsl], op=ALU.mult),
                 reads=[t_ps[psw], t_sin], writes=[t_tmp[1]])
            P.op("dve", lambda e, asl=asl: e.tensor_tensor(out=kpe_dup[:, asl], in0=tmpf[0], in1=tmpf[1], op=ALU.add),
                 reads=t_tmp, writes=[t_kpe[at]])

    def attention(plan, blk):
        wp.live = 1
        T0 = blk * NT
        nk512 = (T0 + NT) // 512
        ADA2 = (12, 14, 16, 18, 20, 21, 22, 23, 24)
        for pr in range(8):
            if plan:
                wp.declare(("pair", blk, pr), w_pair_d[pr].rearrange("p (k o) -> p k o", k=1), 1, 3072)
                if blk == 0:
                    mod_blocks(True, ADA2[pr], ADA2[pr + 1])
                continue
            rot["lo"], rot["cnt"] = 5, 3
            if pr == 0:
                fence(scr_toks + t_A)
                P.op("pool", lambda e: e.memset(Qr[0][64:128, :], 0.0), writes=[t_Qr[0]])
                P.op("pool", lambda e: e.memset(Qr[1][0:64, :], 0.0), writes=[t_Qr[1]])
            wv_, wt = wp.next(("pair", blk, pr))
            w = wv_[:, 0, :]
            qn = w[:, 0:1024].rearrange("p (k o) -> p k o", k=4)
            qr = w[:, 1024:1536].rearrange("p (k o) -> p k o", k=4)
            qs = w[:, 1536:2048].rearrange("p (k o) -> p k o", k=4)
            kn = w[:, 2048:2560].rearrange("p (k o) -> p k o", k=2)
            vv = w[:, 2560:3072].rearrange("p (k o) -> p k o", k=2)
            qg_all = [t_qg[c][tt] for c in range(4) for tt in range(2)]
            for tt in range(2):
                tsl = slice(tt * 512, (tt + 1) * 512)
                for hd in range(2):
                    pb = pbank()

                    def mm(e, hd=hd, pb=pb, tsl=tsl, qn=qn):
                        for c in range(4):
                            ins = e.matmul(ps[pb][:, :], lhsT=qn[:, c, hd * 128:(hd + 1) * 128], rhs=qgT[:, c, tsl],
                                           start=(c == 0), stop=(c == 3))
                        return ins
                    P.op("pe", mm, reads=[wt] + qg_all, writes=[t_ps[pb]])
                    P.op("dve", lambda e, hd=hd, pb=pb, tsl=tsl: e.tensor_copy(out=QnT[hd][:, tsl], in_=ps[pb][:, :]),
                         reads=[t_ps[pb]], writes=[t_QnT[hd]])
                pr_, psw = pbank(), pbank()
                for pbx, wsel in ((pr_, qr), (psw, qs)):
                    def mm(e, pbx=pbx, wsel=wsel, tsl=tsl):
                        for c in range(4):
                            ins = e.matmul(ps[pbx][:, :], lhsT=wsel[:, c, :], rhs=qgT[:, c, tsl], start=(c == 0), stop=(c == 3))
                        return ins
                    P.op("pe", mm, reads=[wt] + qg_all, writes=[t_ps[pbx]])
                P.op("dve", lambda e, tsl=tsl, pr_=pr_: e.tensor_tensor(out=rt1, in0=ps[pr_][:, :], in1=cos2[:, tsl], op=ALU.mult),
                     reads=[t_ps[pr_], t_cos], writes=[t_rt1])
                P.op("dve", lambda e, tsl=tsl, psw=psw: e.tensor_tensor(out=rec, in0=ps[psw][:, :], in1=sin_s[:, tsl], op=ALU.mult),
                     reads=[t_ps[psw], t_sin], writes=[t_rec])
                P.op("dve", lambda e, tsl=tsl: e.tensor_tensor(out=Qr[0][0:64, tsl], in0=rt1[0:64, :], in1=rec[0:64, :], op=ALU.add),
                     reads=[t_rt1, t_rec], writes=[t_Qr[0]])
                P.op("dve", lambda e, tsl=tsl: e.tensor_tensor(out=Qr[1][64:128, tsl], in0=rt1[64:128, :], in1=rec[64:128, :], op=ALU.add),
                     reads=[t_rt1, t_rec], writes=[t_Qr[1]])
            for hd in range(2):
                KnT_h, Vh_h, t_KnT_h, t_Vh_h = KV[hd]
                for kt in range(nk512):
                    ksl = slice(kt * 512, (kt + 1) * 512)
                    pb = pbank()

                    def mm(e, hd=hd, pb=pb, ksl=ksl, kn=kn):
                        for c in range(2):
                            ins = e.matmul(ps[pb][:, :], lhsT=kn[:, c, hd * 128:(hd + 1) * 128], rhs=kvgT[:, c, ksl],
                                           start=(c == 0), stop=(c == 1))
                        return ins
                    P.op("pe", mm, reads=[wt, t_kvg[0][kt], t_kvg[1][kt]], writes=[t_ps[pb]])
                    P.op("dve", lambda e, pb=pb, ksl=ksl, KnT_h=KnT_h: e.tensor_copy(out=KnT_h[:, ksl], in_=ps[pb][:, :]),
                         reads=[t_ps[pb]], writes=[t_KnT_h[kt]])
            for kt in range(nk512):
                for half in range(2):
                    pb2 = pbank()

                    def mmv(e, pb2=pb2, kt=kt, half=half, vv=vv):
                        for i in range(2):
                            kb = kt * 4 + half * 2 + i
                            for c in range(2):
                                ins = e.matmul(ps[pb2][:, i * 256:(i + 1) * 256], lhsT=kvgT[:, c, kb * 128:(kb + 1) * 128],
                                               rhs=vv[:, c, :], start=(c == 0), stop=(c == 1))
                        return ins
                    P.op("pe", mmv, reads=[wt, t_kvg[0][kt], t_kvg[1][kt]], writes=[t_ps[pb2]])
                    for hd in range(2):
                        Vdst, t_Vdst = KV[hd][1], KV[hd][3]
                        kb0 = kt * 4 + half * 2
                        P.op("act", lambda e, pb2=pb2, hd=hd, kb0=kb0, Vdst=Vdst: e.activation(
                            out=Vdst[:, kb0:kb0 + 2, :],
                            in_=ps[pb2][:, :].rearrange("p (i h d) -> p i h d", i=2, h=2)[:, :, hd, :], func=AF.Copy),
                            reads=[t_ps[pb2]], writes=[t_Vdst])
            for hd in range(2):
                h = 2 * pr + hd
                KnT_h, Vh_h, t_KnT_h, t_Vh_h = KV[hd]
                LOOK = 2
                pend = []
                cnt = 0

                def finalize(tt, po, psm, h=h):
                    P.op("dve", lambda e: e.reciprocal(out=rec, in_=ps[psm][:, :]), reads=[t_ps[psm]], writes=[t_rec])
                    P.op("dve", lambda e: e.tensor_tensor(out=oT[:, h, tt * 512:(tt + 1) * 512], in0=ps[po][:, :], in1=rec,
                                                          op=ALU.mult), reads=[t_ps[po], t_rec], writes=[t_oT[h][tt]])

                def emit_pend(ent, t_Vh_h=t_Vh_h):
                    f, k_, po_, psm_, fin = ent
                    P.op("pe", f, reads=[t_Vh_h, t_pt[k_], t_ones], writes=[t_ps[po_], t_ps[psm_]])
                    if fin is not None:
                        finalize(fin, po_, psm_)

                for tt in range(2):
                    qa = (T0 + 512 * tt) // 128
                    nkb = qa + 4
                    po, psm = ((2, 3), (1, 4))[tt]
                    for kb in range(nkb):
                        i = kb - qa
                        c0 = 128 * i if i > 0 else 0
                        qsl = slice(tt * 512 + c0, (tt + 1) * 512)
                        pb = gbank()
                        sl_ = cnt % 4
                        cnt += 1

                        def mms(e, hd=hd, kb=kb, i=i, c0=c0, qsl=qsl, pb=pb, KnT_h=KnT_h):
                            e.matmul(ps[pb][:, c0:512], lhsT=KnT_h[:, kb * 128:(kb + 1) * 128], rhs=QnT[hd][:, qsl],
                                     start=True, stop=False)
                            ins = e.matmul(ps[pb][:, c0:512], lhsT=kpe_dup[:, kb * 128:(kb + 1) * 128], rhs=Qr[hd][:, qsl],
                                           start=False, stop=(i < 0))
                            if i >= 0:
                                ins = e.matmul(ps[pb][:, c0:c0 + 128], lhsT=ident[:, :], rhs=maskneg[:, :], start=False, stop=True)
                            return ins
                        P.op("pe", mms, reads=[t_KnT_h[kb // 4], t_QnT[hd], t_Qr[hd], t_kpe[kb // 4], t_ident, t_mask],
                             writes=[t_ps[pb]])
                        P.op("act", lambda e, pb=pb, c0=c0, sl_=sl_: e.activation(out=ptile[sl_][:, c0:512], in_=ps[pb][:, c0:512],
                                                                                   func=AF.Exp, scale=SCALE),
                             reads=[t_ps[pb]], writes=[t_pt[sl_]])

                        def mmo(e, kb=kb, c0=c0, sl_=sl_, nkb=nkb, po=po, psm=psm, Vh_h=Vh_h):
                            e.matmul(ps[po][:, c0:512], lhsT=Vh_h[:, kb, :], rhs=ptile[sl_][:, c0:512], start=(kb == 0),
                                     stop=(kb == nkb - 1))
                            return e.matmul(ps[psm][:, c0:512], lhsT=ones[:, :], rhs=ptile[sl_][:, c0:512], start=(kb == 0),
                                            stop=(kb == nkb - 1))
                        pend.append((mmo, sl_, po, psm, tt if kb == nkb - 1 else None))
                        if len(pend) > LOOK:
                            emit_pend(pend.pop(0))
                for ent in pend:
                    emit_pend(ent)
            if blk == 0:
                mod_blocks(False, ADA2[pr], ADA2[pr + 1])
            rot["lo"], rot["cnt"] = 4, 4

    def branch_b(plan, blk):
        for j in range(8):
            if plan:
                wp.declare(("bg", blk, j), wsrc(w_bg_d, j * 512, 512), KC, 512)
                continue
            if j == 0:
                fence(scr_toks)
            wv, wt = wp.next(("bg", blk, j))
            for oc in range(2):
                o = j * 2 + oc
                for tt in range(2):
                    pg, py = gbank(), gbank()
                    sl_ = (oc * 2 + tt) % 2

                    def mmg(e, wv=wv, oc=oc, tt=tt, pg=pg):
                        for kc in range(KC):
                            ins = e.matmul(ps[pg][:, :], lhsT=wv[:, kc, 256 + oc * 128:256 + (oc + 1) * 128],
                                           rhs=hT[:, kc, tt * 512:(tt + 1) * 512], start=(kc == 0), stop=(kc == KC - 1))
                        return ins
                    P.op("pe", mmg, reads=[wt] + [t_hT[kc][n] for kc in range(KC) for n in range(4 * tt, 4 * tt + 4)],
                         writes=[t_ps[pg]])
                    P.op("act", lambda e, pg=pg, sl_=sl_: e.activation(out=tmpf[sl_], in_=ps[pg][:, :], func=AF.Sigmoid),
                         reads=[t_ps[pg]], writes=[t_tmp[sl_]])

                    def mmy(e, wv=wv, oc=oc, tt=tt, py=py):
                        for g in range(16):
                            ins = e.matmul(ps[py][:, :], lhsT=wv[:, g, oc * 128:(oc + 1) * 128],
                                           rhs=oT[:, g, tt * 512:(tt + 1) * 512], start=(g == 0), stop=(g == 15))
                        return ins
                    P.op("pe", mmy, reads=[wt] + [t_oT[h][tt] for h in range(16)], writes=[t_ps[py]])
                    P.op("dve", lambda e, py=py, sl_=sl_: e.tensor_tensor(out=tmpf[sl_], in0=ps[py][:, :], in1=tmpf[sl_], op=ALU.mult),
                         reads=[t_ps[py], t_tmp[sl_]], writes=[t_tmp[sl_]])
                    P.op("dve", lambda e, o=o, tt=tt, sl_=sl_: e.tensor_tensor(
                        out=mergedT[:, o, tt * 512:(tt + 1) * 512], in0=mergedT[:, o, tt * 512:(tt + 1) * 512], in1=tmpf[sl_],
                        op=ALU.add), reads=[t_tmp[sl_], t_mg[o][tt]], writes=[t_mg[o][tt]])

    def wout_phase(plan, blk):
        for j in range(4):
            if plan:
                wp.declare(("wo", blk, j), wsrc(w_out_d, j * 512, 512), KC, 512)
                continue
            if j == 0:
                fence(scr_toks + big0_toks)
            wv, wt = wp.next(("wo", blk, j))
            for oc in range(4):
                o = j * 4 + oc
                for tt in range(2):
                    pb = gbank()

                    def mm(e, wv=wv, oc=oc, tt=tt, pb=pb):
                        for kc in range(KC):
                            ins = e.matmul(ps[pb][:, :], lhsT=wv[:, kc, oc * 128:(oc + 1) * 128],
                                           rhs=mergedT[:, kc, tt * 512:(tt + 1) * 512], start=(kc == 0), stop=(kc == KC - 1))
                        return ins
                    P.op("pe", mm, reads=[wt] + [t_mg[kc][tt] for kc in range(KC)], writes=[t_ps[pb]])
                    P.op("act", lambda e, o=o, tt=tt, pb=pb: e.activation(out=yT[:, o, tt * 512:(tt + 1) * 512], in_=ps[pb][:, :],
                                                                           func=AF.Copy), reads=[t_ps[pb]], writes=[t_yT[o][tt]])
                epi_chunk_stats(0, o, part="sq")
                if o >= 1:
                    epi_chunk_stats(0, o - 1, part="mmc")
        if not plan:
            epi_chunk_stats(0, KC - 1, part="mmc")

    def epi_chunk_stats(sl, o, part="all"):
        psc = ps[3]
        nt = NT // 128
        k = o % 2
        if part in ("all", "sq"):
            P.op("act", lambda e: e.activation(out=sqw[k], in_=yT[:, o, :], func=AF.Square),
                 reads=[t_yT[o][0], t_yT[o][1]], writes=[t_sqe[k]])

        def mmc(e):
            for n in range(nt):
                ins = e.matmul(psc[:, n:n + 1], lhsT=sqw[k][:, n * 128:(n + 1) * 128], rhs=ones[:, 0:1],
                               start=(o == 0 and n == 0), stop=(o == KC - 1 and n == nt - 1))
            return ins
        if part in ("all", "mmc"):
            P.op("pe", mmc, reads=[t_ones, t_sqe[k]], writes=[t_ps[3]])
        if part == "mmc":
            return
        if part == "sq":
            pass
        P.op("dve", lambda e: e.tensor_scalar(out=yT[:, o, :], in0=yT[:, o, :], scalar1=prm[:, 3 * sl + 2, o:o + 1],
                                              scalar2=None, op0=ALU.mult),
             reads=[t_prm, t_sqe[k]], writes=[t_yT[o][0], t_yT[o][1]])

    def epilogue(plan, T0, sl, res_d, res_toks, stats_done=False):
        if plan:
            return
        psc = ps[3]
        nt = NT // 128
        if not stats_done:
            fence(scr_toks)
            for o in range(KC):
                epi_chunk_stats(sl, o)
        rcol = stat[:, 24:24 + nt]
        P.op("act", lambda e: e.activation(out=rcol, in_=psc[:, 0:nt], func=AF.Sqrt, scale=1.0 / D, bias=epsc[:, 0:1]),
             reads=[t_ps[3], t_epsc], writes=[t_stat[24]])
        P.op("dve", lambda e: e.reciprocal(out=rcol, in_=rcol), reads=[t_stat[24]], writes=[t_stat[24]])
        hb_ = 0

        def load(n):
            b, r0 = n % 2, T0 + n * 128
            P.dma("sp", lambda e: e.dma_start(out=xs[b], in_=res_d[r0:r0 + 128, :]),
                  reads=([res_toks[r0 // 128]] if res_toks else []), writes=[t_xs[b]])

        load(0)
        load(1)
        for n in range(nt):
            b = n % 2
            r0 = T0 + n * 128
            tt = n // 4
            for q4 in range(4):
                pb = 4 + hb_ % 4
                hb_ += 1

                def tr(e, n=n, q4=q4, pb=pb):
                    for k in range(4):
                        o = q4 * 4 + k
                        ins = e.transpose(out=ps[pb][:, k * 128:(k + 1) * 128], in_=yT[:, o, n * 128:(n + 1) * 128],
                                          identity=identf[:, :])
                    return ins
                P.op("pe", tr, reads=[t_yT[q4 * 4 + k][tt] for k in range(4)] + [t_ident], writes=[t_ps[pb]])
                P.op("dve", lambda e, b=b, q4=q4, pb=pb, n=n: e.scalar_tensor_tensor(
                    out=xs[b][:, q4 * 512:(q4 + 1) * 512], in0=ps[pb][:, :], scalar=rcol[:, n:n + 1],
                    in1=xs[b][:, q4 * 512:(q4 + 1) * 512], op0=ALU.mult, op1=ALU.add),
                    reads=[t_ps[pb], t_stat[24]], writes=[t_xs[b]])
            P.dma("sp", lambda e, b=b, r0=r0: e.dma_start(out=out_d[r0:r0 + 128, :], in_=xs[b]),
                  reads=[t_xs[b]], writes=[d_x1[r0 // 128]], sem_tok=t_st[b])
            if n + 2 < nt:
                load(n + 2)

    def ffn_up(plan):
        for j in range(FC // 2):
            if plan:
                wp.declare(("up", j), wsrc(w_up2_d, j * 512, 512), KC, 512)
                continue
            if j == 0:
                fence(scr_toks)
            wv, wt = wp.next(("up", j))
            for f2 in range(2):
                fc = 2 * j + f2
                for tt in range(4):
                    tsl = slice(tt * 512, (tt + 1) * 512)
                    hts = [t_h2T[kc][n] for kc in range(KC) for n in range(4 * tt, 4 * tt + 4)]
                    for gvsel in range(2):
                        pb = gbank()
                        ch = fc + FC * gvsel
                        c0 = f2 * 256 + gvsel * 128

                        def mm(e, wv=wv, c0=c0, tsl=tsl, pb=pb):
                            for kc in range(KC):
                                ins = e.matmul(ps[pb][:, :], lhsT=wv[:, kc, c0:c0 + 128], rhs=h2T[:, kc, tsl],
                                               start=(kc == 0), stop=(kc == KC - 1))
                            return ins
                        P.op("pe", mm, reads=[wt] + hts, writes=[t_ps[pb]])
                        hb, yb = hbuf[gvsel], ybuf[gvsel]
                        if tt == 0:
                            P.op("dve", lambda e, hb=hb: e.memset(hb[:, 0:2], 0.0), writes=[t_hb[gvsel]])
                        P.op("act", lambda e, hb=hb, pb=pb: e.activation(out=hb[:, 2:514], in_=ps[pb][:, :], func=AF.Copy),
                             reads=[t_ps[pb]], writes=[t_hb[gvsel]])
                        P.op("act", lambda e, yb=yb, pb=pb, ch=ch: e.activation(
                            out=yb, in_=ps[pb][:, :], func=AF.Identity, scale=cols[:, C_CW + 176 + ch:C_CW + 176 + ch + 1],
                            bias=cols[:, C_CB + ch:C_CB + ch + 1]), reads=[t_ps[pb], t_cols], writes=[t_yb[gvsel]])
                        for tap, off in ((1, 1), (0, 0)):
                            P.op("dve", lambda e, hb=hb, yb=yb, ch=ch, tap=tap, off=off: e.scalar_tensor_tensor(
                                out=yb, in0=hb[:, off:off + 512], scalar=cols[:, C_CW + 88 * tap + ch:C_CW + 88 * tap + ch + 1],
                                in1=yb, op0=ALU.mult, op1=ALU.add), reads=[t_hb[gvsel], t_cols], writes=[t_yb[gvsel]])
                        P.op("dve", lambda e, hb=hb: e.tensor_copy(out=hb[:, 0:2], in_=hb[:, 512:514]),
                             reads=[t_yb[gvsel]], writes=[t_hb[gvsel]])
                    k = tt % 2
                    P.op("act", lambda e: e.activation(out=sgb, in_=ybuf[0], func=AF.Silu), reads=[t_yb[0]], writes=[t_sg])
                    P.op("dve", lambda e, k=k: e.tensor_tensor(out=gvs[k], in0=sgb, in1=ybuf[1], op=ALU.mult),
                         reads=[t_sg, t_yb[1]], writes=[t_gvs[k]])
                    P.dma("sp", lambda e, k=k, fc=fc, tsl=tsl: e.dma_start(out=gv_d[fc, :, tsl], in_=gvs[k]),
                          reads=[t_gvs[k]], writes=[d_gv[fc][tt]], sem_tok=t_gvst[k])

    def ffn_down(plan):
        NG = FC // 4
        gi = 0
        for th in range(2):
            for oq in range(4):
                for g in range(NG):
                    if plan:
                        wp.declare(("dn", th, oq, g), w_down_d[g * 512:(g + 1) * 512, oq * 512:(oq + 1) * 512]
                                   .rearrange("(k p) o -> p k o", p=128), 4, 512)
                        continue
                    if th == 0 and oq == 0 and g == 0:
                        fence(scr_toks + big0_toks + mg_toks)
                    wv, wt = wp.next(("dn", th, oq, g))
                    sl_ = gi % 3
                    gi += 1
                    P.dma("sp", lambda e, sl_=sl_, g=g, th=th: e.dma_start(
                        out=gslot[sl_], in_=gv_d[4 * g:4 * g + 4, :, th * 1024:(th + 1) * 1024].rearrange("f p t -> p f t")),
                        reads=[d_gv[4 * g + f][2 * th + t2] for f in range(4) for t2 in range(2)], writes=[t_gslot[sl_]])

                    def mm(e, wv=wv, sl_=sl_, g=g):
                        for f in range(4):
                            for oc in range(4):
                                for t2 in range(2):
                                    ins = e.matmul(ps[oc * 2 + t2][:, :], lhsT=wv[:, f, oc * 128:(oc + 1) * 128],
                                                   rhs=gslot[sl_][:, f, t2 * 512:(t2 + 1) * 512],
                                                   start=(g == 0 and f == 0), stop=(g == NG - 1 and f == 3))
                        return ins
                    P.op("pe", mm, reads=[wt, t_gslot[sl_]], writes=t_ps)
                if plan:
                    continue
                for oc in range(4):
                    for t2 in range(2):
                        o = oq * 4 + oc
                        if t2 == 0:
                            P.op("act", lambda e, o=o, oc=oc, t2=t2: e.activation(out=yT[:, o, t2 * 512:(t2 + 1) * 512],
                                                                                   in_=ps[oc * 2 + t2][:, :], func=AF.Copy),
                                 reads=[t_ps[oc * 2 + t2]], writes=[t_yT[o][t2]])
                        else:
                            P.op("dve", lambda e, o=o, oc=oc, t2=t2: e.tensor_copy(out=yT[:, o, t2 * 512:(t2 + 1) * 512],
                                                                                    in_=ps[oc * 2 + t2][:, :]),
                                 reads=[t_ps[oc * 2 + t2]], writes=[t_yT[o][t2]])
            epilogue(plan, th * 1024, 1, out_d, d_x1)

    def all_phases(plan):
        consts(plan)
        phase_mod(plan, 0)
        for blk in range(NB if stop_after == "all" else 1):
            if not plan:
                fence(big0_toks + mg_toks)
            prologue(plan, x_d, blk * NT, NT // 128, 0, hT, t_hT)
            if not plan:
                dump("hT", hT, all_hT)
            if stop_after == "p1":
                return
            gmlp_setup(plan)
            vphase(plan, blk)
            mixing(plan)
            uphase(plan, blk)
            if not plan:
                dump("aT", big0[:, 16384:32768], t_A)
            if stop_after == "gmlp":
                return
            if blk == 0:
                phase_mod(plan, 1)
            branch_a(plan, blk)
            if not plan:
                dump("mgA", mergedT[:, :, :], [t for r in t_mg for t in r])
            if stop_after == "brA":
                return
            latents(plan, blk)
            if not plan:
                dump("kpe", kpe_dup[:, :], t_kpe)
                dump("kvg", kvgT[:, :, :], [t for r in t_kvg for t in r])
                dump("qg", qgT[:, :, :], [t for r in t_qg for t in r])
            if not plan:
                dump("cos", cos2[:, :], [t_cos])
                dump("sin", sin_s[:, :], [t_sin])
            if stop_after in ("lat", "lat0", "lat1", "lat2"):
                return
            attention(plan, blk)
            if blk == 0:
                phase_mod(plan, 2, stream=False)
            if not plan:
                dump("oT", big0[:, 16384:32768], all_oT)
            if stop_after == "att":
                return
            branch_b(plan, blk)
            if not plan:
                dump("mg", mergedT[:, :, :], [t for r in t_mg for t in r])
            if stop_after == "brB":
                return
            wout_phase(plan, blk)
            epilogue(plan, blk * NT, 0, x_d, None, stats_done=True)
            if stop_after == "x1":
                break
        if stop_after == "x1":
            return
        if not plan:
            fence(big0_toks + mg_toks)
        prologue(plan, out_d, 0, S // 128, 1, h2T, t_h2T, src_toks=d_x1)
        ffn_up(plan)
        ffn_down(plan)

    all_phases(True)
    all_phases(False)

    done_rows = {"all": S, "x1": NT}.get(stop_after, 0)
    fence(scr_toks)
    t_o = P.dtok("o")
    for i in range(done_rows // 128, S // 128):
        b = i % 2
        P.dma("sp", lambda e, b=b, i=i: e.dma_start(out=xs[b], in_=x_d[i * 128:(i + 1) * 128, :]), writes=[t_xs[b]])
        P.dma("sp", lambda e, b=b, i=i: e.dma_start(out=out_d[i * 128:(i + 1) * 128, :], in_=xs[b]),
              reads=[t_xs[b]], writes=[d_x1[i]], sem_tok=t_o)
    d_out.extend(d_x1)
    P.wait_all("sp", d_out)
    P.emit()
    return nc


def _col(v):
    v = np.asarray(v, dtype=np.float32).reshape(-1, 128)
    return np.ascontiguousarray(v.T)


def make_in_maps(inp, cores):
    f32 = lambda k: np.ascontiguousarray(inp[k], dtype=np.float32)
    w_in = f32("w_in")
    kpe = w_in[:, O_KPE:O_KPE + 64]
    swp = np.concatenate([kpe[:, 32:64], kpe[:, 0:32]], axis=1)
    w_lat = np.ascontiguousarray(np.concatenate([w_in[:, O_QL:O_QL + 768], kpe, kpe, swp, swp], axis=1))
    w_uq, w_ukv = f32("w_uq"), f32("w_ukv")
    w_pair = np.zeros((8, 128, 3072), np.float32)
    chunked = lambda a: a.reshape(-1, 128, a.shape[1]).transpose(1, 0, 2)
    for pr in range(8):
        parts = []
        hs = (2 * pr, 2 * pr + 1)
        parts.append(np.concatenate([w_uq[:, h * 192:h * 192 + 128] for h in hs], axis=1))
        parts.append(np.concatenate([w_uq[:, h * 192 + 128:h * 192 + 192] for h in hs], axis=1))
        parts.append(np.concatenate([np.concatenate([w_uq[:, h * 192 + 160:h * 192 + 192],
                                                     w_uq[:, h * 192 + 128:h * 192 + 160]], axis=1) for h in hs], axis=1))
        parts.append(np.concatenate([w_ukv[:, h * 256:h * 256 + 128] for h in hs], axis=1))
        parts.append(np.concatenate([w_ukv[:, h * 256 + 128:h * 256 + 256] for h in hs], axis=1))
        w_pair[pr] = np.concatenate([chunked(a).reshape(128, -1) for a in parts], axis=1)
    def fuse(wb, g0):
        return np.ascontiguousarray(np.concatenate(
            [np.concatenate([wb[:, j * 256:(j + 1) * 256], w_in[:, g0 + j * 256:g0 + (j + 1) * 256]], axis=1) for j in range(8)], axis=1))
    shared = {"w_ada": f32("w_ada"), "w_in": w_in, "w_ag": fuse(f32("w_branch_a"), O_GA), "w_bg": fuse(f32("w_branch_b"), O_GB),
              "w_lat": w_lat, "w_pair": w_pair, "w_out": f32("w_out"), "w_down": f32("w_down"),
              "w_up2": np.ascontiguousarray(np.asarray(inp["w_up"], np.float32).reshape(D, 2, FC, 128).transpose(0, 2, 1, 3)
                                            .reshape(D, 2 * D_FF)),
              "wsT": np.ascontiguousarray(np.transpose(np.asarray(inp["gm_w_s"], np.float32), (2, 0, 1))),
              "bs": np.ascontiguousarray(np.asarray(inp["gm_b_s"], np.float32).reshape(1, 2048))}
    maps = []
    for b in cores:
        cols = np.zeros((128, NCOL), np.float32)
        cols[:, C_BADA:C_BADA + 96] = _col(inp["b_ada"])
        cols[:, C_G1:C_G1 + 16] = _col(inp["pre_norm1_g"])
        cols[:, C_G2:C_G2 + 16] = _col(inp["pre_norm2_g"])
        cols[:, C_GP1:C_GP1 + 16] = _col(inp["post_norm1_g"])
        cols[:, C_GP2:C_GP2 + 16] = _col(inp["post_norm2_g"])
        cols[:, C_LNG:C_LNG + 16] = _col(inp["gm_ln_g"])
        cols[:, C_LNB:C_LNB + 16] = _col(inp["gm_ln_b"])
        cols[:, C_QG:C_QG + 4] = _col(inp["q_norm_g"])
        cols[:, C_KVG:C_KVG + 2] = _col(inp["kv_norm_g"])
        cols[:, C_C:C_C + 16] = _col(inp["c"][b])
        for k in range(3):
            cols[:, C_CW + 88 * k:C_CW + 88 * (k + 1)] = _col(inp["conv_w"][k])
        cols[:, C_CB:C_CB + 88] = _col(inp["conv_b"])
        pidx = np.arange(128)
        cols[:, C_INV] = (10000.0 ** (-(2.0 * (pidx % 32)) / 64.0)).astype(np.float32)
        sgn = np.where((pidx % 64) < 32, -1.0, 1.0).astype(np.float32)
        cols[:, C_SGN] = sgn
        cols[:, C_NPI] = np.float32(-np.pi)
        cols[:, C_NPS] = (np.float32(-np.pi) * sgn).astype(np.float32)
        m = dict(shared)
        m["x"] = np.ascontiguousarray(inp["x"][b], dtype=np.float32)
        m["pos"] = np.ascontiguousarray(inp["positions"][b], dtype=np.int32).reshape(1, S)
        m["cols"] = cols
        maps.append(m)
    return maps


def kernel(**inputs):
    nc = build()
    maps = make_in_maps(inputs, list(range(8)))
    res = run_bass_kernel_spmd(nc, maps, core_ids=list(range(8)))
    return np.stack([np.asarray(r["out"], dtype=np.float32) for r in res.results], axis=0)
```

```python
import numpy as np
import ml_dtypes
import concourse.bass as bass
import concourse.mybir as mybir
from concourse.bass_utils import run_bass_kernel_spmd

F32 = mybir.dt.float32
BF16 = mybir.dt.bfloat16
I32 = mybir.dt.int32
AF = mybir.ActivationFunctionType
ALU = mybir.AluOpType
AX = mybir.AxisListType

D = 2048
S = 2048
KC = D // 128
NB = 2
NT = S // NB
EPS = 1e-6
D_FF = 5632
FC = D_FF // 128
N_HEADS = 16

C_BADA, C_G1, C_G2, C_GP1, C_GP2, C_LNG, C_LNB, C_QG, C_KVG, C_C, C_CW, C_CB = (
    0, 96, 112, 128, 144, 160, 176, 192, 196, 198, 214, 478)
C_INV, C_SGN, C_NPI, C_NPS = 566, 567, 568, 569
NCOL = 570


class Ev:
    __slots__ = ("sem", "val")

    def __init__(self, sem, val):
        self.sem = sem
        self.val = val


class Tok:
    __slots__ = ("w", "r", "dsem", "dcnt", "name", "excl")

    def __init__(self, name=""):
        self.excl = False
        self.w = None
        self.r = {}
        self.dsem = None
        self.dcnt = 0
        self.name = name


class Prog:
    ENG = ("pe", "act", "dve", "pool", "sp")

    def __init__(self, nc):
        self.nc = nc
        self.ops = {e: [] for e in self.ENG}
        self.sems = {e: nc.alloc_semaphore("s_" + e) for e in self.ENG}
        self.cnt = {e: 0 for e in self.ENG}
        self.waited = {e: {} for e in self.ENG}
        self.semobj = {self.sems[e].num: self.sems[e] for e in self.ENG}
        self.ntok = 0

    def tok(self, name=""):
        return Tok(name)

    def toks(self, n):
        return [Tok() for _ in range(n)]

    def dtok(self, name=""):
        t = Tok(name)
        self.ntok += 1
        t.dsem = self.nc.alloc_semaphore("d%d_%s" % (self.ntok, name))
        self.semobj[t.dsem.num] = t.dsem
        return t

    def _deps(self, eng, reads, writes, excl_own=None):
        need = {}
        for t in reads:
            if t.w is not None:
                need[t.w.sem] = max(need.get(t.w.sem, 0), t.w.val)
            if t.excl:
                for s, v in t.r.items():
                    if s != excl_own:
                        need[s] = max(need.get(s, 0), v)
        for t in writes:
            if t.w is not None:
                need[t.w.sem] = max(need.get(t.w.sem, 0), t.w.val)
            for s, v in t.r.items():
                need[s] = max(need.get(s, 0), v)
        w = self.waited[eng]
        waits = []
        if eng == "pe":
            need.pop(self.sems["pe"].num, None)
        for s, v in need.items():
            if w.get(s, 0) < v:
                waits.append((s, v))
                w[s] = v
        return waits

    def _mark(self, ev, reads, writes):
        for t in reads:
            t.r[ev.sem] = max(t.r.get(ev.sem, 0), ev.val)
        for t in writes:
            t.w = ev
            t.r = {}

    def op(self, eng, fn, reads=(), writes=()):
        waits = self._deps(eng, reads, writes, excl_own=self.sems[eng].num)
        self.cnt[eng] += 1
        ev = Ev(self.sems[eng].num, self.cnt[eng])
        self.ops[eng].append((waits, fn, (self.sems[eng].num, 1)))
        self._mark(ev, reads, writes)
        return ev

    def dma(self, eng, fn, reads=(), writes=(), sem_tok=None):
        st = sem_tok if sem_tok is not None else writes[0]
        assert st.dsem is not None
        waits = self._deps(eng, reads, writes)
        st.dcnt += 16
        ev = Ev(st.dsem.num, st.dcnt)
        self.ops[eng].append((waits, fn, (st.dsem.num, 16)))
        self._mark(ev, reads, writes)
        return ev

    def wait_all(self, eng, toks):
        waits = self._deps(eng, [], toks)
        self.ops[eng].append((waits, None, None))

    def emit(self):
        nc, ops, semobj = self.nc, self.ops, self.semobj

        def run(e, lst):
            for waits, fn, inc in lst:
                for s, v in waits:
                    e.wait_ge(semobj[s], v)
                if fn is None:
                    continue
                ins = fn(e)
                if inc is not None:
                    ins.then_inc(semobj[inc[0]], inc[1])

        with nc.Block() as block:
            @block.tensor
            def _(e):
                run(e, ops["pe"])

            @block.scalar
            def _(e):
                run(e, ops["act"])

            @block.vector
            def _(e):
                run(e, ops["dve"])

            @block.gpsimd
            def _(e):
                run(e, ops["pool"])

            @block.sync
            def _(e):
                run(e, ops["sp"])


class WPool:
    def __init__(self, P, nc, nslots, elems):
        self.P = P
        self.slots = [nc.alloc_sbuf_tensor("wslot%d" % i, [128, elems], BF16) for i in range(nslots)]
        self.toks = [P.dtok("w%d" % i) for i in range(nslots)]
        self.plan = []
        self.issued = 0
        self.cur = 0
        self.live = 1

    def declare(self, tag, src, kc, ncols):
        self.plan.append((tag, src, kc, ncols))

    def _issue(self, i):
        tag, src, kc, ncols = self.plan[i]
        s = i % len(self.slots)
        dst = self.slots[s][:, 0:kc * ncols].rearrange("p (k o) -> p k o", k=kc)
        self.P.dma("pool", lambda e, dst=dst, src=src: e.dma_start(out=dst, in_=src), writes=[self.toks[s]])

    def next(self, tag):
        i = self.cur
        assert self.plan[i][0] == tag, (self.plan[i][0], tag)
        while self.issued < min(len(self.plan), i + 1 + len(self.slots) - self.live):
            self._issue(self.issued)
            self.issued += 1
        self.cur += 1
        _, _, kc, ncols = self.plan[i]
        s = i % len(self.slots)
        return self.slots[s][:, 0:kc * ncols].rearrange("p (k o) -> p k o", k=kc), self.toks[s]


O_U, O_V, O_QL, O_KVL, O_KPE, O_GA, O_GB = 0, 2048, 4096, 4608, 4864, 4928, 6976


def build(stop_after="all", dbg=()):
    nc = bass.Bass("TRN2", target_bir_lowering=False)
    P = Prog(nc)
    dt = lambda name, shape, ty, kind: nc.dram_tensor(name, shape, ty, kind=kind).ap()
    x_d = dt("x", [S, D], F32, "ExternalInput")
    pos_d = dt("pos", [1, S], I32, "ExternalInput")
    cols_d = dt("cols", [128, NCOL], F32, "ExternalInput")
    w_ada_d = dt("w_ada", [D, 6 * D], F32, "ExternalInput")
    w_in_d = dt("w_in", [D, 9024], F32, "ExternalInput")
    wsT_d = dt("wsT", [128, 16, 128], F32, "ExternalInput")
    bs_d = dt("bs", [1, 2048], F32, "ExternalInput")
    w_ag_d = dt("w_ag", [D, 2 * D], F32, "ExternalInput")
    w_bg_d = dt("w_bg", [D, 2 * D], F32, "ExternalInput")
    w_lat_d = dt("w_lat", [D, 1024], F32, "ExternalInput")
    w_pair_d = dt("w_pair", [8, 128, 3072], F32, "ExternalInput")
    w_out_d = dt("w_out", [D, D], F32, "ExternalInput")
    w_up2_d = dt("w_up2", [D, 2 * D_FF], F32, "ExternalInput")
    w_down_d = dt("w_down", [D_FF, D], F32, "ExternalInput")
    out_d = dt("out", [S, D], F32, "ExternalOutput")
    gv_d = nc.dram_tensor("gv_scr", [FC, 128, S], BF16).ap()
    dbg_d = {}
    for name, shape, ty in dbg:
        dbg_d[name] = dt("dbg_" + name, shape, ty, "ExternalOutput")

    sb = lambda name, shape, ty: nc.alloc_sbuf_tensor("sb_" + name, shape, ty)
    cols = sb("cols", [128, NCOL], F32)
    modc = sb("modc", [128, 96], F32)
    prm = sb("prm", [128, 6, KC], F32)
    scb = sb("scb", [128, KC], BF16)
    ident = sb("ident", [128, 128], BF16)
    identf = sb("identf", [128, 128], F32)
    ones = sb("ones", [128, 128], BF16)
    stat = sb("stat", [128, 64], F32)
    epsc = sb("epsc", [128, 1], F32)
    fdummy = sb("fdummy", [128, 2], F32)
    big0 = sb("big0", [128, 32768], BF16)
    hT = big0[:, 0:16384].rearrange("p (k t) -> p k t", k=KC)
    A3 = big0[:, 16384:32768].rearrange("p (n c) -> p n c", n=8)
    A4 = big0[:, 16384:32768].rearrange("p (n g q) -> p n g q", n=8, g=16)
    mergedT = sb("mergedT", [128, KC, NT], BF16)
    scr = sb("scr", [128, 12288], BF16)
    xs = [scr[:, 0:4096].bitcast(F32), scr[:, 4096:8192].bitcast(F32)]
    xn = [scr[:, 8192:10240], scr[:, 10240:12288]]
    WmT = scr[:, 0:2048].rearrange("p (g q) -> p g q", g=16)
    bsb = scr[:, 2048:6144].bitcast(F32).rearrange("p (g q) -> p g q", g=16)
    Cg = scr[:, 6144:10240].bitcast(F32).rearrange("p (g q) -> p g q", g=16)
    tmpb = [scr[:, 10240:10752], scr[:, 11264:11776]]
    tmpf = [scr[:, 10240:11264].bitcast(F32), scr[:, 11264:12288].bitcast(F32)]
    junkv = scr[:, 10240:12288]
    oT = big0[:, 16384:32768].rearrange("p (h t) -> p h t", h=16)
    qgT = sb("qgT", [128, 4, NT], BF16)
    kvgT = sb("kvgT", [128, 2, S], BF16)
    kpe_dup = sb("kpe_dup", [128, S], BF16)
    cos2 = sb("cos2", [128, NT], BF16)
    sin_s = sb("sin_s", [128, NT], BF16)
    maskneg = sb("maskneg", [128, 128], BF16)
    KnT_b = sb("KnT_b", [128, S], BF16)
    Vh_b = sb("Vh_b", [128, 16, 128], BF16)
    masktmp = sb("masktmp", [128, 128], F32)
    sqb = [scr[:, 0:512], scr[:, 512:1024]]
    posi = scr[:, 2048:4096].bitcast(I32)
    angf = scr[:, 4096:6144].bitcast(F32)
    targ = scr[:, 6144:8192].bitcast(F32)
    QnT = [scr[:, 0:1024], scr[:, 1024:2048]]
    Qr = [scr[:, 2048:3072], scr[:, 3072:4096]]
    KnT = scr[:, 4096:6144]
    Vh = scr[:, 6144:8192].rearrange("p (k d) -> p k d", k=16)
    ptile = [scr[:, 8192 + 512 * k:8192 + 512 * (k + 1)] for k in range(4)]
    rec = scr[:, 10240:11264].bitcast(F32)
    rt1 = scr[:, 11264:12288].bitcast(F32)
    yT = big0[:, :].bitcast(F32).rearrange("p (k t) -> p k t", k=KC)
    h2T = big0[:, :].rearrange("p (k t) -> p k t", k=KC)
    sqe = [scr[:, 8192:8704], scr[:, 8704:9216]]
    sqw = [scr[:, 8192:9216], scr[:, 9216:10240]]
    rstd_e = scr[:, 10240:12288].bitcast(F32)
    hbuf = [scr[:, 0:1028].bitcast(F32), scr[:, 1056:2084].bitcast(F32)]
    ybuf = [scr[:, 2112:3136].bitcast(F32), scr[:, 3136:4160].bitcast(F32)]
    sgb = scr[:, 4160:4672]
    gvs = [scr[:, 4672:5184], scr[:, 5184:5696]]
    gslot = [mergedT[:, :, :].rearrange("p k t -> p (k t)")[:, 4096 * i:4096 * (i + 1)].rearrange("p (f t) -> p f t", f=4)
             for i in range(3)]
    ps = [nc.alloc_psum_tensor("ps%d" % i, [128, 512], F32) for i in range(8)]
    t_ps = P.toks(8)
    for t in t_ps:
        t.excl = True
    t_pmod = t_ps[0]

    t_cols = P.dtok("cols")
    t_modc, t_prm, t_scb, t_ident, t_epsc, t_ones, t_fd = P.toks(7)
    t_stat = P.toks(64)
    t_xs = [P.dtok("xs0"), P.dtok("xs1")]
    t_xn = P.toks(2)
    t_hT = [[P.tok() for _ in range(NT // 128)] for _ in range(KC)]
    t_A = P.toks(8)
    t_mg = [[P.tok() for _ in range(2)] for _ in range(KC)]
    t_WmT = P.dtok("wmT")
    t_bsb = P.dtok("bsb")
    t_Cg, = P.toks(1)
    t_tmp = P.toks(2)
    t_dbg = P.dtok("dbg")
    d_out = []
    all_hT = [t for row in t_hT for t in row]
    t_qg = [[P.tok() for _ in range(2)] for _ in range(4)]
    t_kvg = [[P.tok() for _ in range(4)] for _ in range(2)]
    t_rq = P.toks(2)
    t_rkv = P.toks(4)
    t_rkvc, t_mask, t_cos, t_sin, t_posi, t_angf, t_targ = P.toks(7)
    t_posi = P.dtok("posi")
    t_kpe = P.toks(4)
    t_sqb = P.toks(2)
    t_QnT, t_Qr, t_pt = P.toks(2), P.toks(2), P.toks(4)
    t_KnT = P.toks(4)
    t_Vh, t_rec, t_rt1 = P.toks(3)
    t_oT = [[P.tok() for _ in range(2)] for _ in range(16)]
    t_KnT_b = P.toks(4)
    t_Vh_b = P.tok()
    KV = ((KnT, Vh, t_KnT, t_Vh), (KnT_b, Vh_b, t_KnT_b, t_Vh_b))
    t_yT = [[P.tok() for _ in range(2)] for _ in range(KC)]
    t_h2T = [[P.tok() for _ in range(S // 128)] for _ in range(KC)]
    t_sqe = P.toks(2)
    t_rse = P.toks(2)
    t_hb, t_yb = P.toks(2), P.toks(2)
    t_sg, = P.toks(1)
    t_gvs = P.toks(2)
    t_gslot = [P.dtok("gs%d" % i) for i in range(3)]
    t_gvst = [P.dtok("gvst%d" % i) for i in range(2)]
    d_x1 = P.toks(S // 128)
    d_gv = [[P.tok() for _ in range(4)] for _ in range(FC)]
    t_st = [P.dtok("st0"), P.dtok("st1")]
    all_oT = [t for r in t_oT for t in r]
    big0_toks = all_hT + t_A + all_oT + [t for r in t_yT for t in r] + [t for r in t_h2T for t in r]
    mg_toks = [t for r in t_mg for t in r] + t_gslot
    scr_toks = t_xs + t_xn + [t_WmT, t_bsb, t_Cg] + t_tmp + t_sqb + [t_posi, t_angf, t_targ] + t_QnT + t_Qr + t_pt \
        + t_KnT + [t_Vh, t_rec, t_rt1] + t_sqe + t_rse + t_hb + t_yb + [t_sg] + t_gvs

    wp = WPool(P, nc, 3, KC * 512)
    rot = {"n": 0, "lo": 4, "cnt": 4}

    def gbank():
        b = rot["lo"] + rot["n"] % rot["cnt"]
        rot["n"] += 1
        return b

    prot = {"n": 0}

    def pbank():
        b = 1 + prot["n"] % 7
        prot["n"] += 1
        return b

    def fence(toks):
        P.op("pool", lambda e: e.memset(fdummy[:, 0:1], 0.0), writes=list(toks) + [t_fd])

    def dump(name, src_ap, toks):
        if name in dbg_d:
            d = P.tok()
            P.dma("sp", lambda e: e.dma_start(out=dbg_d[name], in_=src_ap), reads=list(toks), writes=[d], sem_tok=t_dbg)
            d_out.append(d)

    def wsrc(w_d, c0, ncols):
        return w_d[:, c0:c0 + ncols].rearrange("(k p) o -> p k o", p=128)

    def consts(plan):
        if plan:
            return
        P.op("pool", lambda e: e.memset(epsc[:, :], EPS), writes=[t_epsc])
        P.op("pool", lambda e: e.memset(ones[:, :], 1.0), writes=[t_ones])
        P.dma("sp", lambda e: e.dma_start(out=cols[:, :], in_=cols_d), writes=[t_cols])
        P.op("pool", lambda e: e.memset(identf[:, :], 1.0), writes=[t_ident])
        P.op("pool", lambda e: e.affine_select(out=identf[:, :], in_=identf[:, :], pattern=[[-1, 128]],
                                                compare_op=ALU.is_equal, fill=0.0, base=0, channel_multiplier=1),
             reads=[t_ident], writes=[t_ident])
        P.op("pool", lambda e: e.tensor_copy(out=ident[:, :], in_=identf[:, :]), reads=[t_ident], writes=[t_ident])
        P.op("pool", lambda e: e.memset(masktmp[:, :], 0.0), writes=[t_mask])
        P.op("pool", lambda e: e.affine_select(out=masktmp[:, :], in_=masktmp[:, :], pattern=[[1, 128]],
                                                compare_op=ALU.is_ge, fill=-30000.0, base=0, channel_multiplier=-1),
             reads=[t_mask], writes=[t_mask])
        P.op("pool", lambda e: e.tensor_copy(out=maskneg[:, :], in_=masktmp[:, :]), reads=[t_mask], writes=[t_mask])

    MOD_PARTS = ((0, 8), (8, 12), (12, 24))

    def mod_blocks(plan, j0, j1):
        pmod = ps[0]
        for j in range(j0, j1):
            if plan:
                wp.declare(("ada", j), wsrc(w_ada_d, j * 512, 512), KC, 512)
                continue
            wv, wt = wp.next(("ada", j))

            def mm(e, wv=wv, j=j):
                for oc in range(4):
                    col = j * 4 + oc
                    for kc in range(KC):
                        ins = e.matmul(pmod[:, col:col + 1], lhsT=wv[:, kc, oc * 128:(oc + 1) * 128],
                                       rhs=scb[:, kc:kc + 1], start=(kc == 0), stop=(kc == KC - 1))
                return ins
            P.op("pe", mm, reads=[wt, t_scb], writes=[t_pmod])

    def phase_mod(plan, part, stream=True):
        j0, j1 = MOD_PARTS[part]
        pmod = ps[0]
        if not plan and part == 0:
            P.op("act", lambda e: e.activation(out=scb[:, :], in_=cols[:, C_C:C_C + KC], func=AF.Silu),
                 reads=[t_cols], writes=[t_scb])
        if stream:
            mod_blocks(plan, j0, j1)
        if plan:
            return
        c0, c1 = j0 * 4, j1 * 4
        P.op("dve", lambda e: e.tensor_tensor(out=modc[:, c0:c1], in0=pmod[:, c0:c1], in1=cols[:, C_BADA + c0:C_BADA + c1],
                                              op=ALU.add), reads=[t_pmod, t_cols], writes=[t_modc])
        def gs_sh(sl, c_sh, c_sc, c_g):
            P.op("dve", lambda e: e.scalar_tensor_tensor(out=prm[:, 3 * sl + 0, :], in0=modc[:, c_sc:c_sc + KC], scalar=1.0,
                                                         in1=cols[:, c_g:c_g + KC], op0=ALU.add, op1=ALU.mult),
                 reads=[t_modc, t_cols], writes=[t_prm])
            P.op("dve", lambda e: e.tensor_copy(out=prm[:, 3 * sl + 1, :], in_=modc[:, c_sh:c_sh + KC]),
                 reads=[t_modc], writes=[t_prm])

        def gg(sl, c_ga, c_gp):
            P.op("dve", lambda e: e.tensor_tensor(out=prm[:, 3 * sl + 2, :], in0=modc[:, c_ga:c_ga + KC],
                                                  in1=cols[:, c_gp:c_gp + KC], op=ALU.mult),
                 reads=[t_modc, t_cols], writes=[t_prm])
        if part == 0:
            gs_sh(0, 0, 16, C_G1)
        elif part == 1:
            gg(0, 32, C_GP1)
        else:
            gs_sh(1, 48, 64, C_G2)
            gg(1, 80, C_GP2)
            dump("modc", modc[:, :], [t_modc])

    def rstd_from(sq, rs, t_sq, t_rs, n):
        P.op("act", lambda e: e.activation(out=rs, in_=sq, func=AF.Sqrt, scale=1.0 / n, bias=epsc[:, 0:1]),
             reads=[t_sq, t_epsc], writes=[t_rs])
        P.op("dve", lambda e: e.reciprocal(out=rs, in_=rs), reads=[t_rs], writes=[t_rs])

    def prologue(plan, src_d, T0, ntiles, sl, dst, t_dst, src_toks=None):
        if plan:
            return
        fence(scr_toks)

        def stage_a(i):
            b = i % 2
            r0 = T0 + i * 128
            P.dma("sp", lambda e, b=b, r0=r0: e.dma_start(out=xs[b], in_=src_d[r0:r0 + 128, :]),
                  reads=([src_toks[r0 // 128]] if src_toks else []), writes=[t_xs[b]])
            sq, rs = stat[:, 2 * b:2 * b + 1], stat[:, 2 * b + 1:2 * b + 2]
            P.op("act", lambda e, b=b, sq=sq: e.activation(out=xn[b], in_=xs[b], func=AF.Square, accum_out=sq),
                 reads=[t_xs[b]], writes=[t_xn[b], t_stat[2 * b]])
            rstd_from(sq, rs, t_stat[2 * b], t_stat[2 * b + 1], D)
            P.op("dve", lambda e, b=b, rs=rs: e.tensor_scalar(out=xn[b], in0=xs[b], scalar1=rs, scalar2=None, op0=ALU.mult),
                 reads=[t_xs[b], t_stat[2 * b + 1]], writes=[t_xn[b]])

        def stage_b(i):
            b = i % 2
            for half in range(2):
                pb = 2 * b + half
                pv = ps[pb][:, :].bitcast(BF16)

                def tr(e, b=b, half=half, pv=pv):
                    for k in range(8):
                        kc = half * 8 + k
                        ins = e.transpose(out=pv[:, k * 128:(k + 1) * 128], in_=xn[b][:, kc * 128:(kc + 1) * 128],
                                          identity=ident[:, :])
                    return ins
                P.op("pe", tr, reads=[t_xn[b], t_ident], writes=[t_ps[pb]])
                for k in range(8):
                    kc = half * 8 + k
                    if half == 0:
                        P.op("act", lambda e, kc=kc, k=k, pv=pv, i=i: e.activation(
                            out=dst[:, kc, i * 128:(i + 1) * 128], in_=pv[:, k * 128:(k + 1) * 128], func=AF.Identity,
                            scale=prm[:, 3 * sl + 0, kc:kc + 1], bias=prm[:, 3 * sl + 1, kc:kc + 1]),
                            reads=[t_ps[pb], t_prm], writes=[t_dst[kc][i]])
                    else:
                        P.op("dve", lambda e, kc=kc, k=k, pv=pv, i=i: e.tensor_scalar(
                            out=dst[:, kc, i * 128:(i + 1) * 128], in0=pv[:, k * 128:(k + 1) * 128],
                            scalar1=prm[:, 3 * sl + 0, kc:kc + 1], scalar2=prm[:, 3 * sl + 1, kc:kc + 1],
                            op0=ALU.mult, op1=ALU.add), reads=[t_ps[pb], t_prm], writes=[t_dst[kc][i]])

        stage_a(0)
        for i in range(ntiles):
            if i + 1 < ntiles:
                stage_a(i + 1)
            stage_b(i)

    def gmlp_setup(plan):
        if plan:
            return
        fence(scr_toks)
        P.dma("pool", lambda e: e.dma_start(out=WmT, in_=wsT_d), writes=[t_WmT])
        P.op("pool", lambda e: e.affine_select(out=WmT, in_=WmT, pattern=[[0, 16], [1, 128]], compare_op=ALU.is_ge,
                                                fill=0.0, base=0, channel_multiplier=-1), reads=[t_WmT], writes=[t_WmT])
        P.dma("sp", lambda e: e.dma_start(out=bsb.rearrange("p g q -> p (g q)"), in_=bs_d.partition_broadcast(128)),
              writes=[t_bsb])
        for q4 in range(4):
            def mm(e, q4=q4):
                for k in range(4):
                    g = q4 * 4 + k
                    ins = e.matmul(ps[q4][:, k * 128:(k + 1) * 128], lhsT=ones[:, :], rhs=WmT[:, g, :], start=True, stop=True)
                return ins
            P.op("pe", mm, reads=[t_ones, t_WmT], writes=[t_ps[q4]])
            for k in range(4):
                g = q4 * 4 + k
                P.op("dve", lambda e, q4=q4, k=k, g=g: e.scalar_tensor_tensor(
                    out=Cg[:, g, :], in0=ps[q4][:, k * 128:(k + 1) * 128], scalar=cols[:, C_LNB + g:C_LNB + g + 1],
                    in1=bsb[:, g, :], op0=ALU.mult, op1=ALU.add), reads=[t_ps[q4], t_cols, t_bsb], writes=[t_Cg])

    def vphase(plan, blk):
        for j in range(4):
            if plan:
                wp.declare(("v", blk, j), wsrc(w_in_d, O_V + j * 512, 512), KC, 512)
                continue
            wv, wt = wp.next(("v", blk, j))
            for n in range(8):
                pb = gbank()

                def mm(e, wv=wv, n=n, pb=pb):
                    for kc in range(KC):
                        ins = e.matmul(ps[pb][:, :], lhsT=hT[:, kc, n * 128:(n + 1) * 128], rhs=wv[:, kc, :],
                                       start=(kc == 0), stop=(kc == KC - 1))
                    return ins
                P.op("pe", mm, reads=[wt] + [t_hT[kc][n] for kc in range(KC)], writes=[t_ps[pb]])
                P.op("act", lambda e, n=n, j=j, pb=pb: e.activation(out=A3[:, n, j * 512:(j + 1) * 512], in_=ps[pb][:, :],
                                                                      func=AF.Gelu_apprx_tanh),
                     reads=[t_ps[pb]], writes=[t_A[n]])

    def mixing(plan):
        if plan:
            return

        def cols6(n):
            c = 32 + 6 * (n % 3)
            return [stat[:, c + k:c + k + 1] for k in range(6)], t_stat[c:c + 6]

        def a_act(n):
            (s1, s2, mu, var, rsd, nb), ts = cols6(n)
            P.op("act", lambda e: e.activation(out=junkv, in_=A3[:, n, :], func=AF.Identity, accum_out=s1),
                 reads=[t_A[n]], writes=[t_tmp[0], t_tmp[1], ts[0]])
            P.op("act", lambda e: e.activation(out=junkv, in_=A3[:, n, :], func=AF.Square, accum_out=s2),
                 reads=[t_A[n]], writes=[t_tmp[0], t_tmp[1], ts[1]])

        def a_dve1(n):
            (s1, s2, mu, var, rsd, nb), ts = cols6(n)
            P.op("dve", lambda e: e.tensor_scalar(out=mu, in0=s1, scalar1=1.0 / 2048, scalar2=None, op0=ALU.mult),
                 reads=[ts[0]], writes=[ts[2]])
            P.op("dve", lambda e: e.tensor_tensor(out=var, in0=mu, in1=mu, op=ALU.mult), reads=[ts[2]], writes=[ts[3]])
            P.op("dve", lambda e: e.scalar_tensor_tensor(out=var, in0=s2, scalar=1.0 / 2048, in1=var, op0=ALU.mult,
                                                         op1=ALU.subtract), reads=[ts[1], ts[3]], writes=[ts[3]])

        def a_sqrt(n):
            (s1, s2, mu, var, rsd, nb), ts = cols6(n)
            P.op("act", lambda e: e.activation(out=rsd, in_=var, func=AF.Sqrt, scale=1.0, bias=epsc[:, 0:1]),
                 reads=[ts[3], t_epsc], writes=[ts[4]])

        def a_dve2(n):
            (s1, s2, mu, var, rsd, nb), ts = cols6(n)
            P.op("dve", lambda e: e.reciprocal(out=rsd, in_=rsd), reads=[ts[4]], writes=[ts[4]])
            P.op("dve", lambda e: e.scalar_tensor_tensor(out=nb, in0=mu, scalar=-1.0, in1=rsd, op0=ALU.mult, op1=ALU.mult),
                 reads=[ts[2], ts[4]], writes=[ts[5]])

        def b_norm_mm(n):
            (s1, s2, mu, var, rsd, nb), ts = cols6(n)
            P.op("dve", lambda e: e.tensor_scalar(out=A3[:, n, :], in0=A3[:, n, :], scalar1=rsd, scalar2=nb,
                                                  op0=ALU.mult, op1=ALU.add), reads=[ts[4], ts[5]], writes=[t_A[n]])
            for q4 in range(4):
                pb = 4 * (n % 2) + q4

                def mm(e, q4=q4, pb=pb):
                    for k in range(4):
                        g = q4 * 4 + k
                        ins = e.matmul(ps[pb][:, k * 128:(k + 1) * 128], lhsT=A3[:, n, g * 128:(g + 1) * 128],
                                       rhs=WmT[:, g, :], start=True, stop=True)
                    return ins
                P.op("pe", mm, reads=[t_A[n], t_WmT], writes=[t_ps[pb]])

        def b_ev(n):
            for q4 in range(4):
                pb = 4 * (n % 2) + q4
                P.op("dve", lambda e, pb=pb, q4=q4: e.tensor_copy(
                    out=A4[:, n, 4 * q4:4 * q4 + 4, :].rearrange("p g q -> p (g q)"), in_=ps[pb][:, :]),
                    reads=[t_ps[pb]], writes=[t_A[n]])

        a_act(0); a_dve1(0); a_sqrt(0); a_dve2(0); b_norm_mm(0)
        a_act(1); a_dve1(1)
        for k in range(8):
            if k + 1 < 8:
                a_sqrt(k + 1)
                a_dve2(k + 1)
                b_norm_mm(k + 1)
            b_ev(k)
            if k + 2 < 8:
                a_act(k + 2)
                a_dve1(k + 2)

    def uphase(plan, blk):
        for j in range(4):
            if plan:
                wp.declare(("u", blk, j), wsrc(w_in_d, O_U + j * 512, 512), KC, 512)
                continue
            wv, wt = wp.next(("u", blk, j))
            for oc in range(4):
                g = j * 4 + oc
                for tt in range(2):
                    pb = gbank()
                    sl_ = (oc * 2 + tt) % 2

                    def mm(e, wv=wv, oc=oc, tt=tt, pb=pb):
                        for kc in range(KC):
                            ins = e.matmul(ps[pb][:, :], lhsT=wv[:, kc, oc * 128:(oc + 1) * 128],
                                           rhs=hT[:, kc, tt * 512:(tt + 1) * 512], start=(kc == 0), stop=(kc == KC - 1))
                        return ins
                    P.op("pe", mm, reads=[wt] + [t_hT[kc][n] for kc in range(KC) for n in range(4 * tt, 4 * tt + 4)],
                         writes=[t_ps[pb]])
                    P.op("act", lambda e, pb=pb, sl_=sl_: e.activation(out=tmpb[sl_], in_=ps[pb][:, :], func=AF.Gelu_apprx_tanh),
                         reads=[t_ps[pb]], writes=[t_tmp[sl_]])
                    av = A4[:, 4 * tt:4 * tt + 4, g, :]
                    P.op("dve", lambda e, av=av, g=g: e.scalar_tensor_tensor(
                        out=av, in0=av, scalar=cols[:, C_LNG + g:C_LNG + g + 1], in1=Cg[:, g:g + 1, :].broadcast_to([128, 4, 128]),
                        op0=ALU.mult, op1=ALU.add), reads=[t_cols, t_Cg], writes=t_A[4 * tt:4 * tt + 4])
                    P.op("dve", lambda e, av=av, sl_=sl_: e.tensor_tensor(
                        out=av, in0=tmpb[sl_].rearrange("p (n q) -> p n q", n=4), in1=av, op=ALU.mult),
                        reads=[t_tmp[sl_]], writes=t_A[4 * tt:4 * tt + 4])

    def branch_a(plan, blk):
        for j in range(8):
            if plan:
                wp.declare(("ag", blk, j), wsrc(w_ag_d, j * 512, 512), KC, 512)
                continue
            if j == 0:
                fence(scr_toks)
            wv, wt = wp.next(("ag", blk, j))
            for oc in range(2):
                o = j * 2 + oc
                for tt in range(2):
                    pg, py = gbank(), gbank()
                    sl_ = (oc * 2 + tt) % 2

                    def mmg(e, wv=wv, oc=oc, tt=tt, pg=pg):
                        for kc in range(KC):
                            ins = e.matmul(ps[pg][:, :], lhsT=wv[:, kc, 256 + oc * 128:256 + (oc + 1) * 128],
                                           rhs=hT[:, kc, tt * 512:(tt + 1) * 512], start=(kc == 0), stop=(kc == KC - 1))
                        return ins
                    P.op("pe", mmg, reads=[wt] + [t_hT[kc][n] for kc in range(KC) for n in range(4 * tt, 4 * tt + 4)],
                         writes=[t_ps[pg]])
                    P.op("act", lambda e, pg=pg, sl_=sl_: e.activation(out=tmpf[sl_], in_=ps[pg][:, :], func=AF.Sigmoid),
                         reads=[t_ps[pg]], writes=[t_tmp[sl_]])

                    def mmy(e, wv=wv, oc=oc, tt=tt, py=py):
                        for g in range(16):
                            ins = e.matmul(ps[py][:, :], lhsT=wv[:, g, oc * 128:(oc + 1) * 128],
                                           rhs=A4[:, 4 * tt:4 * tt + 4, g, :], start=(g == 0), stop=(g == 15))
                        return ins
                    P.op("pe", mmy, reads=[wt] + t_A[4 * tt:4 * tt + 4], writes=[t_ps[py]])
                    P.op("dve", lambda e, o=o, tt=tt, py=py, sl_=sl_: e.tensor_tensor(
                        out=mergedT[:, o, tt * 512:(tt + 1) * 512], in0=ps[py][:, :], in1=tmpf[sl_], op=ALU.mult),
                        reads=[t_ps[py], t_tmp[sl_]], writes=[t_mg[o][tt]])

    SCALE = float(192 ** -0.5)
    PI = float(np.pi)

    def latents(plan, blk):
        T0 = blk * NT
        wblk = []
        wp.live = 2
        for j in range(2):
            if plan:
                wp.declare(("lat", blk, j), wsrc(w_lat_d, j * 512, 512), KC, 512)
            else:
                wblk.append(wp.next(("lat", blk, j)))
        if plan:
            wp.live = 1
            return
        fence(scr_toks)
        def rope_tables():
            P.dma("sp", lambda e: e.dma_start(out=posi, in_=pos_d[:, T0:T0 + NT].partition_broadcast(128)), writes=[t_posi])
            P.op("dve", lambda e: e.tensor_copy(out=angf, in_=posi), reads=[t_posi], writes=[t_angf])
            P.op("dve", lambda e: e.tensor_scalar(out=angf, in0=angf, scalar1=cols[:, C_INV:C_INV + 1], scalar2=None, op0=ALU.mult),
                 reads=[t_angf, t_cols], writes=[t_angf])
            HI = 6.28125
            LO = float(2 * np.pi - 6.28125)
            P.op("dve", lambda e: e.tensor_scalar(out=targ, in0=angf, scalar1=float(1 / (2 * np.pi)), scalar2=None, op0=ALU.mult),
                 reads=[t_angf], writes=[t_targ])
            P.op("dve", lambda e: e.tensor_copy(out=posi, in_=targ), reads=[t_targ], writes=[t_posi])
            P.op("dve", lambda e: e.tensor_copy(out=targ, in_=posi), reads=[t_posi], writes=[t_targ])
            P.op("dve", lambda e: e.scalar_tensor_tensor(out=angf, in0=targ, scalar=-HI, in1=angf, op0=ALU.mult, op1=ALU.add),
                 reads=[t_targ, t_angf], writes=[t_angf])
            P.op("dve", lambda e: e.scalar_tensor_tensor(out=angf, in0=targ, scalar=-LO, in1=angf, op0=ALU.mult, op1=ALU.add),
                 reads=[t_targ, t_angf], writes=[t_angf])
            P.op("dve", lambda e: e.tensor_scalar(out=angf, in0=angf, scalar1=-PI, scalar2=PI, op0=ALU.max, op1=ALU.min),
                 reads=[t_angf], writes=[t_angf])
            P.op("act", lambda e: e.activation(out=sin_s[:, :], in_=angf, func=AF.Sin, scale=cols[:, C_SGN:C_SGN + 1]),
                 reads=[t_angf, t_cols], writes=[t_sin])
            P.op("dve", lambda e: e.tensor_scalar(out=targ, in0=angf, scalar1=PI / 2, scalar2=2 * PI, op0=ALU.is_gt, op1=ALU.mult),
                 reads=[t_angf], writes=[t_targ])
            P.op("dve", lambda e: e.scalar_tensor_tensor(out=targ, in0=angf, scalar=PI / 2, in1=targ, op0=ALU.add, op1=ALU.subtract),
                 reads=[t_angf, t_targ], writes=[t_targ])
            P.op("dve", lambda e: e.tensor_scalar(out=targ, in0=targ, scalar1=-PI, scalar2=PI, op0=ALU.max, op1=ALU.min),
                 reads=[t_targ], writes=[t_targ])
            P.op("act", lambda e: e.activation(out=cos2[:, :], in_=targ, func=AF.Sin), reads=[t_targ], writes=[t_cos])
        (wq, wqt), (wk, wkt) = wblk
        lvl = {"lat0": 0, "lat1": 1, "lat2": 2}.get(stop_after, 3)
        if lvl < 3:
            rope_tables()
        for tt in range(2 if lvl > 0 else 0):
            hts = [t_hT[kc][n] for kc in range(KC) for n in range(4 * tt, 4 * tt + 4)]
            tsl = slice(tt * 512, (tt + 1) * 512)
            asl = slice(T0 + tt * 512, T0 + (tt + 1) * 512)
            at = (T0 // 512) + tt
            pq = 2

            def ones_q(c, pq=pq):
                P.op("pe", lambda e: e.matmul(ps[pq][:, :], lhsT=ones[:, :], rhs=sqb[c % 2], start=(c == 0), stop=(c == 3)),
                     reads=[t_ones, t_sqb[c % 2]], writes=[t_ps[pq]])
            for c in range(4):
                pb = gbank()

                def mm(e, c=c, pb=pb, tsl=tsl):
                    for kc in range(KC):
                        ins = e.matmul(ps[pb][:, :], lhsT=wq[:, kc, c * 128:(c + 1) * 128], rhs=hT[:, kc, tsl],
                                       start=(kc == 0), stop=(kc == KC - 1))
                    return ins
                P.op("pe", mm, reads=[wqt] + hts, writes=[t_ps[pb]])
                P.op("act", lambda e, c=c, pb=pb: e.activation(out=sqb[c % 2], in_=ps[pb][:, :], func=AF.Square),
                     reads=[t_ps[pb]], writes=[t_sqb[c % 2]])
                P.op("act", lambda e, c=c, pb=pb, tsl=tsl: e.activation(out=qgT[:, c, tsl], in_=ps[pb][:, :], func=AF.Copy,
                                                                        scale=cols[:, C_QG + c:C_QG + c + 1]),
                     reads=[t_ps[pb], t_cols], writes=[t_qg[c][tt]])
                if c >= 1:
                    ones_q(c - 1)
            ones_q(3)
            P.op("act", lambda e, tsl=tsl: e.activation(out=tmpf[0], in_=ps[pq][:, :], func=AF.Sqrt, scale=1.0 / 512,
                                                        bias=epsc[:, 0:1]), reads=[t_ps[pq], t_epsc], writes=[t_tmp[0]])
            P.op("dve", lambda e: e.reciprocal(out=tmpf[0], in_=tmpf[0]), reads=[t_tmp[0]], writes=[t_tmp[0]])
            for c in range(4):
                P.op("dve", lambda e, c=c, tsl=tsl: e.tensor_tensor(out=qgT[:, c, tsl], in0=qgT[:, c, tsl], in1=tmpf[0], op=ALU.mult),
                     reads=[t_tmp[0]], writes=[t_qg[c][tt]])
            if lvl < 2:
                continue
            pk = 3

            def ones_k(c, pk=pk):
                P.op("pe", lambda e: e.matmul(ps[pk][:, :], lhsT=ones[:, :], rhs=sqb[c], start=(c == 0), stop=(c == 1)),
                     reads=[t_ones, t_sqb[c]], writes=[t_ps[pk]])
            for c in range(2):
                pb = gbank()

                def mm(e, c=c, pb=pb, tsl=tsl):
                    for kc in range(KC):
                        ins = e.matmul(ps[pb][:, :], lhsT=wk[:, kc, c * 128:(c + 1) * 128], rhs=hT[:, kc, tsl],
                                       start=(kc == 0), stop=(kc == KC - 1))
                    return ins
                P.op("pe", mm, reads=[wkt] + hts, writes=[t_ps[pb]])
                P.op("act", lambda e, c=c, pb=pb: e.activation(out=sqb[c], in_=ps[pb][:, :], func=AF.Square),
                     reads=[t_ps[pb]], writes=[t_sqb[c]])
                P.op("act", lambda e, c=c, pb=pb, asl=asl: e.activation(out=kvgT[:, c, asl], in_=ps[pb][:, :], func=AF.Copy,
                                                                        scale=cols[:, C_KVG + c:C_KVG + c + 1]),
                     reads=[t_ps[pb], t_cols], writes=[t_kvg[c][at]])
                if c >= 1:
                    ones_k(c - 1)
            ones_k(1)
            P.op("act", lambda e: e.activation(out=tmpf[1], in_=ps[pk][:, :], func=AF.Sqrt, scale=1.0 / 256, bias=epsc[:, 0:1]),
                 reads=[t_ps[pk], t_epsc], writes=[t_tmp[1]])
            P.op("dve", lambda e: e.reciprocal(out=tmpf[1], in_=tmpf[1]), reads=[t_tmp[1]], writes=[t_tmp[1]])
            for c in range(2):
                P.op("dve", lambda e, c=c, asl=asl: e.tensor_tensor(out=kvgT[:, c, asl], in0=kvgT[:, c, asl], in1=tmpf[1], op=ALU.mult),
                     reads=[t_tmp[1]], writes=[t_kvg[c][at]])
            if lvl < 3:
                continue
            if tt == 0:
                rope_tables()
            pr_, psw = gbank(), gbank()
            for pbx, c0 in ((pr_, 256), (psw, 384)):
                def mm(e, pbx=pbx, c0=c0, tsl=tsl):
                    for kc in range(KC):
                        ins = e.matmul(ps[pbx][:, :], lhsT=wk[:, kc, c0:c0 + 128], rhs=hT[:, kc, tsl],
                                       start=(kc == 0), stop=(kc == KC - 1))
                    return ins
                P.op("pe", mm, reads=[wkt] + hts, writes=[t_ps[pbx]])
            P.op("dve", lambda e, tsl=tsl, pr_=pr_: e.tensor_tensor(out=tmpf[0], in0=ps[pr_][:, :], in1=cos2[:, tsl], op=ALU.mult),
                 reads=[t_ps[pr_], t_cos], writes=[t_tmp[0]])
            P.op("dve", lambda e, tsl=tsl, psw=psw: e.tensor_tensor(out=tmpf[1], in0=ps[psw][:, :], in1=sin_s[:, tsl], op=ALU.mult),
                 reads=[t_ps[psw], t_sin], writes=[t_tmp[1]])
            P.op("dve", lambda e, asl=asl: e.tensor_tensor(out=kpe_dup[:, asl], in0=tmpf[0], in1=tmpf[1], op=ALU.add),
                 reads=t_tmp, writes=[t_kpe[at]])

    def attention(plan, blk):
        wp.live = 1
        T0 = blk * NT
        nk512 = (T0 + NT) // 512
        ADA2 = (12, 14, 16, 18, 20, 21, 22, 23, 24)
        for pr in range(8):
            if plan:
                wp.declare(("pair", blk, pr), w_pair_d[pr].rearrange("p (k o) -> p k o", k=1), 1, 3072)
                if blk == 0:
                    mod_blocks(True, ADA2[pr], ADA2[pr + 1])
                continue
            rot["lo"], rot["cnt"] = 5, 3
            if pr == 0:
                fence(scr_toks + t_A)
                P.op("pool", lambda e: e.memset(Qr[0][64:128, :], 0.0), writes=[t_Qr[0]])
                P.op("pool", lambda e: e.memset(Qr[1][0:64, :], 0.0), writes=[t_Qr[1]])
            wv_, wt = wp.next(("pair", blk, pr))
            w = wv_[:, 0, :]
            qn = w[:, 0:1024].rearrange("p (k o) -> p k o", k=4)
            qr = w[:, 1024:1536].rearrange("p (k o) -> p k o", k=4)
            qs = w[:, 1536:2048].rearrange("p (k o) -> p k o", k=4)
            kn = w[:, 2048:2560].rearrange("p (k o) -> p k o", k=2)
            vv = w[:, 2560:3072].rearrange("p (k o) -> p k o", k=2)
            qg_all = [t_qg[c][tt] for c in range(4) for tt in range(2)]
            for tt in range(2):
                tsl = slice(tt * 512, (tt + 1) * 512)
                for hd in range(2):
                    pb = pbank()

                    def mm(e, hd=hd, pb=pb, tsl=tsl, qn=qn):
                        for c in range(4):
                            ins = e.matmul(ps[pb][:, :], lhsT=qn[:, c, hd * 128:(hd + 1) * 128], rhs=qgT[:, c, tsl],
                                           start=(c == 0), stop=(c == 3))
                        return ins
                    P.op("pe", mm, reads=[wt] + qg_all, writes=[t_ps[pb]])
                    P.op("dve", lambda e, hd=hd, pb=pb, tsl=tsl: e.tensor_copy(out=QnT[hd][:, tsl], in_=ps[pb][:, :]),
                         reads=[t_ps[pb]], writes=[t_QnT[hd]])
                pr_, psw = pbank(), pbank()
                for pbx, wsel in ((pr_, qr), (psw, qs)):
                    def mm(e, pbx=pbx, wsel=wsel, tsl=tsl):
                        for c in range(4):
                            ins = e.matmul(ps[pbx][:, :], lhsT=wsel[:, c, :], rhs=qgT[:, c, tsl], start=(c == 0), stop=(c == 3))
                        return ins
                    P.op("pe", mm, reads=[wt] + qg_all, writes=[t_ps[pbx]])
                P.op("dve", lambda e, tsl=tsl, pr_=pr_: e.tensor_tensor(out=rt1, in0=ps[pr_][:, :], in1=cos2[:, tsl], op=ALU.mult),
                     reads=[t_ps[pr_], t_cos], writes=[t_rt1])
                P.op("dve", lambda e, tsl=tsl, psw=psw: e.tensor_tensor(out=rec, in0=ps[psw][:, :], in1=sin_s[:, tsl], op=ALU.mult),
                     reads=[t_ps[psw], t_sin], writes=[t_rec])
                P.op("dve", lambda e, tsl=tsl: e.tensor_tensor(out=Qr[0][0:64, tsl], in0=rt1[0:64, :], in1=rec[0:64, :], op=ALU.add),
                     reads=[t_rt1, t_rec], writes=[t_Qr[0]])
                P.op("dve", lambda e, tsl=tsl: e.tensor_tensor(out=Qr[1][64:128, tsl], in0=rt1[64:128, :], in1=rec[64:128, :], op=ALU.add),
                     reads=[t_rt1, t_rec], writes=[t_Qr[1]])
            for hd in range(2):
                KnT_h, Vh_h, t_KnT_h, t_Vh_h = KV[hd]
                for kt in range(nk512):
                    ksl = slice(kt * 512, (kt + 1) * 512)
                    pb = pbank()

                    def mm(e, hd=hd, pb=pb, ksl=ksl, kn=kn):
                        for c in range(2):
                            ins = e.matmul(ps[pb][:, :], lhsT=kn[:, c, hd * 128:(hd + 1) * 128], rhs=kvgT[:, c, ksl],
                                           start=(c == 0), stop=(c == 1))
                        return ins
                    P.op("pe", mm, reads=[wt, t_kvg[0][kt], t_kvg[1][kt]], writes=[t_ps[pb]])
                    P.op("dve", lambda e, pb=pb, ksl=ksl, KnT_h=KnT_h: e.tensor_copy(out=KnT_h[:, ksl], in_=ps[pb][:, :]),
                         reads=[t_ps[pb]], writes=[t_KnT_h[kt]])
            for kt in range(nk512):
                for half in range(2):
                    pb2 = pbank()

                    def mmv(e, pb2=pb2, kt=kt, half=half, vv=vv):
                        for i in range(2):
                            kb = kt * 4 + half * 2 + i
                            for c in range(2):
                                ins = e.matmul(ps[pb2][:, i * 256:(i + 1) * 256], lhsT=kvgT[:, c, kb * 128:(kb + 1) * 128],
                                               rhs=vv[:, c, :], start=(c == 0), stop=(c == 1))
                        return ins
                    P.op("pe", mmv, reads=[wt, t_kvg[0][kt], t_kvg[1][kt]], writes=[t_ps[pb2]])
                    for hd in range(2):
                        Vdst, t_Vdst = KV[hd][1], KV[hd][3]
                        kb0 = kt * 4 + half * 2
                        P.op("act", lambda e, pb2=pb2, hd=hd, kb0=kb0, Vdst=Vdst: e.activation(
                            out=Vdst[:, kb0:kb0 + 2, :],
                            in_=ps[pb2][:, :].rearrange("p (i h d) -> p i h d", i=2, h=2)[:, :, hd, :], func=AF.Copy),
                            reads=[t_ps[pb2]], writes=[t_Vdst])
            for hd in range(2):
                h = 2 * pr + hd
                KnT_h, Vh_h, t_KnT_h, t_Vh_h = KV[hd]
                LOOK = 2
                pend = []
                cnt = 0

                def finalize(tt, po, psm, h=h):
                    P.op("dve", lambda e: e.reciprocal(out=rec, in_=ps[psm][:, :]), reads=[t_ps[psm]], writes=[t_rec])
                    P.op("dve", lambda e: e.tensor_tensor(out=oT[:, h, tt * 512:(tt + 1) * 512], in0=ps[po][:, :], in1=rec,
                                                          op=ALU.mult), reads=[t_ps[po], t_rec], writes=[t_oT[h][tt]])

                def emit_pend(ent, t_Vh_h=t_Vh_h):
                    f, k_, po_, psm_, fin = ent
                    P.op("pe", f, reads=[t_Vh_h, t_pt[k_], t_ones], writes=[t_ps[po_], t_ps[psm_]])
                    if fin is not None:
                        finalize(fin, po_, psm_)

                for tt in range(2):
                    qa = (T0 + 512 * tt) // 128
                    nkb = qa + 4
                    po, psm = ((2, 3), (1, 4))[tt]
                    for kb in range(nkb):
                        i = kb - qa
                        c0 = 128 * i if i > 0 else 0
                        qsl = slice(tt * 512 + c0, (tt + 1) * 512)
                        pb = gbank()
                        sl_ = cnt % 4
                        cnt += 1

                        def mms(e, hd=hd, kb=kb, i=i, c0=c0, qsl=qsl, pb=pb, KnT_h=KnT_h):
                            e.matmul(ps[pb][:, c0:512], lhsT=KnT_h[:, kb * 128:(kb + 1) * 128], rhs=QnT[hd][:, qsl],
                                     start=True, stop=False)
                            ins = e.matmul(ps[pb][:, c0:512], lhsT=kpe_dup[:, kb * 128:(kb + 1) * 128], rhs=Qr[hd][:, qsl],
                                           start=False, stop=(i < 0))
                            if i >= 0:
                                ins = e.matmul(ps[pb][:, c0:c0 + 128], lhsT=ident[:, :], rhs=maskneg[:, :], start=False, stop=True)
                            return ins
                        P.op("pe", mms, reads=[t_KnT_h[kb // 4], t_QnT[hd], t_Qr[hd], t_kpe[kb // 4], t_ident, t_mask],
                             writes=[t_ps[pb]])
                        P.op("act", lambda e, pb=pb, c0=c0, sl_=sl_: e.activation(out=ptile[sl_][:, c0:512], in_=ps[pb][:, c0:512],
                                                                                   func=AF.Exp, scale=SCALE),
                             reads=[t_ps[pb]], writes=[t_pt[sl_]])

                        def mmo(e, kb=kb, c0=c0, sl_=sl_, nkb=nkb, po=po, psm=psm, Vh_h=Vh_h):
                            e.matmul(ps[po][:, c0:512], lhsT=Vh_h[:, kb, :], rhs=ptile[sl_][:, c0:512], start=(kb == 0),
                                     stop=(kb == nkb - 1))
                            return e.matmul(ps[psm][:, c0:512], lhsT=ones[:, :], rhs=ptile[sl_][:, c0:512], start=(kb == 0),
                                            stop=(kb == nkb - 1))
                        pend.append((mmo, sl_, po, psm, tt if kb == nkb - 1 else None))
                        if len(pend) > LOOK:
                            emit_pend(pend.pop(0))
                for ent in pend:
                    emit_pend(ent)
            if blk == 0:
                mod_blocks(False, ADA2[pr], ADA2[pr + 1])
            rot["lo"], rot["cnt"] = 4, 4

    def branch_b(plan, blk):
        for j in range(8):
            if plan:
                wp.declare(("bg", blk, j), wsrc(w_bg_d, j * 512, 512), KC, 512)
                continue
            if j == 0:
                fence(scr_toks)
            wv, wt = wp.next(("bg", blk, j))
            for oc in range(2):
                o = j * 2 + oc
                for tt in range(2):
                    pg, py = gbank(), gbank()
                    sl_ = (oc * 2 + tt) % 2

                    def mmg(e, wv=wv, oc=oc, tt=tt, pg=pg):
                        for kc in range(KC):
                            ins = e.matmul(ps[pg][:, :], lhsT=wv[:, kc, 256 + oc * 128:256 + (oc + 1) * 128],
                                           rhs=hT[:, kc, tt * 512:(tt + 1) * 512], start=(kc == 0), stop=(kc == KC - 1))
                        return ins
                    P.op("pe", mmg, reads=[wt] + [t_hT[kc][n] for kc in range(KC) for n in range(4 * tt, 4 * tt + 4)],
                         writes=[t_ps[pg]])
                    P.op("act", lambda e, pg=pg, sl_=sl_: e.activation(out=tmpf[sl_], in_=ps[pg][:, :], func=AF.Sigmoid),
                         reads=[t_ps[pg]], writes=[t_tmp[sl_]])

                    def mmy(e, wv=wv, oc=oc, tt=tt, py=py):
                        for g in range(16):
                            ins = e.matmul(ps[py][:, :], lhsT=wv[:, g, oc * 128:(oc + 1) * 128],
                                           rhs=oT[:, g, tt * 512:(tt + 1) * 512], start=(g == 0), stop=(g == 15))
                        return ins
                    P.op("pe", mmy, reads=[wt] + [t_oT[h][tt] for h in range(16)], writes=[t_ps[py]])
                    P.op("dve", lambda e, py=py, sl_=sl_: e.tensor_tensor(out=tmpf[sl_], in0=ps[py][:, :], in1=tmpf[sl_], op=ALU.mult),
                         reads=[t_ps[py], t_tmp[sl_]], writes=[t_tmp[sl_]])
                    P.op("dve", lambda e, o=o, tt=tt, sl_=sl_: e.tensor_tensor(
                        out=mergedT[:, o, tt * 512:(tt + 1) * 512], in0=mergedT[:, o, tt * 512:(tt + 1) * 512], in1=tmpf[sl_],
                        op=ALU.add), reads=[t_tmp[sl_], t_mg[o][tt]], writes=[t_mg[o][tt]])

    def wout_phase(plan, blk):
        for j in range(4):
            if plan:
                wp.declare(("wo", blk, j), wsrc(w_out_d, j * 512, 512), KC, 512)
                continue
            if j == 0:
                fence(scr_toks + big0_toks)
            wv, wt = wp.next(("wo", blk, j))
            for oc in range(4):
                o = j * 4 + oc
                for tt in range(2):
                    pb = gbank()

                    def mm(e, wv=wv, oc=oc, tt=tt, pb=pb):
                        for kc in range(KC):
                            ins = e.matmul(ps[pb][:, :], lhsT=wv[:, kc, oc * 128:(oc + 1) * 128],
                                           rhs=mergedT[:, kc, tt * 512:(tt + 1) * 512], start=(kc == 0), stop=(kc == KC - 1))
                        return ins
                    P.op("pe", mm, reads=[wt] + [t_mg[kc][tt] for kc in range(KC)], writes=[t_ps[pb]])
                    P.op("act", lambda e, o=o, tt=tt, pb=pb: e.activation(out=yT[:, o, tt * 512:(tt + 1) * 512], in_=ps[pb][:, :],
                                                                           func=AF.Copy), reads=[t_ps[pb]], writes=[t_yT[o][tt]])
                epi_chunk_stats(0, o, part="sq")
                if o >= 1:
                    epi_chunk_stats(0, o - 1, part="mmc")
        if not plan:
            epi_chunk_stats(0, KC - 1, part="mmc")

    def epi_chunk_stats(sl, o, part="all"):
        psc = ps[3]
        nt = NT // 128
        k = o % 2
        if part in ("all", "sq"):
            P.op("act", lambda e: e.activation(out=sqw[k], in_=yT[:, o, :], func=AF.Square),
                 reads=[t_yT[o][0], t_yT[o][1]], writes=[t_sqe[k]])

        def mmc(e):
            for n in range(nt):
                ins = e.matmul(psc[:, n:n + 1], lhsT=sqw[k][:, n * 128:(n + 1) * 128], rhs=ones[:, 0:1],
                               start=(o == 0 and n == 0), stop=(o == KC - 1 and n == nt - 1))
            return ins
        if part in ("all", "mmc"):
            P.op("pe", mmc, reads=[t_ones, t_sqe[k]], writes=[t_ps[3]])
        if part == "mmc":
            return
        if part == "sq":
            pass
        P.op("dve", lambda e: e.tensor_scalar(out=yT[:, o, :], in0=yT[:, o, :], scalar1=prm[:, 3 * sl + 2, o:o + 1],
                                              scalar2=None, op0=ALU.mult),
             reads=[t_prm, t_sqe[k]], writes=[t_yT[o][0], t_yT[o][1]])

    def epilogue(plan, T0, sl, res_d, res_toks, stats_done=False):
        if plan:
            return
        psc = ps[3]
        nt = NT // 128
        if not stats_done:
            fence(scr_toks)
            for o in range(KC):
                epi_chunk_stats(sl, o)
        rcol = stat[:, 24:24 + nt]
        P.op("act", lambda e: e.activation(out=rcol, in_=psc[:, 0:nt], func=AF.Sqrt, scale=1.0 / D, bias=epsc[:, 0:1]),
             reads=[t_ps[3], t_epsc], writes=[t_stat[24]])
        P.op("dve", lambda e: e.reciprocal(out=rcol, in_=rcol), reads=[t_stat[24]], writes=[t_stat[24]])
        hb_ = 0

        def load(n):
            b, r0 = n % 2, T0 + n * 128
            P.dma("sp", lambda e: e.dma_start(out=xs[b], in_=res_d[r0:r0 + 128, :]),
                  reads=([res_toks[r0 // 128]] if res_toks else []), writes=[t_xs[b]])

        load(0)
        load(1)
        for n in range(nt):
            b = n % 2
            r0 = T0 + n * 128
            tt = n // 4
            for q4 in range(4):
                pb = 4 + hb_ % 4
                hb_ += 1

                def tr(e, n=n, q4=q4, pb=pb):
                    for k in range(4):
                        o = q4 * 4 + k
                        ins = e.transpose(out=ps[pb][:, k * 128:(k + 1) * 128], in_=yT[:, o, n * 128:(n + 1) * 128],
                                          identity=identf[:, :])
                    return ins
                P.op("pe", tr, reads=[t_yT[q4 * 4 + k][tt] for k in range(4)] + [t_ident], writes=[t_ps[pb]])
                P.op("dve", lambda e, b=b, q4=q4, pb=pb, n=n: e.scalar_tensor_tensor(
                    out=xs[b][:, q4 * 512:(q4 + 1) * 512], in0=ps[pb][:, :], scalar=rcol[:, n:n + 1],
                    in1=xs[b][:, q4 * 512:(q4 + 1) * 512], op0=ALU.mult, op1=ALU.add),
                    reads=[t_ps[pb], t_stat[24]], writes=[t_xs[b]])
            P.dma("sp", lambda e, b=b, r0=r0: e.dma_start(out=out_d[r0:r0 + 128, :], in_=xs[b]),
                  reads=[t_xs[b]], writes=[d_x1[r0 // 128]], sem_tok=t_st[b])
            if n + 2 < nt:
                load(n + 2)

    def ffn_up(plan):
        for j in range(FC // 2):
            if plan:
                wp.declare(("up", j), wsrc(w_up2_d, j * 512, 512), KC, 512)
                continue
            if j == 0:
                fence(scr_toks)
            wv, wt = wp.next(("up", j))
            for f2 in range(2):
                fc = 2 * j + f2
                for tt in range(4):
                    tsl = slice(tt * 512, (tt + 1) * 512)
                    hts = [t_h2T[kc][n] for kc in range(KC) for n in range(4 * tt, 4 * tt + 4)]
                    for gvsel in range(2):
                        pb = gbank()
                        ch = fc + FC * gvsel
                        c0 = f2 * 256 + gvsel * 128

                        def mm(e, wv=wv, c0=c0, tsl=tsl, pb=pb):
                            for kc in range(KC):
                                ins = e.matmul(ps[pb][:, :], lhsT=wv[:, kc, c0:c0 + 128], rhs=h2T[:, kc, tsl],
                                               start=(kc == 0), stop=(kc == KC - 1))
                            return ins
                        P.op("pe", mm, reads=[wt] + hts, writes=[t_ps[pb]])
                        hb, yb = hbuf[gvsel], ybuf[gvsel]
                        if tt == 0:
                            P.op("dve", lambda e, hb=hb: e.memset(hb[:, 0:2], 0.0), writes=[t_hb[gvsel]])
                        P.op("act", lambda e, hb=hb, pb=pb: e.activation(out=hb[:, 2:514], in_=ps[pb][:, :], func=AF.Copy),
                             reads=[t_ps[pb]], writes=[t_hb[gvsel]])
                        P.op("act", lambda e, yb=yb, pb=pb, ch=ch: e.activation(
                            out=yb, in_=ps[pb][:, :], func=AF.Identity, scale=cols[:, C_CW + 176 + ch:C_CW + 176 + ch + 1],
                            bias=cols[:, C_CB + ch:C_CB + ch + 1]), reads=[t_ps[pb], t_cols], writes=[t_yb[gvsel]])
                        for tap, off in ((1, 1), (0, 0)):
                            P.op("dve", lambda e, hb=hb, yb=yb, ch=ch, tap=tap, off=off: e.scalar_tensor_tensor(
                                out=yb, in0=hb[:, off:off + 512], scalar=cols[:, C_CW + 88 * tap + ch:C_CW + 88 * tap + ch + 1],
                                in1=yb, op0=ALU.mult, op1=ALU.add), reads=[t_hb[gvsel], t_cols], writes=[t_yb[gvsel]])
                        P.op("dve", lambda e, hb=hb: e.tensor_copy(out=hb[:, 0:2], in_=hb[:, 512:514]),
                             reads=[t_yb[gvsel]], writes=[t_hb[gvsel]])
                    k = tt % 2
                    P.op("act", lambda e: e.activation(out=sgb, in_=ybuf[0], func=AF.Silu), reads=[t_yb[0]], writes=[t_sg])
                    P.op("dve", lambda e, k=k: e.tensor_tensor(out=gvs[k], in0=sgb, in1=ybuf[1], op=ALU.mult),
                         reads=[t_sg, t_yb[1]], writes=[t_gvs[k]])
                    P.dma("sp", lambda e, k=k, fc=fc, tsl=tsl: e.dma_start(out=gv_d[fc, :, tsl], in_=gvs[k]),
                          reads=[t_gvs[k]], writes=[d_gv[fc][tt]], sem_tok=t_gvst[k])

    def ffn_down(plan):
        NG = FC // 4
        gi = 0
        for th in range(2):
            for oq in range(4):
                for g in range(NG):
                    if plan:
                        wp.declare(("dn", th, oq, g), w_down_d[g * 512:(g + 1) * 512, oq * 512:(oq + 1) * 512]
                                   .rearrange("(k p) o -> p k o", p=128), 4, 512)
                        continue
                    if th == 0 and oq == 0 and g == 0:
                        fence(scr_toks + big0_toks + mg_toks)
                    wv, wt = wp.next(("dn", th, oq, g))
                    sl_ = gi % 3
                    gi += 1
                    P.dma("sp", lambda e, sl_=sl_, g=g, th=th: e.dma_start(
                        out=gslot[sl_], in_=gv_d[4 * g:4 * g + 4, :, th * 1024:(th + 1) * 1024].rearrange("f p t -> p f t")),
                        reads=[d_gv[4 * g + f][2 * th + t2] for f in range(4) for t2 in range(2)], writes=[t_gslot[sl_]])

                    def mm(e, wv=wv, sl_=sl_, g=g):
                        for f in range(4):
                            for oc in range(4):
                                for t2 in range(2):
                                    ins = e.matmul(ps[oc * 2 + t2][:, :], lhsT=wv[:, f, oc * 128:(oc + 1) * 128],
                                                   rhs=gslot[sl_][:, f, t2 * 512:(t2 + 1) * 512],
                                                   start=(g == 0 and f == 0), stop=(g == NG - 1 and f == 3))
                        return ins
                    P.op("pe", mm, reads=[wt, t_gslot[sl_]], writes=t_ps)
                if plan:
                    continue
                for oc in range(4):
                    for t2 in range(2):
                        o = oq * 4 + oc
                        if t2 == 0:
                            P.op("act", lambda e, o=o, oc=oc, t2=t2: e.activation(out=yT[:, o, t2 * 512:(t2 + 1) * 512],
                                                                                   in_=ps[oc * 2 + t2][:, :], func=AF.Copy),
                                 reads=[t_ps[oc * 2 + t2]], writes=[t_yT[o][t2]])
                        else:
                            P.op("dve", lambda e, o=o, oc=oc, t2=t2: e.tensor_copy(out=yT[:, o, t2 * 512:(t2 + 1) * 512],
                                                                                    in_=ps[oc * 2 + t2][:, :]),
                                 reads=[t_ps[oc * 2 + t2]], writes=[t_yT[o][t2]])
            epilogue(plan, th * 1024, 1, out_d, d_x1)

    def all_phases(plan):
        consts(plan)
        phase_mod(plan, 0)
        for blk in range(NB if stop_after == "all" else 1):
            if not plan:
                fence(big0_toks + mg_toks)
            prologue(plan, x_d, blk * NT, NT // 128, 0, hT, t_hT)
            if not plan:
                dump("hT", hT, all_hT)
            if stop_after == "p1":
                return
            gmlp_setup(plan)
            vphase(plan, blk)
            mixing(plan)
            uphase(plan, blk)
            if not plan:
                dump("aT", big0[:, 16384:32768], t_A)
            if stop_after == "gmlp":
                return
            if blk == 0:
                phase_mod(plan, 1)
            branch_a(plan, blk)
            if not plan:
                dump("mgA", mergedT[:, :, :], [t for r in t_mg for t in r])
            if stop_after == "brA":
                return
            latents(plan, blk)
            if not plan:
                dump("kpe", kpe_dup[:, :], t_kpe)
                dump("kvg", kvgT[:, :, :], [t for r in t_kvg for t in r])
                dump("qg", qgT[:, :, :], [t for r in t_qg for t in r])
            if not plan:
                dump("cos", cos2[:, :], [t_cos])
                dump("sin", sin_s[:, :], [t_sin])
            if stop_after in ("lat", "lat0", "lat1", "lat2"):
                return
            attention(plan, blk)
            if blk == 0:
                phase_mod(plan, 2, stream=False)
            if not plan:
                dump("oT", big0[:, 16384:32768], all_oT)
            if stop_after == "att":
                return
            branch_b(plan, blk)
            if not plan:
                dump("mg", mergedT[:, :, :], [t for r in t_mg for t in r])
            if stop_after == "brB":
                return
            wout_phase(plan, blk)
            epilogue(plan, blk * NT, 0, x_d, None, stats_done=True)
            if stop_after == "x1":
                break
        if stop_after == "x1":
            return
        if not plan:
            fence(big0_toks + mg_toks)
        prologue(plan, out_d, 0, S // 128, 1, h2T, t_h2T, src_toks=d_x1)
        ffn_up(plan)
        ffn_down(plan)

    all_phases(True)
    all_phases(False)

    done_rows = {"all": S, "x1": NT}.get(stop_after, 0)
    fence(scr_toks)
    t_o = P.dtok("o")
    for i in range(done_rows // 128, S // 128):
        b = i % 2
        P.dma("sp", lambda e, b=b, i=i: e.dma_start(out=xs[b], in_=x_d[i * 128:(i + 1) * 128, :]), writes=[t_xs[b]])
        P.dma("sp", lambda e, b=b, i=i: e.dma_start(out=out_d[i * 128:(i + 1) * 128, :], in_=xs[b]),
              reads=[t_xs[b]], writes=[d_x1[i]], sem_tok=t_o)
    d_out.extend(d_x1)
    P.wait_all("sp", d_out)
    P.emit()
    return nc


def _col(v):
    v = np.asarray(v, dtype=np.float32).reshape(-1, 128)
    return np.ascontiguousarray(v.T)


def make_in_maps(inp, cores):
    f32 = lambda k: np.ascontiguousarray(inp[k], dtype=np.float32)
    w_in = f32("w_in")
    kpe = w_in[:, O_KPE:O_KPE + 64]
    swp = np.concatenate([kpe[:, 32:64], kpe[:, 0:32]], axis=1)
    w_lat = np.ascontiguousarray(np.concatenate([w_in[:, O_QL:O_QL + 768], kpe, kpe, swp, swp], axis=1))
    w_uq, w_ukv = f32("w_uq"), f32("w_ukv")
    w_pair = np.zeros((8, 128, 3072), np.float32)
    chunked = lambda a: a.reshape(-1, 128, a.shape[1]).transpose(1, 0, 2)
    for pr in range(8):
        parts = []
        hs = (2 * pr, 2 * pr + 1)
        parts.append(np.concatenate([w_uq[:, h * 192:h * 192 + 128] for h in hs], axis=1))
        parts.append(np.concatenate([w_uq[:, h * 192 + 128:h * 192 + 192] for h in hs], axis=1))
        parts.append(np.concatenate([np.concatenate([w_uq[:, h * 192 + 160:h * 192 + 192],
                                                     w_uq[:, h * 192 + 128:h * 192 + 160]], axis=1) for h in hs], axis=1))
        parts.append(np.concatenate([w_ukv[:, h * 256:h * 256 + 128] for h in hs], axis=1))
        parts.append(np.concatenate([w_ukv[:, h * 256 + 128:h * 256 + 256] for h in hs], axis=1))
        w_pair[pr] = np.concatenate([chunked(a).reshape(128, -1) for a in parts], axis=1)
    def fuse(wb, g0):
        return np.ascontiguousarray(np.concatenate(
            [np.concatenate([wb[:, j * 256:(j + 1) * 256], w_in[:, g0 + j * 256:g0 + (j + 1) * 256]], axis=1) for j in range(8)], axis=1))
    shared = {"w_ada": f32("w_ada"), "w_in": w_in, "w_ag": fuse(f32("w_branch_a"), O_GA), "w_bg": fuse(f32("w_branch_b"), O_GB),
              "w_lat": w_lat, "w_pair": w_pair, "w_out": f32("w_out"), "w_down": f32("w_down"),
              "w_up2": np.ascontiguousarray(np.asarray(inp["w_up"], np.float32).reshape(D, 2, FC, 128).transpose(0, 2, 1, 3)
                                            .reshape(D, 2 * D_FF)),
              "wsT": np.ascontiguousarray(np.transpose(np.asarray(inp["gm_w_s"], np.float32), (2, 0, 1))),
              "bs": np.ascontiguousarray(np.asarray(inp["gm_b_s"], np.float32).reshape(1, 2048))}
    maps = []
    for b in cores:
        cols = np.zeros((128, NCOL), np.float32)
        cols[:, C_BADA:C_BADA + 96] = _col(inp["b_ada"])
        cols[:, C_G1:C_G1 + 16] = _col(inp["pre_norm1_g"])
        cols[:, C_G2:C_G2 + 16] = _col(inp["pre_norm2_g"])
        cols[:, C_GP1:C_GP1 + 16] = _col(inp["post_norm1_g"])
        cols[:, C_GP2:C_GP2 + 16] = _col(inp["post_norm2_g"])
        cols[:, C_LNG:C_LNG + 16] = _col(inp["gm_ln_g"])
        cols[:, C_LNB:C_LNB + 16] = _col(inp["gm_ln_b"])
        cols[:, C_QG:C_QG + 4] = _col(inp["q_norm_g"])
        cols[:, C_KVG:C_KVG + 2] = _col(inp["kv_norm_g"])
        cols[:, C_C:C_C + 16] = _col(inp["c"][b])
        for k in range(3):
            cols[:, C_CW + 88 * k:C_CW + 88 * (k + 1)] = _col(inp["conv_w"][k])
        cols[:, C_CB:C_CB + 88] = _col(inp["conv_b"])
        pidx = np.arange(128)
        cols[:, C_INV] = (10000.0 ** (-(2.0 * (pidx % 32)) / 64.0)).astype(np.float32)
        sgn = np.where((pidx % 64) < 32, -1.0, 1.0).astype(np.float32)
        cols[:, C_SGN] = sgn
        cols[:, C_NPI] = np.float32(-np.pi)
        cols[:, C_NPS] = (np.float32(-np.pi) * sgn).astype(np.float32)
        m = dict(shared)
        m["x"] = np.ascontiguousarray(inp["x"][b], dtype=np.float32)
        m["pos"] = np.ascontiguousarray(inp["positions"][b], dtype=np.int32).reshape(1, S)
        m["cols"] = cols
        maps.append(m)
    return maps


def kernel(**inputs):
    nc = build()
    maps = make_in_maps(inputs, list(range(8)))
    res = run_bass_kernel_spmd(nc, maps, core_ids=list(range(8)))
    return np.stack([np.asarray(r["out"], dtype=np.float32) for r in res.results], axis=0)
```
